# Optimizing a Trainium2 kernel written in Bass

```python
import jax, jax.numpy as jnp
from jax import lax
import numpy as np

D_MODEL = 1024
BATCH = 32
SEQ = 256
DEPTH = 2
DEC_BATCH = 2
DEC_SEQ = 2048
PAST_LEN = 512

GRID_W = 64
HEAD_DIM = 64
N_Q_HEADS = 8
N_KV_HEADS = 2
Q_PER_KV = N_Q_HEADS // N_KV_HEADS
ATTN_W = N_Q_HEADS * HEAD_DIM
KV_W = N_KV_HEADS * HEAD_DIM
POOL_W = D_MODEL // 4
POOL_GROUPS = 4
POOL_GW = POOL_W // POOL_GROUPS
POOL_WINDOWS = (2, 4, 8, 16)
CONV_W = D_MODEL // 4
CONV_K = 3
N_BRANCH = 3
FFN_HIDDEN = -(-8 * D_MODEL // (3 * 256)) * 256
Q_BLOCK = 128
ROPE_THETA = 10000.0
ROPE_FREQS = HEAD_DIM // 4
EPS = 1e-6

OFF_CB = POOL_W
OFF_CC = OFF_CB + CONV_W
OFF_CH = OFF_CC + CONV_W
OFF_Q = OFF_CH + CONV_W
OFF_K = OFF_Q + ATTN_W
OFF_V = OFF_K + KV_W
OFF_G = OFF_V + KV_W
IN_W = OFF_G + N_BRANCH * D_MODEL

kernel_name = "hybrid_diffusion_pool_conv_gqa_step"


def rmsnorm(x, g):
    xf = x.astype(jnp.float32)
    y = xf * lax.rsqrt(jnp.mean(xf * xf, axis=-1, keepdims=True) + EPS)
    return (y * g.astype(jnp.float32)).astype(x.dtype)


def rope_tables(L):
    rows = L // GRID_W
    pos_row = jnp.repeat(jnp.arange(rows), GRID_W).astype(jnp.float32)
    pos_col = jnp.tile(jnp.arange(GRID_W), rows).astype(jnp.float32)
    inv = ROPE_THETA ** (-jnp.arange(ROPE_FREQS, dtype=jnp.float32) / ROPE_FREQS)
    ang = jnp.stack([pos_row[:, None] * inv, pos_col[:, None] * inv], axis=1)
    return jnp.cos(ang), jnp.sin(ang)


def apply_rope(x, cos, sin):
    B, L, H, _ = x.shape
    xr = x.reshape(B, L, H, 2, 2, ROPE_FREQS)
    x1, x2 = xr[..., 0, :], xr[..., 1, :]
    c = cos[None, :, None].astype(x.dtype)
    s = sin[None, :, None].astype(x.dtype)
    out = jnp.stack([x1 * c - x2 * s, x1 * s + x2 * c], axis=-2)
    return out.reshape(B, L, H, HEAD_DIM)


def pool_mixer(u, w_pool, pool_scale):
    B, L, _ = u.shape
    ug = u.reshape(B, L, POOL_GROUPS, POOL_GW)
    cs = jnp.cumsum(ug.astype(jnp.float32), axis=1)
    cs = jnp.concatenate([jnp.zeros((B, 1, POOL_GROUPS, POOL_GW), jnp.float32), cs], axis=1)
    t = jnp.arange(L)
    pooled = []
    for g, w in enumerate(POOL_WINDOWS):
        lo = jnp.clip(t - w // 2, 0, L)
        hi = jnp.clip(t + w // 2, 0, L)
        cnt = (hi - lo).astype(jnp.float32)
        pooled.append((cs[:, hi, g] - cs[:, lo, g]) / cnt[None, :, None])
    pooled = jnp.stack(pooled, axis=2).astype(u.dtype) - ug
    y = jnp.einsum('blgc,gcd->blgd', pooled, w_pool).reshape(B, L, POOL_W)
    return y * pool_scale


def dwconv3(u, conv_w):
    return lax.conv_general_dilated(u, conv_w[:, None, :].astype(u.dtype), window_strides=(1,),
                                    padding=((1, 1),), dimension_numbers=('NWC', 'WIO', 'NWC'),
                                    feature_group_count=CONV_W)


def attention(q, k, v):
    B, L, _, _ = q.shape
    nb = L // Q_BLOCK
    qb = q.reshape(B, nb, Q_BLOCK, N_KV_HEADS, Q_PER_KV, HEAD_DIM).transpose(1, 0, 2, 3, 4, 5)
    scale = HEAD_DIM ** -0.5

    def block(qblk):
        s = jnp.einsum('bqkgd,btkd->bkgqt', qblk, k, preferred_element_type=jnp.float32) * scale
        p = jax.nn.softmax(s, axis=-1).astype(v.dtype)
        return jnp.einsum('bkgqt,btkd->bqkgd', p, v)

    out = lax.map(block, qb)
    return out.transpose(1, 0, 2, 3, 4, 5).reshape(B, L, ATTN_W)


def mixer(h, p, rope, ctx_kv):
    B, L, _ = h.shape
    z = h @ p['w_in']
    u_pool, cb, cc, ch, q, k, v, gates = jnp.split(
        z, [OFF_CB, OFF_CC, OFF_CH, OFF_Q, OFF_K, OFF_V, OFF_G], axis=-1)
    pool_out = pool_mixer(u_pool, p['w_pool'], p['pool_scale'])
    conv_out = cb * dwconv3(cc * ch, p['conv_w'])
    q = rmsnorm(q.reshape(B, L, N_Q_HEADS, HEAD_DIM), p['q_norm_g'])
    k = rmsnorm(k.reshape(B, L, N_KV_HEADS, HEAD_DIM), p['k_norm_g'])
    v = v.reshape(B, L, N_KV_HEADS, HEAD_DIM)
    if rope is not None:
        q = apply_rope(q, rope[0], rope[1])
        k = apply_rope(k, rope[0], rope[1])
    if ctx_kv is None:
        k_all, v_all = k, v
    else:
        k_all = jnp.concatenate([k, ctx_kv[0]], axis=1)
        v_all = jnp.concatenate([v, ctx_kv[1]], axis=1)
    attn_out = attention(q, k_all, v_all)
    g = jax.nn.sigmoid(gates).reshape(B, L, N_BRANCH, D_MODEL)
    merged = (g[:, :, 0] * (pool_out @ p['w_br_pool'])
              + g[:, :, 1] * (conv_out @ p['w_br_conv'])
              + g[:, :, 2] * (attn_out @ p['w_br_attn']))
    return merged @ p['w_o'], k, v


def layer(x, cvec, p, rope, ctx_kv):
    mod = jax.nn.silu(cvec) @ p['w_ada'] + p['b_ada']
    sh1, sc1, g1, sh2, sc2, g2 = [m[:, None, :] for m in jnp.split(mod, 6, axis=-1)]
    h = rmsnorm(x, p['norm1_g']) * (1 + sc1) + sh1
    out, k, v = mixer(h, p, rope, ctx_kv)
    x = x + g1 * out
    h2 = rmsnorm(x, p['norm2_g']) * (1 + sc2) + sh2
    ffn = (jax.nn.silu(h2 @ p['w_gate']) * (h2 @ p['w_up'])) @ p['w_down']
    return x + g2 * ffn, k, v


def setup_inputs(seed: int = 0) -> dict:
    key = jax.random.key(seed)
    ks = jax.random.split(key, 32)
    f = jnp.float32
    n = lambda i, shape, s: jax.random.normal(ks[i], shape, f) * s
    D = D_MODEL
    return {
        'x_prompt': n(0, (BATCH, SEQ, D), 1.0),
        'x_sample': n(1, (DEC_BATCH, DEC_SEQ, D), 1.0),
        'cache_k': n(2, (DEC_BATCH, DEPTH, PAST_LEN, N_KV_HEADS, HEAD_DIM), 1.0),
        'cache_v': n(3, (DEC_BATCH, DEPTH, PAST_LEN, N_KV_HEADS, HEAD_DIM), 1.0),
        'c': n(4, (DEC_BATCH, D), 1.0),
        'c_ctx': n(5, (D,), 1.0),
        'norm1_g': 1.0 + n(6, (DEPTH, D), 0.02),
        'norm2_g': 1.0 + n(7, (DEPTH, D), 0.02),
        'w_ada': n(8, (DEPTH, D, 6 * D), 0.5 * D ** -0.5),
        'b_ada': n(9, (DEPTH, 6 * D), 0.02),
        'w_in': n(10, (DEPTH, D, IN_W), D ** -0.5),
        'w_pool': n(11, (DEPTH, POOL_GROUPS, POOL_GW, POOL_GW), POOL_GW ** -0.5),
        'pool_scale': 1.0 + n(12, (DEPTH, POOL_W), 0.02),
        'conv_w': n(13, (DEPTH, CONV_K, CONV_W), CONV_K ** -0.5),
        'q_norm_g': 1.0 + n(14, (DEPTH, HEAD_DIM), 0.02),
        'k_norm_g': 1.0 + n(15, (DEPTH, HEAD_DIM), 0.02),
        'w_br_pool': n(16, (DEPTH, POOL_W, D), POOL_W ** -0.5),
        'w_br_conv': n(17, (DEPTH, CONV_W, D), CONV_W ** -0.5),
        'w_br_attn': n(18, (DEPTH, ATTN_W, D), ATTN_W ** -0.5),
        'w_o': n(19, (DEPTH, D, D), D ** -0.5),
        'w_gate': n(20, (DEPTH, D, FFN_HIDDEN), D ** -0.5),
        'w_up': n(21, (DEPTH, D, FFN_HIDDEN), D ** -0.5),
        'w_down': n(22, (DEPTH, FFN_HIDDEN, D), FFN_HIDDEN ** -0.5),
        'final_g': 1.0 + n(23, (D,), 0.02),
    }


def reference(x_prompt, x_sample, cache_k, cache_v, c, c_ctx, norm1_g, norm2_g, w_ada, b_ada, w_in,
              w_pool, pool_scale, conv_w, q_norm_g, k_norm_g, w_br_pool, w_br_conv, w_br_attn, w_o,
              w_gate, w_up, w_down, final_g):
    stacked = {'norm1_g': norm1_g, 'norm2_g': norm2_g, 'w_ada': w_ada, 'b_ada': b_ada, 'w_in': w_in,
               'w_pool': w_pool, 'pool_scale': pool_scale, 'conv_w': conv_w, 'q_norm_g': q_norm_g,
               'k_norm_g': k_norm_g, 'w_br_pool': w_br_pool, 'w_br_conv': w_br_conv,
               'w_br_attn': w_br_attn, 'w_o': w_o, 'w_gate': w_gate, 'w_up': w_up, 'w_down': w_down}

    xp = x_prompt
    c_prompt = jnp.broadcast_to(c_ctx, (x_prompt.shape[0], D_MODEL))
    new_k, new_v = [], []
    for i in range(DEPTH):
        p = {name: w[i] for name, w in stacked.items()}
        xp, k, v = layer(xp, c_prompt, p, None, None)
        new_k.append(k)
        new_v.append(v)
    y_prompt = rmsnorm(xp, final_g)
    new_cache_k = jnp.stack(new_k, axis=1)
    new_cache_v = jnp.stack(new_v, axis=1)

    xs = x_sample
    rope = rope_tables(x_sample.shape[1])
    for i in range(DEPTH):
        p = {name: w[i] for name, w in stacked.items()}
        xs, _, _ = layer(xs, c, p, rope, (cache_k[:, i], cache_v[:, i]))
    y_sample = rmsnorm(xs, final_g)

    return (y_prompt, y_sample, new_cache_k, new_cache_v)
```

```python
import contextlib
import numpy as np
import ml_dtypes
import concourse.bass as bass
import concourse.mybir as mybir
from concourse.bass_utils import run_bass_kernel_spmd

F32 = mybir.dt.float32
BF16 = mybir.dt.bfloat16
AF = mybir.ActivationFunctionType
ALU = mybir.AluOpType
AX = mybir.AxisListType

D = 1024
DEPTH = 2
NT = 1536
IN_W = 4864
OFF_Q = 1024
OFF_G = 1792
FFN = 2816
EPS = 1e-6
NSLOT = 3
PUBR = 528
P_N1G, P_N2G, P_FG, P_PSC, P_CVW, P_CV, P_BADA = 0, 16, 32, 40, 44, 56, 72


class Buf:
    __slots__ = ("name", "w", "r")

    def __init__(self, name=""):
        self.name = name
        self.w = None
        self.r = []


def transfer(old, new):
    ts = []
    for b in old:
        if b.w is not None:
            ts.append(b.w)
        ts.extend(b.r)
    red = {}
    for s, v in ts:
        if red.get(s, 0) < v:
            red[s] = v
    ts = list(red.items())
    for b in new:
        b.w = None
        b.r = list(ts)


class Sched:
    ENGS = ("pe", "act", "dve", "pool", "sp")

    def __init__(self, nc, stack, n_lanes=None):
        self.nc = nc
        self.stack = stack
        self.prog = {e: [] for e in self.ENGS}
        self.sems = []
        self.esem = {}
        self.cnt = {}
        for e in ("pe", "act", "dve", "pool"):
            self.esem[e] = self._newsem("s_" + e)
            self.cnt[e] = 0
        self.seen = {e: {} for e in self.ENGS}
        n_lanes = n_lanes or {"sp": 12, "pool": 8, "act": 6}
        self.lanes = {q: [[self._newsem(f"l_{q}{i}"), 0] for i in range(n)] for q, n in n_lanes.items()}
        self.lane_rr = {q: 0 for q in n_lanes}
        self.customs = []
        self.n_wait = 0
        self.n_ins = {e: 0 for e in self.ENGS}

    def _newsem(self, name):
        s = self.stack.enter_context(self.nc.semaphore(name))
        self.sems.append(s)
        return len(self.sems) - 1

    def _deps(self, engine, reads, writes):
        need = {}

        def add(t):
            s, v = t
            if engine == "pe" and s == self.esem["pe"]:
                return
            if need.get(s, 0) < v:
                need[s] = v
        for b in reads:
            if b.w is not None:
                add(b.w)
        for b in writes:
            if b.w is not None:
                add(b.w)
            for t in b.r:
                add(t)
        out = []
        seen = self.seen[engine]
        for s, v in need.items():
            if seen.get(s, 0) >= v:
                continue
            seen[s] = v
            out.append((s, v))
        return out

    def _commit(self, ticket, reads, writes):
        for b in writes:
            b.w = ticket
            b.r = []
        for b in reads:
            b.r.append(ticket)

    def op(self, engine, fn, reads=(), writes=()):
        waits = self._deps(engine, reads, writes)
        sem = self.esem[engine]
        self.cnt[engine] += 1
        ticket = (sem, self.cnt[engine])
        sems = self.sems

        def run(e, waits=waits, fn=fn, sem=sem):
            for s, v in waits:
                e.wait_ge(sems[s], v)
            ins = fn(e)
            ins.then_inc(sems[sem], 1)
        self.prog[engine].append(run)
        self.n_wait += len(waits)
        self.n_ins[engine] += 1
        self._commit(ticket, reads, writes)
        return ticket

    def dma(self, queue, out, in_, reads=(), writes=(), **kw):
        lanes = self.lanes[queue]
        i = self.lane_rr[queue]
        self.lane_rr[queue] = (i + 1) % len(lanes)
        lane = lanes[i]
        s = lane[0]
        waits = self._deps(queue, reads, writes)
        seen = self.seen[queue]
        if seen.get(s, 0) < lane[1]:
            seen[s] = lane[1]
            waits.append((s, lane[1]))
        lane[1] += 16
        ticket = (s, lane[1])
        sems = self.sems

        def run(e, waits=waits, s=s, out=out, in_=in_, kw=kw):
            for ws, v in waits:
                e.wait_ge(sems[ws], v)
            e.dma_start(out=out, in_=in_, **kw).then_inc(sems[s], 16)
        self.prog[queue].append(run)
        self.n_wait += len(waits)
        self.n_ins[queue] += 1
        self._commit(ticket, reads, writes)
        return ticket

    def custom(self, engine, fn, reads=(), writes=(), sem_inc=1):
        s = self._newsem(f"c_{len(self.sems)}")
        waits = self._deps(engine, reads, writes)
        ticket = (s, sem_inc)
        self.customs.append(ticket)
        sems = self.sems

        def run(e, waits=waits, s=s):
            for ws, v in waits:
                e.wait_ge(sems[ws], v)
            fn(e).then_inc(sems[s], sem_inc)
        self.prog[engine].append(run)
        self._commit(ticket, reads, writes)
        return ticket

    def finish(self):
        waits = []
        for q, lanes in self.lanes.items():
            for s, c in lanes:
                if c > 0:
                    waits.append((s, c))
        for e in ("pe", "act", "dve", "pool"):
            if self.cnt[e] > 0:
                waits.append((self.esem[e], self.cnt[e]))
        waits.extend(self.customs)
        sems = self.sems

        def run(e, waits=waits):
            for s, v in waits:
                e.wait_ge(sems[s], v)
        self.prog["sp"].append(run)

    def replay(self):
        nc = self.nc
        with nc.Block() as block:
            @block.sync
            def _(e):
                for f in self.prog["sp"]:
                    f(e)

            @block.scalar
            def _(e):
                for f in self.prog["act"]:
                    f(e)

            @block.vector
            def _(e):
                for f in self.prog["dve"]:
                    f(e)

            @block.gpsimd
            def _(e):
                for f in self.prog["pool"]:
                    f(e)

            @block.tensor
            def _(e):
                for f in self.prog["pe"]:
                    f(e)


class Ring:
    def __init__(self, items):
        self.items = items
        self.i = 0

    def next(self):
        it = self.items[self.i]
        self.i = (self.i + 1) % len(self.items)
        return it


class WStream:
    def __init__(self, S, slots):
        self.S = S
        self.slots = slots
        self.keys = []
        self.plan = []
        self.loaded = 0
        self.gate = []
        self.acquired = 0
        self.released = 0

    def add(self, pieces, key=None):
        self.plan.append(pieces)
        self.keys.append(key)

    def _pump(self):
        ns = len(self.slots)
        while self.loaded < len(self.plan) and self.loaded - ns < self.released:
            n = self.loaded
            t, b = self.slots[n % ns]
            for pi, (off, kc, ncols, src, plo, phi) in enumerate(self.plan[n]):
                dst = t[plo:phi, off:off + kc * ncols].rearrange("p (k n) -> p k n", k=kc)
                self.S.dma("pool", dst, src, reads=(self.gate if n == 0 else ()), writes=[b[pi]])
            self.loaded += 1

    def acquire(self, expect=None):
        self._pump()
        n = self.acquired
        assert n < self.loaded, "weight ring deadlock: too many slabs held"
        assert expect is None or self.keys[n] == expect, (n, self.keys[n], expect)
        self.acquired += 1
        t, b = self.slots[n % len(self.slots)]
        return t, b

    def release(self):
        self.released += 1
        self._pump()


def build_program():
    nc = bass.Bass("TRN2", target_bir_lowering=False)

    def din(name, shape, dt=F32):
        return nc.dram_tensor(name, list(shape), dt, kind="ExternalInput").ap()

    def dout(name, shape, dt=F32):
        return nc.dram_tensor(name, list(shape), dt, kind="ExternalOutput").ap()

    xp = din("xp", [1024, D])
    xs = din("xs", [512, D])
    ck = din("ck", [DEPTH, 512, 128])
    cv = din("cv", [DEPTH, 512, 128])
    cvec = din("cvec", [2, D])
    n1g = din("n1g", [DEPTH, D])
    n2g = din("n2g", [DEPTH, D])
    w_ada = din("w_ada", [DEPTH, D, 6 * D])
    b_ada = din("b_ada", [DEPTH, 6 * D])
    w_in = din("w_in", [DEPTH, D, IN_W])
    w_pool = din("w_pool", [DEPTH, 4, 64, 64])
    pool_scale = din("pool_scale", [DEPTH, 256])
    conv_w = din("conv_w", [DEPTH, 3, 256])
    qg = din("qg", [DEPTH, 64])
    kg = din("kg", [DEPTH, 64])
    w_brp = din("w_brp", [DEPTH, 256, D])
    w_brc = din("w_brc", [DEPTH, 256, D])
    w_bra = din("w_bra", [DEPTH, 512, D])
    w_o = din("w_o", [DEPTH, D, D])
    w_gate = din("w_gate", [DEPTH, D, FFN])
    w_up = din("w_up", [DEPTH, D, FFN])
    w_down = din("w_down", [DEPTH, FFN, D])
    final_g = din("final_g", [D])
    c_ident = din("c_ident", [128, 128])
    c_bm = din("c_bm", [128, 4 * 5 * 128], BF16)
    c_smpd = din("c_smpd", [128, 4 * 2 * 128], BF16)
    c_smph = din("c_smph", [64, 4 * 2 * 128], BF16)
    c_ysel = din("c_ysel", [8, 2], BF16)
    c_ropec = din("c_ropec", [512, 32])
    c_ropes = din("c_ropes", [512, 32])

    yp = dout("yp", [1024, D])
    ys = dout("ys", [512, D])
    nk = dout("nk", [4, DEPTH, 256, 128])
    nv = dout("nv", [4, DEPTH, 256, 128])

    pub = [nc.dram_tensor(f"pub{l}", [PUBR, 256], BF16, kind="Internal").ap() for l in range(DEPTH)]
    gath = [nc.dram_tensor(f"gath{l}", [4 * PUBR, 256], BF16, kind="Internal").ap() for l in range(DEPTH)]
    pubY = [nc.dram_tensor(f"pubY{l}", [2, 256], BF16, kind="Internal").ap() for l in range(DEPTH)]
    gathY = [nc.dram_tensor(f"gathY{l}", [8, 256], BF16, kind="Internal").ap() for l in range(DEPTH)]

    with contextlib.ExitStack() as st:
        S = Sched(nc, st)

        sb_bytes = [0]

        def sb(name, shape, dt):
            n = 1
            for d_ in shape[1:]:
                n *= d_
            sb_bytes[0] += n * (4 if dt == F32 else 2)
            return st.enter_context(nc.sbuf_tensor(name, list(shape), dt))

        pbank = []
        ppair = []
        for i in range(4):
            t = st.enter_context(nc.psum_tensor(f"pp{i}", [128, 1024], F32))
            b0_, b1_ = Buf(f"pb{2 * i}"), Buf(f"pb{2 * i + 1}")
            pbank.append((t[:, 0:512], b0_))
            pbank.append((t[:, 512:1024], b1_))
            ppair.append((t, b0_, b1_))
        ring_all = Ring(pbank)

        ident = sb("ident", [128, 128], F32)
        ones_bf = sb("ones_bf", [128, 128], BF16)
        eps_t = sb("eps_t", [128, 1], F32)
        bm = sb("bm", [128, 4, 5, 128], BF16)
        smpd = sb("smpd", [128, 4, 2, 128], BF16)
        smph = sb("smph", [64, 4, 2, 128], BF16)
        ysel = sb("ysel", [8, 2], BF16)
        ropec = sb("ropec", [128, 4, 32], F32)
        ropes = sb("ropes", [128, 4, 32], F32)
        gst = sb("gst", [128, DEPTH, 2, 64], F32)
        g10 = sb("g10", [128, 640], F32)
        prm = sb("prm", [128, 168], F32)
        csil = sb("csil", [128, 8, 2], BF16)
        modT = sb("modT", [128, DEPTH, 48, 2], F32)
        a12 = sb("a12", [128, DEPTH, 2, 8, 2], F32)
        wpbd = sb("wpbd", [128, DEPTH, 2, 128], BF16)
        b_ident = Buf("ident")
        b_ones = Buf("ones")
        b_epsb = Buf("eps")
        CB = [b_ident, b_ones, b_epsb]
        b_prm = Buf("prm")
        b_csil = Buf("csil")
        b_mod = [Buf(f"mod{l}") for l in range(DEPTH)]
        b_a12 = [Buf(f"a12{l}") for l in range(DEPTH)]
        b_g10 = Buf("g10")
        b_wpbd = Buf("wpbd")

        xT = sb("xT", [128, 8, NT], F32)
        hT = sb("hT", [128, 8, NT], BF16)
        b_xT = [[Buf(f"xT{k}_{t}") for t in range(12)] for k in range(8)]
        b_hT = [[Buf(f"hT{k}_{t}") for t in range(12)] for k in range(8)]

        def xb(k, tb):
            return [b_xT[k][tb * 4 + i] for i in range(4)]

        def hb_tb(tb):
            return [b_hT[k][tb * 4 + i] for k in range(8) for i in range(4)]

        def hb_tile(tt):
            return [b_hT[k][tt] for k in range(8)]

        slots = [(sb(f"wslot{i}", [128, 4096], BF16), [Buf(f"wslot{i}_{j}") for j in range(8)]) for i in range(NSLOT)]
        W = WStream(S, slots)

        R1 = sb("R1", [128, 12288], BF16)
        R2 = sb("R2", [128, 12288], BF16)
        qT = R1[:, 0:6144].rearrange("p (c n) -> p c n", c=4)
        kT = R1[:, 6144:7168]
        pubKT = R1[:, 7168:7680]
        vB = R1[:, 7680:9216].rearrange("p (t c) -> p t c", t=12)
        uB = R1[:, 9216:12288].rearrange("p (t c) -> p t c", t=12)
        b_qT = [Buf(f"qT{t}") for t in range(12)]
        b_kT = [Buf(f"kT{t}") for t in range(8)]
        b_pubKT = [Buf(f"pubKT{t}") for t in range(4)]
        b_vB = [Buf(f"vB{t}") for t in range(12)]
        b_uB = [Buf(f"uB{t}") for t in range(12)]
        A_R1 = b_qT + b_kT + b_pubKT + b_vB + b_uB
        mgT = R1[:, :].rearrange("p (c n) -> p c n", c=8)
        b_mg = [[Buf(f"mg{j}_{tb}") for tb in range(3)] for j in range(8)]
        B_R1 = [b for row in b_mg for b in row]
        aoT = R2[:, 0:6144].rearrange("p (c n) -> p c n", c=4)
        poT = R2[:, 6144:9216].rearrange("p (c n) -> p c n", c=2)
        coT = R2[:, 9216:12288].rearrange("p (c n) -> p c n", c=2)
        b_ao = [[Buf(f"ao{c}_{sg}") for sg in range(5)] for c in range(4)]
        b_po = [[Buf(f"po{c}_{tb}") for tb in range(3)] for c in range(2)]
        b_co = [[Buf(f"co{c}_{tb}") for tb in range(3)] for c in range(2)]
        A_R2 = [b for row in b_ao for b in row] + [b for row in b_po for b in row] + [b for row in b_co for b in row]
        def actT(fi):
            if fi < 8:
                return R1[:, fi * NT:(fi + 1) * NT]
            return R2[:, (fi - 8) * NT:(fi - 7) * NT]
        b_act = [[Buf(f"act{f}_{tb}") for tb in range(3)] for f in range(11)]
        F_ALL = [b for row in b_act for b in row]
        R1f = R1[:, :].bitcast(F32)
        R2f = R2[:, :].bitcast(F32)
        stA = R1f[0:72, 0:128]
        stB = R1f[0:96, 128:256]
        wpst = R1f[:, 256:512].rearrange("p (l g d) -> p l g d", l=DEPTH, g=2)

        kTall = sb("kTall", [128, 2560], BF16)
        vall = sb("vall", [128, 20, 128], BF16)
        uhalo = sb("uhalo", [64, 256], BF16)
        yhalo = sb("yhalo", [8, 256], BF16)
        b_kTall = [Buf(f"kTall{r}") for r in range(5)]
        b_vall = [Buf(f"vall{r}") for r in range(5)]
        b_uhalo = [Buf(f"uh{r}") for r in range(4)]
        b_yhalo = [Buf(f"yh{r}") for r in range(4)]

        def scr(name, shape, dt):
            return (sb(name, shape, dt), Buf(name))
        sF = [scr(f"sF{i}", [128, 640], F32) for i in range(7)]
        sH = [scr(f"sH{i}", [128, 512], BF16) for i in range(4)]
        ptp = [scr(f"ptp{i}", [128, 1024], BF16) for i in range(2)]
        small = [scr(f"sm{i}", [128, 16], F32) for i in range(4)]
        ybuf = [scr(f"ybuf{i}", [128, 516], F32) for i in range(1)]
        ysm = scr("ysm", [128, 2, 514], F32)
        cbs = scr("cbs", [128, 2, 512], BF16)
        vf = [scr(f"vf{i}", [128, 128], F32) for i in range(2)]
        yhal = scr("yhal", [2, 256], BF16)
        yhtmp = (sF[0][0][0:2, 0:256], sF[0][1])
        xin = [(R2f[:, i * 1024:(i + 1) * 1024], Buf(f"xin{i}")) for i in range(6)]

        S.dma("sp", ident[:], c_ident, writes=[b_ident])
        bst = [Buf(f"st{i}") for i in range(7)]
        S.dma("act", stA[0:16, :], n1g.rearrange("l (k c) -> (l k) c", c=128), writes=[bst[0]])
        S.dma("act", stA[16:32, :], n2g.rearrange("l (k c) -> (l k) c", c=128), writes=[bst[1]])
        S.dma("act", stA[32:40, :], final_g.rearrange("(k c) -> k c", c=128), writes=[bst[2]])
        S.dma("act", stA[40:44, :], pool_scale.rearrange("l (k c) -> (l k) c", c=128), writes=[bst[3]])
        S.dma("act", stA[44:56, :], conv_w.rearrange("l j (k c) -> (l j k) c", c=128), writes=[bst[4]])
        S.dma("act", stA[56:72, :], cvec.rearrange("v (k c) -> (v k) c", c=128), writes=[bst[5]])
        S.dma("act", stB[0:96, :], b_ada.rearrange("l (k c) -> (l k) c", c=128), writes=[bst[6]])
        bc = [Buf(f"c{i}") for i in range(16)]
        CONSTS = CB + bc
        bwp = [Buf(f"wp{i}") for i in range(2)]
        S.op("pool", lambda e: e.memset(ones_bf[:], 1.0), writes=[b_ones])
        S.op("pool", lambda e: e.memset(eps_t[:], EPS), writes=[b_epsb])
        for i in range(1):
            S.op("pool", lambda e, i=i: e.memset(ybuf[i][0][:], 0.0), writes=[ybuf[i][1]])
        S.op("pool", lambda e: e.memset(wpbd[:].rearrange("p a b c -> p (a b c)"), 0.0), writes=[b_wpbd])

        def wsl(ap2d):
            return ap2d.rearrange("(k p) n -> p k n", p=128)

        def plan_ada_slab(l, jb):
            W.add([(0, 8, 512, wsl(w_ada[l, :, jb * 512:(jb + 1) * 512]), 0, 128)], key=("ada", l, jb))

        def plan_layer(l):
            W.add([(0, 8, 512, wsl(w_in[l, :, OFF_Q:OFF_Q + 512]), 0, 128)], key=("tmq", l))
            W.add([(0, 8, 256, wsl(w_in[l, :, 1536:1792]), 0, 128),
                   (2048, 8, 256, wsl(w_in[l, :, 0:256]), 0, 128)], key=("tmk", l))
            for i in range(2):
                W.add([(0, 8, 128, wsl(w_in[l, :, 256 + i * 128:256 + (i + 1) * 128]), 0, 128),
                       (1024, 8, 128, wsl(w_in[l, :, 512 + i * 128:512 + (i + 1) * 128]), 0, 128),
                       (2048, 8, 128, wsl(w_in[l, :, 768 + i * 128:768 + (i + 1) * 128]), 0, 128)], key=("conv", l, i))
            if l == 0:
                for jb in range(4, 12):
                    plan_ada_slab(0, jb)
            for j in range(8):
                pcs = []
                for br in range(3):
                    c0 = OFF_G + br * 1024 + j * 128
                    pcs.append((br * 1024, 8, 128, wsl(w_in[l, :, c0:c0 + 128]), 0, 128))
                pcs.append((3072, 2, 128, wsl(w_brp[l, :, j * 128:(j + 1) * 128]), 0, 128))
                pcs.append((3328, 2, 128, wsl(w_brc[l, :, j * 128:(j + 1) * 128]), 0, 128))
                for h2 in range(2):
                    src = w_bra[l, h2 * 256:(h2 + 1) * 256, j * 128:(j + 1) * 128].rearrange("(c p) n -> p c n", p=64)
                    pcs.append((3584, 4, 128, src, h2 * 64, (h2 + 1) * 64))
                W.add(pcs, key=("mrg", l, j))
            for h in range(2):
                W.add([(0, 8, 512, wsl(w_o[l, :, h * 512:(h + 1) * 512]), 0, 128)], key=("wo", l, h))
            nxt = iter(range(12))
            for half in range(2):
                for fi in range(11):
                    f = half * 11 + fi
                    W.add([(0, 8, 128, wsl(w_gate[l, :, f * 128:(f + 1) * 128]), 0, 128),
                           (1024, 8, 128, wsl(w_up[l, :, f * 128:(f + 1) * 128]), 0, 128)], key=("gu", l, f))
                    if l + 1 < DEPTH and fi % 2 == 1:
                        plan_ada_slab(l + 1, next(nxt))
                for c in range(4):
                    W.add([(0, 11, 256, wsl(w_down[l, half * 1408:(half + 1) * 1408, c * 256:(c + 1) * 256]), 0, 128)],
                          key=("down", l, half, c))
                    if l + 1 < DEPTH and c == 1:
                        plan_ada_slab(l + 1, next(nxt))

        for jb in range(4):
            plan_ada_slab(0, jb)
        for l in range(DEPTH):
            plan_layer(l)

        def emit_mod_slab(l, jb, ps=None):
            wt, wb = W.acquire(("ada", l, jb))
            pt, pbf = ps if ps is not None else ring_all.next()

            def mm(e, wt=wt, pt=pt):
                ins = None
                for jj in range(4):
                    for k in range(8):
                        ins = e.matmul(pt[:, jj * 2:jj * 2 + 2], lhsT=wt[:, k * 512 + jj * 128:k * 512 + (jj + 1) * 128],
                                       rhs=csil[:, k, :], start=(k == 0), stop=(k == 7))
                return ins
            S.op("pe", mm, reads=wb + [b_csil], writes=[pbf])
            W.release()
            S.op("dve", lambda e, pt=pt, jb=jb: e.tensor_tensor(
                out=modT[:, l, jb * 4:(jb + 1) * 4, :],
                in0=pt[:, 0:8].rearrange("p (a b) -> p a b", b=2),
                in1=prm[:, P_BADA + l * 48 + jb * 4:P_BADA + l * 48 + jb * 4 + 4].unsqueeze(2).broadcast_to([128, 4, 2]),
                op=ALU.add), reads=[pbf, b_prm], writes=[b_mod[l]])
            if jb == 3 or jb == 9:
                w, sc0, pg = (0, 8, P_N1G) if jb == 3 else (1, 32, P_N2G)
                S.op("dve", lambda e: e.scalar_tensor_tensor(
                    out=a12[:, l, w, :, :], in0=modT[:, l, sc0:sc0 + 8, :], scalar=1.0,
                    in1=prm[:, pg + l * 8:pg + l * 8 + 8].unsqueeze(2).broadcast_to([128, 8, 2]),
                    op0=ALU.add, op1=ALU.mult), reads=[b_mod[l], b_prm], writes=[b_a12[l]])

        def emit_stats(tb, src_bufs_fn):
            pt, pbf = ring_all.next()
            for k in range(8):
                x2, bx2 = sH[k % 2]
                S.op("act", lambda e, k=k, x2=x2: e.activation(out=x2[:], in_=xT[:, k, tb * 512:(tb + 1) * 512], func=AF.Square),
                     reads=xb(k, tb), writes=[bx2])
                S.op("pe", lambda e, k=k, x2=x2, pt=pt: e.matmul(pt[:], lhsT=ones_bf[:], rhs=x2[:], start=(k == 0), stop=(k == 7)),
                     reads=[bx2] + CB, writes=[pbf])
            rs, brs = sF[tb % 2]
            S.op("act", lambda e, pt=pt, rs=rs: e.activation(out=rs[:, 0:512], in_=pt[:], func=AF.Sqrt, scale=1.0 / D, bias=eps_t[:]),
                 reads=[pbf] + CB, writes=[brs])
            S.op("dve", lambda e, rs=rs: e.reciprocal(out=rs[:, 0:512], in_=rs[:, 0:512]), reads=[brs], writes=[brs])
            return rs, brs

        ring5 = Ring(pbank[0:5])
        stat_banks = pbank[5:8]
        ring_x2 = Ring(sH[0:4])

        def stats_chunk(k, only_tb=None):
            for tb in (range(3) if only_tb is None else [only_tb]):
                x2, bx2 = ring_x2.next()
                pt, pbf = stat_banks[tb]
                S.op("act", lambda e, x2=x2, tb=tb: e.activation(out=x2[:], in_=xT[:, k, tb * 512:(tb + 1) * 512], func=AF.Square),
                     reads=xb(k, tb), writes=[bx2])
                S.op("pe", lambda e, x2=x2, pt=pt: e.matmul(pt[:], lhsT=ones_bf[:], rhs=x2[:], start=(k == 0), stop=(k == 7)),
                     reads=[bx2] + CB, writes=[pbf])

        def stats_finish(tb, three=False):
            pt, pbf = stat_banks[tb]
            rs, brs = sF[tb] if three else sF[tb % 2]
            S.op("act", lambda e: e.activation(out=rs[:, 0:512], in_=pt[:], func=AF.Ln, scale=1.0 / D, bias=eps_t[:]),
                 reads=[pbf] + CB, writes=[brs])
            S.op("act", lambda e: e.activation(out=rs[:, 0:512], in_=rs[:, 0:512], func=AF.Exp, scale=-0.5), reads=[brs], writes=[brs])
            return rs, brs

        def emit_norm(l, w, inc=False):
            sh0 = 0 if w == 0 else 24
            order = (2, 0, 1) if w == 0 else (0, 1, 2)
            rss = {tb: stats_finish(tb, three=True) for tb in order}
            for tb in order:
                emit_norm_tb(l, w, sh0, tb, rss[tb])

        def emit_norm_tb(l, w, sh0, tb, rsb):
            for k in range(8):
                emit_norm_k(l, w, sh0, tb, rsb, k)

        def emit_norm_k(l, w, sh0, tb, rsb, k):
            v = 0 if tb < 2 else 1
            rs, brs = rsb
            if True:
                tm, btm = sF[3 + k % 4]
                S.op("dve", lambda e, k=k, tm=tm: e.tensor_tensor(
                    out=tm[:, 0:512], in0=xT[:, k, tb * 512:(tb + 1) * 512], in1=rs[:, 0:512], op=ALU.mult),
                    reads=xb(k, tb) + [brs], writes=[btm])
                S.op("act", lambda e, k=k, tm=tm: e.activation(
                    out=hT[:, k, tb * 512:(tb + 1) * 512], in_=tm[:, 0:512], func=AF.Identity,
                    scale=a12[:, l, w, k, v:v + 1], bias=modT[:, l, sh0 + k, v:v + 1]),
                    reads=[btm, b_a12[l], b_mod[l]], writes=[b_hT[k][tb * 4 + i] for i in range(4)])

        def emit_params():
            pt, pbf = ring5.next()
            S.op("pe", lambda e, pt=pt: e.transpose(pt[:, 0:72], stA[0:72, :], ident[0:72, 0:72]), reads=bst + CB, writes=[pbf])
            S.op("act", lambda e, pt=pt: e.activation(out=prm[:, 0:72], in_=pt[:, 0:72], func=AF.Copy), reads=[pbf], writes=[b_prm])
            pt, pbf = ring5.next()
            S.op("pe", lambda e, pt=pt: e.transpose(pt[:, 0:96], stB[0:96, :], ident[0:96, 0:96]), reads=bst + CB, writes=[pbf])
            S.op("act", lambda e, pt=pt: e.activation(out=prm[:, 72:168], in_=pt[:, 0:96], func=AF.Copy), reads=[pbf], writes=[b_prm])
            S.op("act", lambda e: e.activation(
                out=csil[:], in_=prm[:, P_CV:P_CV + 16].rearrange("p (v k) -> p k v", v=2), func=AF.Silu),
                reads=[b_prm], writes=[b_csil])


        emit_params()

        for pos, tt in enumerate([8, 9, 10, 11, 0, 1, 2, 3, 4, 5, 6, 7]):
            xi, bxi = xin[pos % 6]
            src = xp[tt * 128:(tt + 1) * 128, :] if tt < 8 else xs[(tt - 8) * 128:(tt - 7) * 128, :]
            S.dma("sp", xi, src, writes=[bxi])
            for half in range(2):
                pt, pbf = ring5.next()

                def tr(e, half=half, pt=pt, xi=xi):
                    ins = None
                    for j in range(4):
                        k = half * 4 + j
                        ins = e.transpose(pt[:, j * 128:(j + 1) * 128], xi[:, k * 128:(k + 1) * 128], ident[:])
                    return ins
                S.op("pe", tr, reads=[bxi] + CB, writes=[pbf])
                eng = "dve"
                if eng == "act":
                    S.op("act", lambda e, half=half, pt=pt, tt=tt: e.activation(
                        out=xT[:, half * 4:(half + 1) * 4, tt * 128:(tt + 1) * 128],
                        in_=pt[:].rearrange("p (a b) -> p a b", a=4), func=AF.Copy),
                        reads=[pbf], writes=[b_xT[half * 4 + j][tt] for j in range(4)])
                else:
                    S.op("dve", lambda e, half=half, pt=pt, tt=tt: e.tensor_copy(
                        out=xT[:, half * 4:(half + 1) * 4, tt * 128:(tt + 1) * 128],
                        in_=pt[:].rearrange("p (a b) -> p a b", a=4)),
                        reads=[pbf], writes=[b_xT[half * 4 + j][tt] for j in range(4)])

            if pos % 4 == 3:
                for k in range(8):
                    stats_chunk(k, only_tb=tt // 4)
                emit_mod_slab(0, pos // 4, ps=ring5.next())
        emit_mod_slab(0, 3, ps=ring5.next())
        S.dma("sp", bm[:].rearrange("p a b c -> p (a b c)"), c_bm, writes=[bc[0]])
        S.dma("sp", smpd[:].rearrange("p a b c -> p (a b c)"), c_smpd, writes=[bc[3]])
        S.dma("sp", smph[:].rearrange("p a b c -> p (a b c)"), c_smph, writes=[bc[4]])
        S.dma("sp", ysel[:], c_ysel, writes=[bc[5]])
        S.dma("sp", ropec[:], c_ropec.rearrange("(t p) f -> p t f", p=128), writes=[bc[6]])
        S.dma("sp", ropes[:], c_ropes.rearrange("(t p) f -> p t f", p=128), writes=[bc[7]])
        for l in range(DEPTH):
            S.dma("sp", gst[:, l, 0, :], qg[l:l + 1, :].partition_broadcast(128).rearrange("p o f -> p (o f)"), writes=[bc[8 + l * 2]])
            S.dma("sp", gst[:, l, 1, :], kg[l:l + 1, :].partition_broadcast(128).rearrange("p o f -> p (o f)"), writes=[bc[9 + l * 2]])
        for h2 in range(2):
            S.dma("sp", wpst[h2 * 64:(h2 + 1) * 64, :, :, :],
                  w_pool[:, h2::2, :, :].rearrange("l g c d -> c l g d"), writes=[bwp[h2]])
        for h2 in range(2):
            S.op("dve", lambda e, h2=h2: e.tensor_copy(
                out=wpbd[h2 * 64:(h2 + 1) * 64, :, :, h2 * 64:(h2 + 1) * 64],
                in_=wpst[h2 * 64:(h2 + 1) * 64, :, :, :]), reads=bwp, writes=[b_wpbd])

        TILE_ORDER = [8, 9, 10, 11, 0, 1, 2, 3, 4, 5, 6, 7]
        ring_q = Ring(pbank[0:2])
        ring_kvu = Ring(pbank[2:4])
        ring_tq = Ring(pbank[4:6])
        ring_tk = Ring(pbank[6:8])
        ring_s = Ring(ppair[0:2])
        ring_num = Ring(pbank[4:6])
        ring_den = Ring(pbank[6:8])
        ring_pt = Ring(ptp)

        def attention_stream(groups, filler=None):
            if filler is None:
                r_s, r_num, r_den = ring_s, ring_num, ring_den
            else:
                r_s, r_num, r_den = Ring(ppair[0:1]), Ring(pbank[2:4]), Ring(pbank[4:6])
            units = []
            for g in groups:
                n = len(g["ktiles"])
                for i, kt in enumerate(g["ktiles"]):
                    units.append((g, i, n, kt))

            def emit_s(u):
                g, i, n, segs = u
                c, q0, qn = g["c"], g["q0"], g["qn"]
                spair, bsa, bsb = r_s.next()
                sa, sbk = spair[:, 0:512], spair[:, 512:1024]

                def smm(e):
                    ins = None
                    for (kap, kbuf, vap, vbuf, c0, cn) in segs:
                        e.matmul(sa[:, c0:c0 + cn], lhsT=kap[0:64, :], rhs=qT[0:64, c, q0 + c0:q0 + c0 + cn], start=True, stop=True)
                        ins = e.matmul(sbk[:, c0:c0 + cn], lhsT=kap[64:128, :], rhs=qT[64:128, c, q0 + c0:q0 + c0 + cn],
                                       start=True, stop=True)
                    return ins
                S.op("pe", smm, reads=[sg[1] for sg in segs] + g["q_bufs"], writes=[bsa, bsb])
                return spair, bsa, bsb

            def emit_rest(u, sres, mid=None):
                g, i, n, segs = u
                c, q0, qn = g["c"], g["q0"], g["qn"]
                spair, bsa, bsb = sres
                if i == 0:
                    g["num"] = r_num.next()
                    g["den"] = r_den.next()
                num, bnum = g["num"]
                den, bden = g["den"]
                assert qn == 512
                ptt, bpa = ring_pt.next()
                bpb = bpa
                pa, pb_ = ptt[:, 0:512], ptt[:, 512:1024]
                S.op("act", lambda e: e.activation(out=ptt[:, 0:1024], in_=spair[:, 0:1024], func=AF.Exp, scale=0.125),
                     reads=[bsa, bsb], writes=[bpa])
                if mid is not None:
                    mid()

                def pv(e):
                    ins = None
                    for si, (kap, kbuf, vap, vbuf, c0, cn) in enumerate(segs):
                        st_, sp_ = (i == 0 and si == 0), (i == n - 1)
                        kw = dict(skip_group_check=True) if len(segs) > 1 else {}
                        e.matmul(num[0:64, c0:c0 + cn], lhsT=vap[:, 0:64], rhs=pa[:, c0:c0 + cn], start=st_, stop=sp_, **kw)
                        e.matmul(num[64:128, c0:c0 + cn], lhsT=vap[:, 64:128], rhs=pb_[:, c0:c0 + cn], start=st_, stop=sp_, **kw)
                        e.matmul(den[0:64, c0:c0 + cn], lhsT=ones_bf[:, 0:64], rhs=pa[:, c0:c0 + cn], start=st_, stop=sp_, **kw)
                        ins = e.matmul(den[64:128, c0:c0 + cn], lhsT=ones_bf[:, 0:64], rhs=pb_[:, c0:c0 + cn], start=st_, stop=sp_, **kw)
                    return ins
                S.op("pe", pv, reads=[bpa, bpb] + [sg[3] for sg in segs] + CB, writes=[bnum, bden])
                if i == n - 1:
                    def fin():
                        rd, brd = sF[1 + c % 2]
                        S.op("act", lambda e: e.activation(out=rd[:, 0:qn], in_=den[:, 0:qn], func=AF.Ln), reads=[bden], writes=[brd])
                        S.op("act", lambda e: e.activation(out=rd[:, 0:qn], in_=rd[:, 0:qn], func=AF.Exp, scale=-1.0), reads=[brd], writes=[brd])
                        S.op("dve", lambda e: e.tensor_tensor(
                            out=aoT[:, c, q0:q0 + qn], in0=num[:, 0:qn], in1=rd[:, 0:qn], op=ALU.mult),
                            reads=[bnum, brd], writes=g["ao_buf"])
                        if g.get("after") is not None:
                            g["after"](g)
                    return fin
                return None

            if filler is not None:
                deferred = None
                for u in units:
                    sres = emit_s(u)
                    fin = emit_rest(u, sres, mid=next(filler, None))
                    if deferred is not None:
                        deferred()
                    deferred = fin
                if deferred is not None:
                    deferred()
                for f in filler:
                    f()
                return
            pending = emit_s(units[0])
            deferred = None
            for idx, u in enumerate(units):
                cur = pending
                if idx + 1 < len(units):
                    pending = emit_s(units[idx + 1])
                fin = emit_rest(u, cur)
                if deferred is not None:
                    deferred()
                deferred = fin
            if deferred is not None:
                deferred()

        def emit_pool(l, tb, ring=None, as_stages=False):
            ring = ring or ring_all
            st1, st2 = [], []
            for pair in range(2):
                a_, b_ = pool_pair(l, tb, pair, ring)
                st1.append(a_)
                st2.append(b_)
            if as_stages:
                return st1 + st2
            for f in (st1[0], st2[0], st1[1], st2[1]):
                f()

        def pool_pair(l, tb, pair, ring):
            pl, bpl = sH[pair]
            if True:

                def mm(e, pair, pt):
                    ins = None
                    for g2 in range(2):
                        g = pair * 2 + g2
                        for jt in range(4):
                            tile = tb * 4 + jt
                            contrib = []
                            if tb < 2:
                                if jt % 2 == 0:
                                    contrib.append((uB[:, tile, g * 64:(g + 1) * 64], bm[:, g, 1, :]))
                                    contrib.append((uB[:, tile + 1, g * 64:(g + 1) * 64], bm[:, g, 4, :]))
                                else:
                                    contrib.append((uB[:, tile, g * 64:(g + 1) * 64], bm[:, g, 2, :]))
                                    contrib.append((uB[:, tile - 1, g * 64:(g + 1) * 64], bm[:, g, 3, :]))
                            else:
                                if jt == 0:
                                    contrib.append((uB[:, tile, g * 64:(g + 1) * 64], smpd[:, g, 0, :]))
                                    contrib.append((uhalo[0:64, g * 64:(g + 1) * 64], smph[0:64, g, 0, :]))
                                elif jt == 3:
                                    contrib.append((uB[:, tile, g * 64:(g + 1) * 64], smpd[:, g, 1, :]))
                                    contrib.append((uhalo[0:64, g * 64:(g + 1) * 64], smph[0:64, g, 1, :]))
                                else:
                                    contrib.append((uB[:, tile, g * 64:(g + 1) * 64], bm[:, g, 0, :]))
                                if jt > 0:
                                    contrib.append((uB[:, tile - 1, g * 64:(g + 1) * 64], bm[:, g, 3, :]))
                                if jt < 3:
                                    contrib.append((uB[:, tile + 1, g * 64:(g + 1) * 64], bm[:, g, 4, :]))
                            n = len(contrib)
                            for ci, (la, ra) in enumerate(contrib):
                                ins = e.matmul(pt[g2 * 64:(g2 + 1) * 64, jt * 128:(jt + 1) * 128], lhsT=la, rhs=ra,
                                               start=(ci == 0), stop=(ci == n - 1))
                    return ins
                rds = [b_uB[tb * 4 + i] for i in range(4)] + CONSTS
                if tb == 2:
                    rds = rds + b_uhalo

                def stage1():
                    pt, pbf = ring.next()
                    S.op("pe", lambda e: mm(e, pair, pt), reads=rds, writes=[pbf])
                    S.op("dve", lambda e: e.tensor_copy(out=pl[:], in_=pt[:]), reads=[pbf], writes=[bpl])

                def stage2():
                    pt2, pbf2 = ring.next()
                    S.op("pe", lambda e: e.matmul(pt2[:], lhsT=wpbd[:, l, pair, :], rhs=pl[:], start=True, stop=True),
                         reads=[bpl, b_wpbd], writes=[pbf2])
                    S.op("act", lambda e: e.activation(
                        out=poT[:, pair, tb * 512:(tb + 1) * 512], in_=pt2[:], func=AF.Copy,
                        scale=prm[:, P_PSC + l * 2 + pair:P_PSC + l * 2 + pair + 1]),
                        reads=[pbf2, b_prm], writes=[b_po[pair][tb]])
                return stage1, stage2

        def emit_conv_core(l, i, tb, yap_l, yap_c, yap_r, ybufs, cb_ap, cb_bufs):
            acc, bacc = sF[5]
            shape3 = len(yap_c.shape) == 3
            accv = acc[:, 0:512].rearrange("p (s n) -> p s n", s=2) if shape3 else acc[:, 0:512]
            wcol = lambda j: prm[:, P_CVW + (l * 3 + j) * 2 + i:P_CVW + (l * 3 + j) * 2 + i + 1]
            S.op("dve", lambda e: e.tensor_scalar(out=accv, in0=yap_c, scalar1=wcol(1), scalar2=None, op0=ALU.mult),
                 reads=ybufs + [b_prm], writes=[bacc])
            S.op("dve", lambda e: e.scalar_tensor_tensor(out=accv, in0=yap_l, scalar=wcol(0), in1=accv, op0=ALU.mult, op1=ALU.add),
                 reads=ybufs + [b_prm, bacc], writes=[bacc])
            S.op("dve", lambda e: e.scalar_tensor_tensor(out=accv, in0=yap_r, scalar=wcol(2), in1=accv, op0=ALU.mult, op1=ALU.add),
                 reads=ybufs + [b_prm, bacc], writes=[bacc])
            S.op("dve", lambda e: e.tensor_tensor(out=coT[:, i, tb * 512:(tb + 1) * 512], in0=cb_ap, in1=acc[:, 0:512], op=ALU.mult),
                 reads=cb_bufs + [bacc], writes=[b_co[i][tb]])

        def emit_layer(l):
            transfer((F_ALL + B_R1) if l > 0 else [b for _, b in xin] + bst + bwp, A_R1 + A_R2)
            if l == 0:
                emit_norm(l, 0, inc=True)
            S.op("dve", lambda e: e.tensor_copy(
                out=g10[:, 0:512].rearrange("p (h d) -> p h d", h=8),
                in_=gst[:, l, 0, :].unsqueeze(1).broadcast_to([128, 8, 64])), reads=CONSTS, writes=[b_g10])
            S.op("dve", lambda e: e.tensor_copy(
                out=g10[:, 512:640].rearrange("p (h d) -> p h d", h=2),
                in_=gst[:, l, 1, :].unsqueeze(1).broadcast_to([128, 2, 64])), reads=CONSTS, writes=[b_g10])
            S.dma("pool", vall[:, 16:20, :], cv[l].rearrange("(t p) c -> p t c", p=128), writes=[b_vall[4]])
            ckin, b_ckin = ysm
            ckv = ckin[:, 0, 0:512].rearrange("p (t c) -> p t c", t=4)
            S.dma("sp", ckv, ck[l].rearrange("(t p) c -> p t c", p=128), writes=[b_ckin])
            wq, bwq = W.acquire(("tmq", l))
            wk, bwk = W.acquire(("tmk", l))
            pend = []
            tm_state = {}

            def tm_post(tt, src, bsrc):
                tq, btq = ring_tq.next()
                tk, btk = ring_tk.next()

                def trq(e):
                    ins = None
                    for c in range(4):
                        ins = e.transpose(tq[:, c * 128:(c + 1) * 128], src[:, c * 128:(c + 1) * 128], ident[:])
                    return ins
                S.op("pe", trq, reads=[bsrc] + CB, writes=[btq])
                S.op("pe", lambda e: e.transpose(tk[:, 0:128], src[:, 512:640], ident[:]), reads=[bsrc] + CB, writes=[btk])
                S.op("act", lambda e: e.activation(out=qT[:, :, tt * 128:(tt + 1) * 128],
                                                   in_=tq[:].rearrange("p (c n) -> p c n", c=4), func=AF.Copy),
                     reads=[btq], writes=[b_qT[tt]])
                if tt < 8:
                    S.op("dve", lambda e: e.tensor_copy(out=kT[:, tt * 128:(tt + 1) * 128], in_=tk[:, 0:128]),
                         reads=[btk], writes=[b_kT[tt]])
                else:
                    S.op("dve", lambda e: e.tensor_copy(out=pubKT[:, (tt - 8) * 128:(tt - 7) * 128], in_=tk[:, 0:128]),
                         reads=[btk], writes=[b_pubKT[tt - 8]])

            for idx, tt in enumerate(TILE_ORDER):
                qa, bqa = ring_q.next()
                kb, bkb = ring_kvu.next()

                def mmq(e, tt=tt, qa=qa):
                    ins = None
                    for k in range(8):
                        ins = e.matmul(qa[:], lhsT=hT[:, k, tt * 128:(tt + 1) * 128], rhs=wq[:, k * 512:(k + 1) * 512],
                                       start=(k == 0), stop=(k == 7))
                    return ins
                S.op("pe", mmq, reads=hb_tile(tt) + bwq, writes=[bqa])

                def mmk(e, tt=tt, kb=kb):
                    ins = None
                    for k in range(8):
                        ins = e.matmul(kb[:, 0:256], lhsT=hT[:, k, tt * 128:(tt + 1) * 128], rhs=wk[:, k * 256:(k + 1) * 256],
                                       start=(k == 0), stop=(k == 7))
                    for k in range(8):
                        ins = e.matmul(kb[:, 256:512], lhsT=hT[:, k, tt * 128:(tt + 1) * 128],
                                       rhs=wk[:, 2048 + k * 256:2048 + (k + 1) * 256], start=(k == 0), stop=(k == 7))
                    return ins
                S.op("pe", mmk, reads=hb_tile(tt) + bwk, writes=[bkb])
                if len(pend) >= 2:
                    tm_post(*pend.pop(0))
                if idx == 5:
                    bp = [Buf(f"pub{l}_{i}") for i in range(4)]
                    S.dma("sp", pub[l][0:256, :].rearrange("(p a) c -> p (a c)", a=2), pubKT, reads=b_pubKT, writes=[bp[0]])
                    S.dma("sp", pub[l][256:512, :].rearrange("(t q) (two c) -> (q two) t c", t=4, two=2), vB[:, 8:12, :],
                          reads=b_vB[8:12], writes=[bp[1]])
                    S.dma("sp", pub[l][512:520, :], uB[0:8, 8, :], reads=[b_uB[8]], writes=[bp[2]])
                    S.dma("sp", pub[l][520:528, :], uB[120:128, 11, :], reads=[b_uB[11]], writes=[bp[3]])
                    tm_state["b_g"] = Buf(f"gath{l}")
                    S.custom("pool", lambda e, l=l: e.collective_compute(
                        "AllGather", ALU.bypass, replica_groups=[[0, 1, 2, 3], [4, 5, 6, 7]], ins=[pub[l]], outs=[gath[l]]),
                        reads=bp, writes=[tm_state["b_g"]])
                sq, bsq = sF[0]
                S.op("act", lambda e, qa=qa, sq=sq: e.activation(out=sq[:, 0:512], in_=qa[:], func=AF.Square), reads=[bqa], writes=[bsq])
                S.op("act", lambda e, kb=kb, sq=sq: e.activation(out=sq[:, 512:640], in_=kb[:, 0:128], func=AF.Square), reads=[bkb], writes=[bsq])
                ss, bss = small[idx % 2]
                S.op("dve", lambda e, sq=sq, ss=ss: e.tensor_reduce(
                    out=ss[:, 0:10], in_=sq[:, 0:640].rearrange("p (h d) -> p h d", h=10), axis=AX.X, op=ALU.add),
                    reads=[bsq], writes=[bss])
                S.op("act", lambda e, ss=ss: e.activation(out=ss[:, 0:10], in_=ss[:, 0:10], func=AF.Sqrt, scale=1.0 / 64, bias=eps_t[:]),
                     reads=[bss] + CB, writes=[bss])
                S.op("dve", lambda e, ss=ss: e.reciprocal(out=ss[:, 0:10], in_=ss[:, 0:10]), reads=[bss], writes=[bss])
                qk, bqk = sF[1 + idx % 2]
                S.op("dve", lambda e, qa=qa, qk=qk, ss=ss: e.tensor_tensor(
                    out=qk[:, 0:512].rearrange("p (c two d) -> p two c d", two=2, d=64),
                    in0=qa[:].rearrange("p (two c d) -> p two c d", two=2, d=64),
                    in1=ss[:, 0:8].rearrange("p (two c) -> p two c", two=2).unsqueeze(3).broadcast_to([128, 2, 4, 64]),
                    op=ALU.mult), reads=[bqa, bss], writes=[bqk])
                S.op("dve", lambda e, kb=kb, qk=qk, ss=ss: e.tensor_tensor(
                    out=qk[:, 512:640].rearrange("p (h d) -> p h d", h=2),
                    in0=kb[:, 0:128].rearrange("p (h d) -> p h d", h=2),
                    in1=ss[:, 8:10].unsqueeze(2).broadcast_to([128, 2, 64]),
                    op=ALU.mult), reads=[bkb, bss], writes=[bqk])
                S.op("dve", lambda e, qk=qk: e.tensor_tensor(out=qk[:, 0:640], in0=qk[:, 0:640], in1=g10[:, :], op=ALU.mult),
                     reads=[bqk, b_g10], writes=[bqk])
                S.op("act", lambda e, kb=kb, tt=tt: e.activation(out=vB[:, tt, :], in_=kb[:, 128:256], func=AF.Copy),
                     reads=[bkb], writes=[b_vB[tt]])
                S.op("act", lambda e, kb=kb, tt=tt: e.activation(out=uB[:, tt, :], in_=kb[:, 256:512], func=AF.Copy),
                     reads=[bkb], writes=[b_uB[tt]])
                if tt < 8:
                    vft, bvf = vf[idx % 2]
                    S.op("act", lambda e, kb=kb, vft=vft: e.activation(out=vft[:], in_=kb[:, 128:256], func=AF.Copy),
                         reads=[bkb], writes=[bvf])
                    s_, hf = tt // 2, tt % 2
                    S.dma("sp", nv[s_, l, hf * 128:(hf + 1) * 128, :], vft[:], reads=[bvf])
                    S.dma("sp", nk[s_, l, hf * 128:(hf + 1) * 128, :], qk[:, 512:640], reads=[bqk])
                    pend.append((tt, qk, bqk))
                else:
                    ti = tt - 8
                    qr, bqr = sF[3 + idx % 2]
                    t1, bt1 = sF[5]
                    t2, bt2 = sF[6]
                    xv = qk[:, 0:640].rearrange("p (h a j f) -> p h a j f", h=10, a=2, j=2)
                    ov = qr[:, 0:640].rearrange("p (h a j f) -> p h a j f", h=10, a=2, j=2)
                    x1, x2 = xv[:, :, :, 0, :], xv[:, :, :, 1, :]
                    o1, o2 = ov[:, :, :, 0, :], ov[:, :, :, 1, :]
                    cb_ = ropec[:, ti, :].rearrange("p (a f) -> p a f", a=2).unsqueeze(1).broadcast_to([128, 10, 2, 16])
                    sb_ = ropes[:, ti, :].rearrange("p (a f) -> p a f", a=2).unsqueeze(1).broadcast_to([128, 10, 2, 16])
                    t1v = t1[:, 0:320].rearrange("p (h a f) -> p h a f", h=10, a=2)
                    t2v = t2[:, 0:320].rearrange("p (h a f) -> p h a f", h=10, a=2)
                    S.op("dve", lambda e, x1=x1, t1v=t1v, cb_=cb_: e.tensor_tensor(out=t1v, in0=x1, in1=cb_, op=ALU.mult),
                         reads=[bqk] + CONSTS, writes=[bt1])
                    S.op("dve", lambda e, x2=x2, t2v=t2v, sb_=sb_: e.tensor_tensor(out=t2v, in0=x2, in1=sb_, op=ALU.mult),
                         reads=[bqk] + CONSTS, writes=[bt2])
                    S.op("dve", lambda e, o1=o1, t1v=t1v, t2v=t2v: e.tensor_tensor(out=o1, in0=t1v, in1=t2v, op=ALU.subtract),
                         reads=[bt1, bt2], writes=[bqr])
                    S.op("dve", lambda e, x1=x1, t1v=t1v, sb_=sb_: e.tensor_tensor(out=t1v, in0=x1, in1=sb_, op=ALU.mult),
                         reads=[bqk] + CONSTS, writes=[bt1])
                    S.op("dve", lambda e, x2=x2, t2v=t2v, cb_=cb_: e.tensor_tensor(out=t2v, in0=x2, in1=cb_, op=ALU.mult),
                         reads=[bqk] + CONSTS, writes=[bt2])
                    S.op("dve", lambda e, o2=o2, t1v=t1v, t2v=t2v: e.tensor_tensor(out=o2, in0=t1v, in1=t2v, op=ALU.add),
                         reads=[bt1, bt2], writes=[bqr])
                    pend.append((tt, qr, bqr))
            while pend:
                tm_post(*pend.pop(0))
            W.release()
            W.release()

            pt, pbf = ring5.next()

            def trc(e, pt=pt):
                ins = None
                for t in range(4):
                    ins = e.transpose(pt[:, t * 128:(t + 1) * 128], ckv[:, t, :], ident[:])
                return ins
            S.op("pe", trc, reads=[b_ckin] + CB, writes=[pbf])
            S.op("act", lambda e, pt=pt: e.activation(out=kTall[:, 2048:2560], in_=pt[:], func=AF.Copy), reads=[pbf], writes=[b_kTall[4]])

            wc = [W.acquire(("conv", l, i)) for i in range(2)]
            b_g = tm_state["b_g"]
            ring_cv = Ring(pbank[6:8])
            yst = {}

            def f_yh():
                pt, pbf = ring_cv.next()

                def yh(e):
                    ins = None
                    for i in range(2):
                        wt = wc[i][0]
                        for which, off in ((0, 1024), (1, 2048)):
                            for k in range(8):
                                ins = e.matmul(pt[0:2, which * 256 + i * 128:which * 256 + (i + 1) * 128],
                                               lhsT=hT[:, k, 1024:1536:511], rhs=wt[:, off + k * 128:off + (k + 1) * 128],
                                               start=(k == 0), stop=(k == 7))
                    return ins
                S.op("pe", yh, reads=hb_tb(2) + wc[0][1] + wc[1][1], writes=[pbf])
                S.op("act", lambda e: e.activation(out=yhtmp[0][:], in_=pt[0:2, 256:512], func=AF.Copy), reads=[pbf], writes=[yhtmp[1]])
                S.op("dve", lambda e: e.tensor_tensor(out=yhal[0][:], in0=pt[0:2, 0:256], in1=yhtmp[0][:], op=ALU.mult),
                     reads=[pbf, yhtmp[1]], writes=[yhal[1]])
                bpy = Buf(f"pubY{l}")
                S.dma("sp", pubY[l], yhal[0][:], reads=[yhal[1]], writes=[bpy])
                yst["b_gY"] = Buf(f"gathY{l}")
                S.custom("pool", lambda e: e.collective_compute(
                    "AllGather", ALU.bypass, replica_groups=[[0, 1, 2, 3], [4, 5, 6, 7]], ins=[pubY[l]], outs=[gathY[l]]),
                    reads=[bpy], writes=[yst["b_gY"]])
            conv_groups = []

            def conv_block(i, tb):
                wt, bwt = wc[i]
                st_ = {}

                def grp(off):
                    pt_, bpt_ = ring_cv.next()

                    def mm(e):
                        ins = None
                        for k in range(8):
                            ins = e.matmul(pt_[:], lhsT=wt[:, off + k * 128:off + (k + 1) * 128],
                                           rhs=hT[:, k, tb * 512:(tb + 1) * 512], start=(k == 0), stop=(k == 7))
                        return ins
                    S.op("pe", mm, reads=hb_tb(tb) + bwt, writes=[bpt_])
                    return pt_, bpt_
                ccs, bccs = sF[6]

                def g_cc():
                    pcc, bcc = grp(1024)
                    S.op("act", lambda e: e.activation(out=ccs[:, 0:512], in_=pcc[:], func=AF.Copy), reads=[bcc], writes=[bccs])

                def g_ch():
                    pch, bch = grp(2048)
                    if tb < 2:
                        yb, byb = ybuf[0]
                        ybv = yb[:, 0:516].rearrange("p (s n) -> p s n", s=2)
                        S.op("dve", lambda e: e.tensor_tensor(
                            out=ybv[:, :, 1:257], in0=pch[:].rearrange("p (s n) -> p s n", s=2),
                            in1=ccs[:, 0:512].rearrange("p (s n) -> p s n", s=2), op=ALU.mult),
                            reads=[bch, bccs], writes=[byb])
                    else:
                        S.op("dve", lambda e: e.tensor_tensor(
                            out=ysm[0][:, i, 1:513], in0=pch[:], in1=ccs[:, 0:512], op=ALU.mult),
                            reads=[bch, bccs], writes=[ysm[1]])

                def g_cb():
                    pcb, bcb = grp(0)
                    if tb < 2:
                        yb, byb = ybuf[0]
                        ybv = yb[:, 0:516].rearrange("p (s n) -> p s n", s=2)
                        emit_conv_core(l, i, tb, ybv[:, :, 0:256], ybv[:, :, 1:257], ybv[:, :, 2:258], [byb], pcb[:], [bcb])
                    else:
                        S.op("act", lambda e: e.activation(out=cbs[0][:, i, :], in_=pcb[:], func=AF.Copy),
                             reads=[bcb], writes=[cbs[1]])
                return [g_cc, g_ch, g_cb]

            for i in range(2):
                for tb in (2, 0, 1):
                    conv_groups.extend(conv_block(i, tb))

            groups = []
            for tb in range(2):
                for c in range(4):
                    kts = []
                    for kt in range(2):
                        segs = []
                        for sq_ in range(2):
                            tile = (tb * 2 + sq_) * 2 + kt
                            segs.append((kT[:, tile * 128:(tile + 1) * 128], b_kT[tile], vB[:, tile, :], b_vB[tile], sq_ * 256, 256))
                        kts.append(segs)
                    groups.append(dict(c=c, q0=tb * 512, qn=512, ktiles=kts, ao_buf=[b_ao[c][tb * 2], b_ao[c][tb * 2 + 1]],
                                       q_bufs=[b_qT[tb * 4 + i] for i in range(4)], after=None))
            for r in range(4):
                S.dma("sp", kTall[:, r * 512:(r + 1) * 512],
                      gath[l][r * PUBR:r * PUBR + 256, :].rearrange("(p a) c -> p (a c)", a=2), reads=[b_g], writes=[b_kTall[r]])
                S.dma("sp", vall[:, r * 4:(r + 1) * 4, :],
                      gath[l][r * PUBR + 256:r * PUBR + 512, :].rearrange("(t q) (two c) -> (q two) t c", t=4, two=2),
                      reads=[b_g], writes=[b_vall[r]])
                S.dma("sp", uhalo[r * 16:(r + 1) * 16, :], gath[l][r * PUBR + 512:r * PUBR + 528, :], reads=[b_g], writes=[b_uhalo[r]])

            def f_ysl():
                for r in range(4):
                    S.dma("sp", yhalo[r * 2:(r + 1) * 2, :], gathY[l][r * 2:(r + 1) * 2, :], reads=[yst["b_gY"]], writes=[b_yhalo[r]])
                pt, pbf = ring_cv.next()

                def ysl(e):
                    ins = None
                    for i in range(2):
                        ins = e.matmul(pt[:, i * 2:i * 2 + 2], lhsT=yhalo[0:8, i * 128:(i + 1) * 128], rhs=ysel[0:8, 0:2], start=True, stop=True)
                    return ins
                S.op("pe", ysl, reads=b_yhalo + CONSTS, writes=[pbf])
                S.op("act", lambda e: e.activation(
                    out=ysm[0][:, :, 0:514:513], in_=pt[:, 0:4].rearrange("p (i w) -> p i w", i=2), func=AF.Copy),
                    reads=[pbf], writes=[ysm[1]])

            def f_convfin():
                for i in range(2):
                    emit_conv_core(l, i, 2, ysm[0][:, i, 0:512], ysm[0][:, i, 1:513], ysm[0][:, i, 2:514], [ysm[1]],
                                   cbs[0][:, i, :], [cbs[1]])
            fillers = conv_groups[:9] + [f_yh] + conv_groups[9:]
            for tb in range(2):
                fillers.extend(emit_pool(l, tb, ring=ring_cv, as_stages=True))
            fillers.extend([f_ysl, f_convfin])
            fillers.extend(emit_pool(l, 2, ring=ring_cv, as_stages=True))

            def spread(fs, n_units):
                it = iter(fs)
                n_two = max(0, len(fs) - n_units)
                for ui in range(n_units):
                    f = next(it, None)
                    g_ = next(it, None) if ui < n_two else None
                    if f is None:
                        return
                    yield (lambda f=f, g_=g_: (f(), g_() if g_ is not None else None))
                for f in it:
                    yield f
            attention_stream(groups, filler=spread(fillers, sum(len(g["ktiles"]) for g in groups)))
            W.release()
            W.release()

            groups = []
            for c in range(4):
                kts = []
                for kt in range(20):
                    kts.append([(kTall[:, kt * 128:(kt + 1) * 128], b_kTall[kt // 4], vall[:, kt, :], b_vall[kt // 4], 0, 512)])
                after = None
                if l == 0:
                    def after(g, c=c):
                        emit_mod_slab(0, 4 + 2 * c, ps=g["num"])
                        emit_mod_slab(0, 5 + 2 * c, ps=g["den"])
                groups.append(dict(c=c, q0=1024, qn=512, ktiles=kts, ao_buf=[b_ao[c][4]], q_bufs=[b_qT[8 + i] for i in range(4)],
                                   after=after))
            attention_stream(groups)

            transfer(A_R1, B_R1)
            for j in range(8):
                wt, bwt = W.acquire(("mrg", l, j))
                for tb in range(3):
                    gb = [ring_all.next() for _ in range(3)]

                    def mg(e, wt=wt, tb=tb, gb=gb):
                        ins = None
                        for br in range(3):
                            for k in range(8):
                                ins = e.matmul(gb[br][0][:], lhsT=wt[:, br * 1024 + k * 128:br * 1024 + (k + 1) * 128],
                                               rhs=hT[:, k, tb * 512:(tb + 1) * 512], start=(k == 0), stop=(k == 7))
                        return ins
                    S.op("pe", mg, reads=hb_tb(tb) + bwt, writes=[g[1] for g in gb])
                    sgs = []
                    for br in range(3):
                        sg, bsg = sH[br]
                        S.op("act", lambda e, br=br, sg=sg, gb=gb: e.activation(out=sg[:], in_=gb[br][0][:], func=AF.Sigmoid),
                             reads=[gb[br][1]], writes=[bsg])
                        sgs.append((sg, bsg))
                    bb = [ring_all.next() for _ in range(3)]

                    def mb(e, wt=wt, tb=tb, bb=bb):
                        ins = None
                        for k in range(2):
                            ins = e.matmul(bb[0][0][:], lhsT=wt[:, 3072 + k * 128:3072 + (k + 1) * 128],
                                           rhs=poT[:, k, tb * 512:(tb + 1) * 512], start=(k == 0), stop=(k == 1))
                        for k in range(2):
                            ins = e.matmul(bb[1][0][:], lhsT=wt[:, 3328 + k * 128:3328 + (k + 1) * 128],
                                           rhs=coT[:, k, tb * 512:(tb + 1) * 512], start=(k == 0), stop=(k == 1))
                        for k in range(4):
                            ins = e.matmul(bb[2][0][:], lhsT=wt[:, 3584 + k * 128:3584 + (k + 1) * 128],
                                           rhs=aoT[:, k, tb * 512:(tb + 1) * 512], start=(k == 0), stop=(k == 3))
                        return ins
                    aobufs = [b_ao[c][sg_] for c in range(4) for sg_ in ((0, 1) if tb == 0 else (2, 3) if tb == 1 else (4,))]
                    S.op("pe", mb, reads=bwt + [b_po[0][tb], b_po[1][tb], b_co[0][tb], b_co[1][tb]] + aobufs,
                         writes=[b[1] for b in bb])
                    t0, bt0 = sF[0]
                    t1, bt1 = sF[1]
                    S.op("dve", lambda e, t0=t0, bb=bb, sgs=sgs: e.tensor_tensor(out=t0[:, 0:512], in0=bb[0][0][:], in1=sgs[0][0][:], op=ALU.mult),
                         reads=[bb[0][1], sgs[0][1]], writes=[bt0])
                    S.op("dve", lambda e, t1=t1, bb=bb, sgs=sgs: e.tensor_tensor(out=t1[:, 0:512], in0=bb[1][0][:], in1=sgs[1][0][:], op=ALU.mult),
                         reads=[bb[1][1], sgs[1][1]], writes=[bt1])
                    S.op("dve", lambda e, t0=t0, t1=t1: e.tensor_tensor(out=t0[:, 0:512], in0=t0[:, 0:512], in1=t1[:, 0:512], op=ALU.add),
                         reads=[bt0, bt1], writes=[bt0])
                    S.op("dve", lambda e, t1=t1, bb=bb, sgs=sgs: e.tensor_tensor(out=t1[:, 0:512], in0=bb[2][0][:], in1=sgs[2][0][:], op=ALU.mult),
                         reads=[bb[2][1], sgs[2][1]], writes=[bt1])
                    S.op("dve", lambda e, t0=t0, t1=t1, j=j, tb=tb: e.tensor_tensor(
                        out=mgT[:, j, tb * 512:(tb + 1) * 512], in0=t0[:, 0:512], in1=t1[:, 0:512], op=ALU.add),
                        reads=[bt0, bt1], writes=[b_mg[j][tb]])
                W.release()

            for h in range(2):
                wt, bwt = W.acquire(("wo", l, h))
                for jj in range(4):
                    j = h * 4 + jj
                    for tb in range(3):
                        v = 0 if tb < 2 else 1
                        pt, pbf = ring5.next()

                        def mo(e, wt=wt, jj=jj, tb=tb, pt=pt):
                            ins = None
                            for k in range(8):
                                ins = e.matmul(pt[:], lhsT=wt[:, k * 512 + jj * 128:k * 512 + (jj + 1) * 128],
                                               rhs=mgT[:, k, tb * 512:(tb + 1) * 512], start=(k == 0), stop=(k == 7))
                            return ins
                        S.op("pe", mo, reads=bwt + [b_mg[k][tb] for k in range(8)], writes=[pbf])
                        S.op("dve", lambda e, pt=pt, j=j, tb=tb, v=v: e.scalar_tensor_tensor(
                            out=xT[:, j, tb * 512:(tb + 1) * 512], in0=pt[:], scalar=modT[:, l, 16 + j, v:v + 1],
                            in1=xT[:, j, tb * 512:(tb + 1) * 512], op0=ALU.mult, op1=ALU.add),
                            reads=[pbf, b_mod[l]] + xb(j, tb), writes=xb(j, tb))
                    if j > 0:
                        stats_chunk(j - 1)
                W.release()
            stats_chunk(7)

            emit_norm(l, 1, inc=True)
            transfer(B_R1 + A_R2, F_ALL)
            nxt = iter(range(12))
            for half in range(2):
                for fi in range(11):
                    wt, bwt = W.acquire(("gu", l, half * 11 + fi))
                    for tb in range(3):
                        pg, bpg = ring_all.next()
                        pu, bpu = ring_all.next()

                        def mgu(e, wt=wt, tb=tb, pg=pg, pu=pu):
                            ins = None
                            for pt_, off in ((pg, 0), (pu, 1024)):
                                for k in range(8):
                                    ins = e.matmul(pt_[:], lhsT=wt[:, off + k * 128:off + (k + 1) * 128],
                                                   rhs=hT[:, k, tb * 512:(tb + 1) * 512], start=(k == 0), stop=(k == 7))
                            return ins
                        S.op("pe", mgu, reads=hb_tb(tb) + bwt, writes=[bpg, bpu])
                        sl, bsl = sF[tb % 2]
                        S.op("act", lambda e, pg=pg, sl=sl: e.activation(out=sl[:, 0:512], in_=pg[:], func=AF.Silu), reads=[bpg], writes=[bsl])
                        S.op("dve", lambda e, pu=pu, sl=sl, fi=fi, tb=tb: e.tensor_tensor(
                            out=actT(fi)[:, tb * 512:(tb + 1) * 512], in0=pu[:], in1=sl[:, 0:512], op=ALU.mult),
                            reads=[bpu, bsl], writes=[b_act[fi][tb]])
                    W.release()
                    if l + 1 < DEPTH and fi % 2 == 1:
                        emit_mod_slab(l + 1, next(nxt))
                pend_d = []
                cnt_d = {0: 0, 1: 0, 2: 0}
                modq_d = []

                def flush_d():
                    k_, tb_ = pend_d.pop(0)
                    stats_chunk(k_, only_tb=tb_)
                    cnt_d[tb_] += 1
                    if cnt_d[tb_] == 8 and l + 1 < DEPTH:
                        rsb_ = stats_finish(tb_, three=True)
                        for k2 in range(8):
                            modq_d.append(lambda tb_=tb_, rsb_=rsb_, k2=k2: emit_norm_k(l + 1, 0, 0, tb_, rsb_, k2))
                for c in range(4):
                    wt, bwt = W.acquire(("down", l, half, c))
                    if half == 1 and c == 3 and l + 1 < DEPTH:
                        jt_order = [(jj, tb) for tb in (2, 0, 1) for jj in range(2)]
                    else:
                        jt_order = [(jj, tb) for jj in range(2) for tb in range(3)]
                    for (jj, tb) in jt_order:
                        j = c * 2 + jj
                        if True:
                            v = 0 if tb < 2 else 1
                            pt, pbf = ring5.next() if half == 1 else ring_all.next()

                            def md(e, wt=wt, jj=jj, tb=tb, pt=pt):
                                ins = None
                                for k in range(11):
                                    ins = e.matmul(pt[:], lhsT=wt[:, k * 256 + jj * 128:k * 256 + (jj + 1) * 128],
                                                   rhs=actT(k)[:, tb * 512:(tb + 1) * 512], start=(k == 0), stop=(k == 10))
                                return ins
                            S.op("pe", md, reads=bwt + [b_act[k][tb] for k in range(11)], writes=[pbf])
                            S.op("dve", lambda e, pt=pt, j=j, tb=tb, v=v: e.scalar_tensor_tensor(
                                out=xT[:, j, tb * 512:(tb + 1) * 512], in0=pt[:], scalar=modT[:, l, 40 + j, v:v + 1],
                                in1=xT[:, j, tb * 512:(tb + 1) * 512], op0=ALU.mult, op1=ALU.add),
                                reads=[pbf, b_mod[l]] + xb(j, tb), writes=xb(j, tb))
                            if half == 1:
                                pend_d.append((j, tb))
                                if len(pend_d) > 3:
                                    flush_d()
                                for _ in range(2):
                                    if modq_d:
                                        modq_d.pop(0)()
                    W.release()
                    if l + 1 < DEPTH and c == 1:
                        emit_mod_slab(l + 1, next(nxt), ps=(ring5.next() if half == 1 else None))
                while pend_d:
                    flush_d()
                while modq_d:
                    modq_d.pop(0)()

        for l in range(DEPTH):
            emit_layer(l)

        b_fin = [Buf(f"fin{i}") for i in range(4)]
        transfer(F_ALL, b_fin)
        yo = [(R1f[:, 4096 + i * 1024:4096 + (i + 1) * 1024], b_fin[i]) for i in range(2)]
        yT4s = [R2f[:, 0:4096].rearrange("p (k n) -> p k n", k=8), R1f[:, 0:4096].rearrange("p (k n) -> p k n", k=8)]
        b_yTs = [b_fin[2], b_fin[3]]
        fin_rs = [stats_finish(tb, three=True) for tb in range(3)]
        for tb in range(3):
            rs, brs = fin_rs[tb]
            yT4, b_yT = yT4s[tb % 2], b_yTs[tb % 2]
            for k in range(8):
                S.op("dve", lambda e, k=k, rs=rs, tb=tb, yT4=yT4: e.scalar_tensor_tensor(
                    out=yT4[:, k, :], in0=xT[:, k, tb * 512:(tb + 1) * 512], scalar=prm[:, P_FG + k:P_FG + k + 1],
                    in1=rs[:, 0:512], op0=ALU.mult, op1=ALU.mult),
                    reads=xb(k, tb) + [brs, b_prm], writes=[b_yT])
            for jt in range(4):
                tt = tb * 4 + jt
                yt, byt = yo[tt % 2]
                for half in range(2):
                    pt, pbf = ring_all.next()

                    def trf(e, half=half, pt=pt, jt=jt, yT4=yT4):
                        ins = None
                        for j in range(4):
                            k = half * 4 + j
                            ins = e.transpose(pt[:, j * 128:(j + 1) * 128], yT4[:, k, jt * 128:(jt + 1) * 128], ident[:])
                        return ins
                    S.op("pe", trf, reads=[b_yT] + CB, writes=[pbf])
                    S.op("act", lambda e, half=half, pt=pt, yt=yt: e.activation(out=yt[:, half * 512:(half + 1) * 512], in_=pt[:], func=AF.Copy),
                         reads=[pbf], writes=[byt])
                dst = yp[tt * 128:(tt + 1) * 128, :] if tt < 8 else ys[(tt - 8) * 128:(tt - 7) * 128, :]
                S.dma("sp", dst, yt, reads=[byt])

        S.finish()
        S.replay()
        build_program.stats = dict(sbuf=sb_bytes[0], n_ins=dict(S.n_ins), n_wait=S.n_wait, n_sems=len(S.sems), slabs=len(W.plan))
    return nc


def _band(Lseq, w):
    M = np.zeros((Lseq, Lseq), np.float32)
    cnt = np.zeros(Lseq, np.float32)
    for t in range(Lseq):
        lo = min(max(t - w // 2, 0), Lseq)
        hi = min(max(t + w // 2, 0), Lseq)
        M[lo:hi, t] = 1.0
        cnt[t] = hi - lo
        M[t, t] -= cnt[t]
    return M, cnt


_CONST_CACHE = {}


def _host_consts():
    if _CONST_CACHE:
        return _CONST_CACHE
    wins = (2, 4, 8, 16)
    Ms, cnts, c256 = [], [], []
    for w in wins:
        M, c = _band(2048, w)
        M = M / c[None, :]
        Ms.append(M)
        cnts.append(c)
        M2, c2 = _band(256, w)
        M2 = M2 / c2[None, :]
        c256.append(c2)
        assert np.array_equal(M2[0:128, 0:128], M[0:128, 0:128])
        assert np.array_equal(M2[128:256, 128:256], M[1920:2048, 1920:2048])
        assert np.array_equal(M2[0:128, 128:256], M[0:128, 128:256])
        assert np.array_equal(M2[128:256, 0:128], M[128:256, 0:128])
    bm = np.zeros((128, 4, 5, 128), np.float32)
    for g in range(4):
        M = Ms[g]
        bm[:, g, 0] = M[128:256, 128:256]
        bm[:, g, 1] = M[0:128, 0:128]
        bm[:, g, 2] = M[1920:2048, 1920:2048]
        bm[:, g, 3] = M[0:128, 128:256]
        bm[:, g, 4] = M[128:256, 0:128]
    per_rank = []
    inv = (10000.0 ** (-np.arange(16, dtype=np.float32) / np.float32(16))).astype(np.float32)
    for r in range(4):
        s0 = r * 512
        smpd = np.zeros((128, 4, 2, 128), np.float32)
        smph = np.zeros((64, 4, 2, 128), np.float32)
        for g in range(4):
            M = Ms[g]
            smpd[:, g, 0] = M[s0:s0 + 128, s0:s0 + 128]
            smpd[:, g, 1] = M[s0 + 384:s0 + 512, s0 + 384:s0 + 512]
            for rr in range(4):
                for i in range(16):
                    tok = rr * 512 + i if i < 8 else rr * 512 + 504 + (i - 8)
                    row = rr * 16 + i
                    if rr == r - 1 and i >= 8:
                        smph[row, g, 0] = M[tok, s0:s0 + 128]
                    if rr == r + 1 and i < 8:
                        smph[row, g, 1] = M[tok, s0 + 384:s0 + 512]
        ysel = np.zeros((8, 2), np.float32)
        if r > 0:
            ysel[(r - 1) * 2 + 1, 0] = 1.0
        if r < 3:
            ysel[(r + 1) * 2 + 0, 1] = 1.0
        t = np.arange(s0, s0 + 512)
        pr = (t // 64).astype(np.float32)
        pc = (t % 64).astype(np.float32)
        ang = np.stack([pr[:, None] * inv[None, :], pc[:, None] * inv[None, :]], axis=1).astype(np.float32)
        per_rank.append(dict(
            c_smpd=smpd.reshape(128, 1024).astype(ml_dtypes.bfloat16),
            c_smph=smph.reshape(64, 1024).astype(ml_dtypes.bfloat16),
            c_ysel=ysel.astype(ml_dtypes.bfloat16),
            c_ropec=np.cos(ang).astype(np.float32).reshape(512, 32),
            c_ropes=np.sin(ang).astype(np.float32).reshape(512, 32),
        ))
    _CONST_CACHE.update(dict(
        c_ident=np.eye(128, dtype=np.float32),
        c_bm=bm.reshape(128, 2560).astype(ml_dtypes.bfloat16),
        per_rank=per_rank,
    ))
    return _CONST_CACHE


_NC = None


def kernel(x_prompt, x_sample, cache_k, cache_v, c, c_ctx, norm1_g, norm2_g, w_ada, b_ada, w_in,
           w_pool, pool_scale, conv_w, q_norm_g, k_norm_g, w_br_pool, w_br_conv, w_br_attn, w_o,
           w_gate, w_up, w_down, final_g):
    global _NC
    f = lambda a: np.ascontiguousarray(np.asarray(a), dtype=np.float32)
    x_prompt, x_sample, cache_k, cache_v, c, c_ctx = map(f, (x_prompt, x_sample, cache_k, cache_v, c, c_ctx))
    shared = dict(n1g=f(norm1_g), n2g=f(norm2_g), w_ada=f(w_ada), b_ada=f(b_ada), w_in=f(w_in), w_pool=f(w_pool),
                  pool_scale=f(pool_scale), conv_w=f(conv_w), qg=f(q_norm_g), kg=f(k_norm_g), w_brp=f(w_br_pool),
                  w_brc=f(w_br_conv), w_bra=f(w_br_attn), w_o=f(w_o), w_gate=f(w_gate), w_up=f(w_up),
                  w_down=f(w_down), final_g=f(final_g))
    hc = _host_consts()
    if _NC is None:
        _NC = build_program()
    nc = _NC
    in_maps = []
    for core in range(8):
        b, r = core // 4, core % 4
        m = dict(shared)
        m["xp"] = x_prompt[core * 4:(core + 1) * 4].reshape(1024, D)
        m["xs"] = x_sample[b, r * 512:(r + 1) * 512, :]
        m["ck"] = cache_k[b].reshape(DEPTH, 512, 128)
        m["cv"] = cache_v[b].reshape(DEPTH, 512, 128)
        m["cvec"] = np.stack([c_ctx, c[b]], axis=0)
        m["c_ident"] = hc["c_ident"]
        m["c_bm"] = hc["c_bm"]
        m.update(hc["per_rank"][r])
        in_maps.append({k: np.ascontiguousarray(v) for k, v in m.items()})
    res = run_bass_kernel_spmd(nc, in_maps, core_ids=list(range(8)))
    R = res.results
    y_prompt = np.concatenate([np.asarray(R[i]["yp"], dtype=np.float32).reshape(4, 256, D) for i in range(8)], axis=0)
    y_sample = np.stack([np.concatenate([np.asarray(R[b * 4 + r]["ys"], dtype=np.float32) for r in range(4)], axis=0)
                         for b in range(2)], axis=0)
    nk_ = np.concatenate([np.asarray(R[i]["nk"], dtype=np.float32).reshape(4, DEPTH, 256, 2, 64) for i in range(8)], axis=0)
    nv_ = np.concatenate([np.asarray(R[i]["nv"], dtype=np.float32).reshape(4, DEPTH, 256, 2, 64) for i in range(8)], axis=0)
    return (y_prompt, y_sample, nk_, nv_)
```

```python
import contextlib
import numpy as np
import ml_dtypes
import concourse.bass as bass
import concourse.mybir as mybir
from concourse.bass_utils import run_bass_kernel_spmd

F32 = mybir.dt.float32
BF16 = mybir.dt.bfloat16
AF = mybir.ActivationFunctionType
ALU = mybir.AluOpType
AX = mybir.AxisListType

D = 1024
DEPTH = 2
NT = 1536
IN_W = 4864
OFF_Q = 1024
OFF_G = 1792
FFN = 2816
EPS = 1e-6
NSLOT = 3
PUBR = 528
P_N1G, P_N2G, P_FG, P_PSC, P_CVW, P_CV, P_BADA = 0, 16, 32, 40, 44, 56, 72


class Buf:
    __slots__ = ("name", "w", "r")

    def __init__(self, name=""):
        self.name = name
        self.w = None
        self.r = []


def transfer(old, new):
    ts = []
    for b in old:
        if b.w is not None:
            ts.append(b.w)
        ts.extend(b.r)
    red = {}
    for s, v in ts:
        if red.get(s, 0) < v:
            red[s] = v
    ts = list(red.items())
    for b in new:
        b.w = None
        b.r = list(ts)


class Sched:
    ENGS = ("pe", "act", "dve", "pool", "sp")

    def __init__(self, nc, stack, n_lanes=None):
        self.nc = nc
        self.stack = stack
        self.prog = {e: [] for e in self.ENGS}
        self.sems = []
        self.esem = {}
        self.cnt = {}
        for e in ("pe", "act", "dve", "pool"):
            self.esem[e] = self._newsem("s_" + e)
            self.cnt[e] = 0
        self.seen = {e: {} for e in self.ENGS}
        n_lanes = n_lanes or {"sp": 12, "pool": 8, "act": 6}
        self.lanes = {q: [[self._newsem(f"l_{q}{i}"), 0] for i in range(n)] for q, n in n_lanes.items()}
        self.lane_rr = {q: 0 for q in n_lanes}
        self.customs = []
        self.n_wait = 0
        self.n_ins = {e: 0 for e in self.ENGS}

    def _newsem(self, name):
        s = self.stack.enter_context(self.nc.semaphore(name))
        self.sems.append(s)
        return len(self.sems) - 1

    def _deps(self, engine, reads, writes):
        need = {}

        def add(t):
            s, v = t
            if engine == "pe" and s == self.esem["pe"]:
                return
            if need.get(s, 0) < v:
                need[s] = v
        for b in reads:
            if b.w is not None:
                add(b.w)
        for b in writes:
            if b.w is not None:
                add(b.w)
            for t in b.r:
                add(t)
        out = []
        seen = self.seen[engine]
        for s, v in need.items():
            if seen.get(s, 0) >= v:
                continue
            seen[s] = v
            out.append((s, v))
        return out

    def _commit(self, ticket, reads, writes):
        for b in writes:
            b.w = ticket
            b.r = []
        for b in reads:
            b.r.append(ticket)

    def op(self, engine, fn, reads=(), writes=()):
        waits = self._deps(engine, reads, writes)
        sem = self.esem[engine]
        self.cnt[engine] += 1
        ticket = (sem, self.cnt[engine])
        sems = self.sems

        def run(e, waits=waits, fn=fn, sem=sem):
            for s, v in waits:
                e.wait_ge(sems[s], v)
            ins = fn(e)
            ins.then_inc(sems[sem], 1)
        self.prog[engine].append(run)
        self.n_wait += len(waits)
        self.n_ins[engine] += 1
        self._commit(ticket, reads, writes)
        return ticket

    def dma(self, queue, out, in_, reads=(), writes=(), **kw):
        lanes = self.lanes[queue]
        i = self.lane_rr[queue]
        self.lane_rr[queue] = (i + 1) % len(lanes)
        lane = lanes[i]
        s = lane[0]
        waits = self._deps(queue, reads, writes)
        seen = self.seen[queue]
        if seen.get(s, 0) < lane[1]:
            seen[s] = lane[1]
            waits.append((s, lane[1]))
        lane[1] += 16
        ticket = (s, lane[1])
        sems = self.sems

        def run(e, waits=waits, s=s, out=out, in_=in_, kw=kw):
            for ws, v in waits:
                e.wait_ge(sems[ws], v)
            e.dma_start(out=out, in_=in_, **kw).then_inc(sems[s], 16)
        self.prog[queue].append(run)
        self.n_wait += len(waits)
        self.n_ins[queue] += 1
        self._commit(ticket, reads, writes)
        return ticket

    def custom(self, engine, fn, reads=(), writes=(), sem_inc=1):
        s = self._newsem(f"c_{len(self.sems)}")
        waits = self._deps(engine, reads, writes)
        ticket = (s, sem_inc)
        self.customs.append(ticket)
        sems = self.sems

        def run(e, waits=waits, s=s):
            for ws, v in waits:
                e.wait_ge(sems[ws], v)
            fn(e).then_inc(sems[s], sem_inc)
        self.prog[engine].append(run)
        self._commit(ticket, reads, writes)
        return ticket

    def finish(self):
        waits = []
        for q, lanes in self.lanes.items():
            for s, c in lanes:
                if c > 0:
                    waits.append((s, c))
        for e in ("pe", "act", "dve", "pool"):
            if self.cnt[e] > 0:
                waits.append((self.esem[e], self.cnt[e]))
        waits.extend(self.customs)
        sems = self.sems

        def run(e, waits=waits):
            for s, v in waits:
                e.wait_ge(sems[s], v)
        self.prog["sp"].append(run)

    def replay(self):
        nc = self.nc
        with nc.Block() as block:
            @block.sync
            def _(e):
                for f in self.prog["sp"]:
                    f(e)

            @block.scalar
            def _(e):
                for f in self.prog["act"]:
                    f(e)

            @block.vector
            def _(e):
                for f in self.prog["dve"]:
                    f(e)

            @block.gpsimd
            def _(e):
                for f in self.prog["pool"]:
                    f(e)

            @block.tensor
            def _(e):
                for f in self.prog["pe"]:
                    f(e)


class Ring:
    def __init__(self, items):
        self.items = items
        self.i = 0

    def next(self):
        it = self.items[self.i]
        self.i = (self.i + 1) % len(self.items)
        return it


class WStream:
    def __init__(self, S, slots):
        self.S = S
        self.slots = slots
        self.keys = []
        self.plan = []
        self.loaded = 0
        self.gate = []
        self.acquired = 0
        self.released = 0

    def add(self, pieces, key=None):
        self.plan.append(pieces)
        self.keys.append(key)

    def _pump(self):
        ns = len(self.slots)
        while self.loaded < len(self.plan) and self.loaded - ns < self.released:
            n = self.loaded
            t, b = self.slots[n % ns]
            for pi, (off, kc, ncols, src, plo, phi) in enumerate(self.plan[n]):
                dst = t[plo:phi, off:off + kc * ncols].rearrange("p (k n) -> p k n", k=kc)
                self.S.dma("pool", dst, src, reads=(self.gate if n == 0 else ()), writes=[b[pi]])
            self.loaded += 1

    def acquire(self, expect=None):
        self._pump()
        n = self.acquired
        assert n < self.loaded, "weight ring deadlock: too many slabs held"
        assert expect is None or self.keys[n] == expect, (n, self.keys[n], expect)
        self.acquired += 1
        t, b = self.slots[n % len(self.slots)]
        return t, b

    def release(self):
        self.released += 1
        self._pump()


def build_program():
    nc = bass.Bass("TRN2", target_bir_lowering=False)

    def din(name, shape, dt=F32):
        return nc.dram_tensor(name, list(shape), dt, kind="ExternalInput").ap()

    def dout(name, shape, dt=F32):
        return nc.dram_tensor(name, list(shape), dt, kind="ExternalOutput").ap()

    xp = din("xp", [1024, D])
    xs = din("xs", [512, D])
    ck = din("ck", [DEPTH, 512, 128])
    cv = din("cv", [DEPTH, 512, 128])
    cvec = din("cvec", [2, D])
    n1g = din("n1g", [DEPTH, D])
    n2g = din("n2g", [DEPTH, D])
    w_ada = din("w_ada", [DEPTH, D, 6 * D])
    b_ada = din("b_ada", [DEPTH, 6 * D])
    w_in = din("w_in", [DEPTH, D, IN_W])
    w_pool = din("w_pool", [DEPTH, 4, 64, 64])
    pool_scale = din("pool_scale", [DEPTH, 256])
    conv_w = din("conv_w", [DEPTH, 3, 256])
    qg = din("qg", [DEPTH, 64])
    kg = din("kg", [DEPTH, 64])
    w_brp = din("w_brp", [DEPTH, 256, D])
    w_brc = din("w_brc", [DEPTH, 256, D])
    w_bra = din("w_bra", [DEPTH, 512, D])
    w_o = din("w_o", [DEPTH, D, D])
    w_gate = din("w_gate", [DEPTH, D, FFN])
    w_up = din("w_up", [DEPTH, D, FFN])
    w_down = din("w_down", [DEPTH, FFN, D])
    final_g = din("final_g", [D])
    c_ident = din("c_ident", [128, 128])
    c_bm = din("c_bm", [128, 4 * 5 * 128], BF16)
    c_smpd = din("c_smpd", [128, 4 * 2 * 128], BF16)
    c_smph = din("c_smph", [64, 4 * 2 * 128], BF16)
    c_ysel = din("c_ysel", [8, 2], BF16)
    c_ropec = din("c_ropec", [512, 32])
    c_ropes = din("c_ropes", [512, 32])

    yp = dout("yp", [1024, D])
    ys = dout("ys", [512, D])
    nk = dout("nk", [4, DEPTH, 256, 128])
    nv = dout("nv", [4, DEPTH, 256, 128])

    pub = [nc.dram_tensor(f"pub{l}", [PUBR, 256], BF16, kind="Internal").ap() for l in range(DEPTH)]
    gath = [nc.dram_tensor(f"gath{l}", [4 * PUBR, 256], BF16, kind="Internal").ap() for l in range(DEPTH)]
    pubY = [nc.dram_tensor(f"pubY{l}", [2, 256], BF16, kind="Internal").ap() for l in range(DEPTH)]
    gathY = [nc.dram_tensor(f"gathY{l}", [8, 256], BF16, kind="Internal").ap() for l in range(DEPTH)]

    with contextlib.ExitStack() as st:
        S = Sched(nc, st)

        sb_bytes = [0]

        def sb(name, shape, dt):
            n = 1
            for d_ in shape[1:]:
                n *= d_
            sb_bytes[0] += n * (4 if dt == F32 else 2)
            return st.enter_context(nc.sbuf_tensor(name, list(shape), dt))

        pbank = []
        ppair = []
        for i in range(4):
            t = st.enter_context(nc.psum_tensor(f"pp{i}", [128, 1024], F32))
            b0_, b1_ = Buf(f"pb{2 * i}"), Buf(f"pb{2 * i + 1}")
            pbank.append((t[:, 0:512], b0_))
            pbank.append((t[:, 512:1024], b1_))
            ppair.append((t, b0_, b1_))
        ring_all = Ring(pbank)

        ident = sb("ident", [128, 128], F32)
        ones_bf = sb("ones_bf", [128, 128], BF16)
        eps_t = sb("eps_t", [128, 1], F32)
        bm = sb("bm", [128, 4, 5, 128], BF16)
        smpd = sb("smpd", [128, 4, 2, 128], BF16)
        smph = sb("smph", [64, 4, 2, 128], BF16)
        ysel = sb("ysel", [8, 2], BF16)
        ropec = sb("ropec", [128, 4, 32], F32)
        ropes = sb("ropes", [128, 4, 32], F32)
        gst = sb("gst", [128, DEPTH, 2, 64], F32)
        g10 = sb("g10", [128, 640], F32)
        prm = sb("prm", [128, 168], F32)
        csil = sb("csil", [128, 8, 2], BF16)
        modT = sb("modT", [128, DEPTH, 48, 2], F32)
        a12 = sb("a12", [128, DEPTH, 2, 8, 2], F32)
        wpbd = sb("wpbd", [128, DEPTH, 2, 128], BF16)
        b_ident = Buf("ident")
        b_ones = Buf("ones")
        b_epsb = Buf("eps")
        CB = [b_ident, b_ones, b_epsb]
        b_prm = Buf("prm")
        b_csil = Buf("csil")
        b_mod = [Buf(f"mod{l}") for l in range(DEPTH)]
        b_a12 = [Buf(f"a12{l}") for l in range(DEPTH)]
        b_g10 = Buf("g10")
        b_wpbd = Buf("wpbd")

        xT = sb("xT", [128, 8, NT], F32)
        hT = sb("hT", [128, 8, NT], BF16)
        b_xT = [[Buf(f"xT{k}_{t}") for t in range(12)] for k in range(8)]
        b_hT = [[Buf(f"hT{k}_{t}") for t in range(12)] for k in range(8)]

        def xb(k, tb):
            return [b_xT[k][tb * 4 + i] for i in range(4)]

        def hb_tb(tb):
            return [b_hT[k][tb * 4 + i] for k in range(8) for i in range(4)]

        def hb_tile(tt):
            return [b_hT[k][tt] for k in range(8)]

        slots = [(sb(f"wslot{i}", [128, 4096], BF16), [Buf(f"wslot{i}_{j}") for j in range(8)]) for i in range(NSLOT)]
        W = WStream(S, slots)

        R1 = sb("R1", [128, 12288], BF16)
        R2 = sb("R2", [128, 12288], BF16)
        qT = R1[:, 0:6144].rearrange("p (c n) -> p c n", c=4)
        kT = R1[:, 6144:7168]
        pubKT = R1[:, 7168:7680]
        vB = R1[:, 7680:9216].rearrange("p (t c) -> p t c", t=12)
        uB = R1[:, 9216:12288].rearrange("p (t c) -> p t c", t=12)
        b_qT = [Buf(f"qT{t}") for t in range(12)]
        b_kT = [Buf(f"kT{t}") for t in range(8)]
        b_pubKT = [Buf(f"pubKT{t}") for t in range(4)]
        b_vB = [Buf(f"vB{t}") for t in range(12)]
        b_uB = [Buf(f"uB{t}") for t in range(12)]
        A_R1 = b_qT + b_kT + b_pubKT + b_vB + b_uB
        mgT = R1[:, :].rearrange("p (c n) -> p c n", c=8)
        b_mg = [[Buf(f"mg{j}_{tb}") for tb in range(3)] for j in range(8)]
        B_R1 = [b for row in b_mg for b in row]
        aoT = R2[:, 0:6144].rearrange("p (c n) -> p c n", c=4)
        poT = R2[:, 6144:9216].rearrange("p (c n) -> p c n", c=2)
        coT = R2[:, 9216:12288].rearrange("p (c n) -> p c n", c=2)
        b_ao = [[Buf(f"ao{c}_{sg}") for sg in range(5)] for c in range(4)]
        b_po = [[Buf(f"po{c}_{tb}") for tb in range(3)] for c in range(2)]
        b_co = [[Buf(f"co{c}_{tb}") for tb in range(3)] for c in range(2)]
        A_R2 = [b for row in b_ao for b in row] + [b for row in b_po for b in row] + [b for row in b_co for b in row]
        def actT(fi):
            if fi < 8:
                return R1[:, fi * NT:(fi + 1) * NT]
            return R2[:, (fi - 8) * NT:(fi - 7) * NT]
        b_act = [[Buf(f"act{f}_{tb}") for tb in range(3)] for f in range(11)]
        F_ALL = [b for row in b_act for b in row]
        R1f = R1[:, :].bitcast(F32)
        R2f = R2[:, :].bitcast(F32)
        stA = R1f[0:72, 0:128]
        stB = R1f[0:96, 128:256]
        wpst = R1f[:, 256:512].rearrange("p (l g d) -> p l g d", l=DEPTH, g=2)

        kTall = sb("kTall", [128, 2560], BF16)
        vall = sb("vall", [128, 20, 128], BF16)
        uhalo = sb("uhalo", [64, 256], BF16)
        yhalo = sb("yhalo", [8, 256], BF16)
        b_kTall = [Buf(f"kTall{r}") for r in range(5)]
        b_vall = [Buf(f"vall{r}") for r in range(5)]
        b_uhalo = [Buf(f"uh{r}") for r in range(4)]
        b_yhalo = [Buf(f"yh{r}") for r in range(4)]

        def scr(name, shape, dt):
            return (sb(name, shape, dt), Buf(name))
        sF = [scr(f"sF{i}", [128, 640], F32) for i in range(7)]
        sH = [scr(f"sH{i}", [128, 512], BF16) for i in range(4)]
        ptp = [scr(f"ptp{i}", [128, 1024], BF16) for i in range(2)]
        small = [scr(f"sm{i}", [128, 16], F32) for i in range(4)]
        ybuf = [scr(f"ybuf{i}", [128, 516], F32) for i in range(1)]
        ysm = scr("ysm", [128, 2, 514], F32)
        cbs = scr("cbs", [128, 2, 512], BF16)
        vf = [scr(f"vf{i}", [128, 128], F32) for i in range(2)]
        yhal = scr("yhal", [2, 256], BF16)
        yhtmp = (sF[0][0][0:2, 0:256], sF[0][1])
        xin = [(R2f[:, i * 1024:(i + 1) * 1024], Buf(f"xin{i}")) for i in range(6)]

        S.dma("sp", ident[:], c_ident, writes=[b_ident])
        bst = [Buf(f"st{i}") for i in range(7)]
        S.dma("act", stA[0:16, :], n1g.rearrange("l (k c) -> (l k) c", c=128), writes=[bst[0]])
        S.dma("act", stA[16:32, :], n2g.rearrange("l (k c) -> (l k) c", c=128), writes=[bst[1]])
        S.dma("act", stA[32:40, :], final_g.rearrange("(k c) -> k c", c=128), writes=[bst[2]])
        S.dma("act", stA[40:44, :], pool_scale.rearrange("l (k c) -> (l k) c", c=128), writes=[bst[3]])
        S.dma("act", stA[44:56, :], conv_w.rearrange("l j (k c) -> (l j k) c", c=128), writes=[bst[4]])
        S.dma("act", stA[56:72, :], cvec.rearrange("v (k c) -> (v k) c", c=128), writes=[bst[5]])
        S.dma("act", stB[0:96, :], b_ada.rearrange("l (k c) -> (l k) c", c=128), writes=[bst[6]])
        bc = [Buf(f"c{i}") for i in range(16)]
        CONSTS = CB + bc
        bwp = [Buf(f"wp{i}") for i in range(2)]
        S.op("pool", lambda e: e.memset(ones_bf[:], 1.0), writes=[b_ones])
        S.op("pool", lambda e: e.memset(eps_t[:], EPS), writes=[b_epsb])
        for i in range(1):
            S.op("pool", lambda e, i=i: e.memset(ybuf[i][0][:], 0.0), writes=[ybuf[i][1]])
        S.op("pool", lambda e: e.memset(wpbd[:].rearrange("p a b c -> p (a b c)"), 0.0), writes=[b_wpbd])

        def wsl(ap2d):
            return ap2d.rearrange("(k p) n -> p k n", p=128)

        def plan_ada_slab(l, jb):
            W.add([(0, 8, 512, wsl(w_ada[l, :, jb * 512:(jb + 1) * 512]), 0, 128)], key=("ada", l, jb))

        def plan_layer(l):
            W.add([(0, 8, 512, wsl(w_in[l, :, OFF_Q:OFF_Q + 512]), 0, 128)], key=("tmq", l))
            W.add([(0, 8, 256, wsl(w_in[l, :, 1536:1792]), 0, 128),
                   (2048, 8, 256, wsl(w_in[l, :, 0:256]), 0, 128)], key=("tmk", l))
            for i in range(2):
                W.add([(0, 8, 128, wsl(w_in[l, :, 256 + i * 128:256 + (i + 1) * 128]), 0, 128),
                       (1024, 8, 128, wsl(w_in[l, :, 512 + i * 128:512 + (i + 1) * 128]), 0, 128),
                       (2048, 8, 128, wsl(w_in[l, :, 768 + i * 128:768 + (i + 1) * 128]), 0, 128)], key=("conv", l, i))
            if l == 0:
                for jb in range(4, 12):
                    plan_ada_slab(0, jb)
            for j in range(8):
                pcs = []
                for br in range(3):
                    c0 = OFF_G + br * 1024 + j * 128
                    pcs.append((br * 1024, 8, 128, wsl(w_in[l, :, c0:c0 + 128]), 0, 128))
                pcs.append((3072, 2, 128, wsl(w_brp[l, :, j * 128:(j + 1) * 128]), 0, 128))
                pcs.append((3328, 2, 128, wsl(w_brc[l, :, j * 128:(j + 1) * 128]), 0, 128))
                for h2 in range(2):
                    src = w_bra[l, h2 * 256:(h2 + 1) * 256, j * 128:(j + 1) * 128].rearrange("(c p) n -> p c n", p=64)
                    pcs.append((3584, 4, 128, src, h2 * 64, (h2 + 1) * 64))
                W.add(pcs, key=("mrg", l, j))
            for h in range(2):
                W.add([(0, 8, 512, wsl(w_o[l, :, h * 512:(h + 1) * 512]), 0, 128)], key=("wo", l, h))
            nxt = iter(range(12))
            for half in range(2):
                for fi in range(11):
                    f = half * 11 + fi
                    W.add([(0, 8, 128, wsl(w_gate[l, :, f * 128:(f + 1) * 128]), 0, 128),
                           (1024, 8, 128, wsl(w_up[l, :, f * 128:(f + 1) * 128]), 0, 128)], key=("gu", l, f))
                    if l + 1 < DEPTH and fi % 2 == 1:
                        plan_ada_slab(l + 1, next(nxt))
                for c in range(4):
                    W.add([(0, 11, 256, wsl(w_down[l, half * 1408:(half + 1) * 1408, c * 256:(c + 1) * 256]), 0, 128)],
                          key=("down", l, half, c))
                    if l + 1 < DEPTH and c == 1:
                        plan_ada_slab(l + 1, next(nxt))

        for jb in range(4):
            plan_ada_slab(0, jb)
        for l in range(DEPTH):
            plan_layer(l)

        def emit_mod_slab(l, jb, ps=None):
            wt, wb = W.acquire(("ada", l, jb))
            pt, pbf = ps if ps is not None else ring_all.next()

            def mm(e, wt=wt, pt=pt):
                ins = None
                for jj in range(4):
                    for k in range(8):
                        ins = e.matmul(pt[:, jj * 2:jj * 2 + 2], lhsT=wt[:, k * 512 + jj * 128:k * 512 + (jj + 1) * 128],
                                       rhs=csil[:, k, :], start=(k == 0), stop=(k == 7))
                return ins
            S.op("pe", mm, reads=wb + [b_csil], writes=[pbf])
            W.release()
            S.op("dve", lambda e, pt=pt, jb=jb: e.tensor_tensor(
                out=modT[:, l, jb * 4:(jb + 1) * 4, :],
                in0=pt[:, 0:8].rearrange("p (a b) -> p a b", b=2),
                in1=prm[:, P_BADA + l * 48 + jb * 4:P_BADA + l * 48 + jb * 4 + 4].unsqueeze(2).broadcast_to([128, 4, 2]),
                op=ALU.add), reads=[pbf, b_prm], writes=[b_mod[l]])
            if jb == 3 or jb == 9:
                w, sc0, pg = (0, 8, P_N1G) if jb == 3 else (1, 32, P_N2G)
                S.op("dve", lambda e: e.scalar_tensor_tensor(
                    out=a12[:, l, w, :, :], in0=modT[:, l, sc0:sc0 + 8, :], scalar=1.0,
                    in1=prm[:, pg + l * 8:pg + l * 8 + 8].unsqueeze(2).broadcast_to([128, 8, 2]),
                    op0=ALU.add, op1=ALU.mult), reads=[b_mod[l], b_prm], writes=[b_a12[l]])

        def emit_stats(tb, src_bufs_fn):
            pt, pbf = ring_all.next()
            for k in range(8):
                x2, bx2 = sH[k % 2]
                S.op("act", lambda e, k=k, x2=x2: e.activation(out=x2[:], in_=xT[:, k, tb * 512:(tb + 1) * 512], func=AF.Square),
                     reads=xb(k, tb), writes=[bx2])
                S.op("pe", lambda e, k=k, x2=x2, pt=pt: e.matmul(pt[:], lhsT=ones_bf[:], rhs=x2[:], start=(k == 0), stop=(k == 7)),
                     reads=[bx2] + CB, writes=[pbf])
            rs, brs = sF[tb % 2]
            S.op("act", lambda e, pt=pt, rs=rs: e.activation(out=rs[:, 0:512], in_=pt[:], func=AF.Sqrt, scale=1.0 / D, bias=eps_t[:]),
                 reads=[pbf] + CB, writes=[brs])
            S.op("dve", lambda e, rs=rs: e.reciprocal(out=rs[:, 0:512], in_=rs[:, 0:512]), reads=[brs], writes=[brs])
            return rs, brs

        ring5 = Ring(pbank[0:5])
        stat_banks = pbank[5:8]
        ring_x2 = Ring(sH[0:4])

        def stats_chunk(k, only_tb=None):
            for tb in (range(3) if only_tb is None else [only_tb]):
                x2, bx2 = ring_x2.next()
                pt, pbf = stat_banks[tb]
                S.op("act", lambda e, x2=x2, tb=tb: e.activation(out=x2[:], in_=xT[:, k, tb * 512:(tb + 1) * 512], func=AF.Square),
                     reads=xb(k, tb), writes=[bx2])
                S.op("pe", lambda e, x2=x2, pt=pt: e.matmul(pt[:], lhsT=ones_bf[:], rhs=x2[:], start=(k == 0), stop=(k == 7)),
                     reads=[bx2] + CB, writes=[pbf])

        def stats_finish(tb, three=False):
            pt, pbf = stat_banks[tb]
            rs, brs = sF[tb] if three else sF[tb % 2]
            S.op("act", lambda e: e.activation(out=rs[:, 0:512], in_=pt[:], func=AF.Ln, scale=1.0 / D, bias=eps_t[:]),
                 reads=[pbf] + CB, writes=[brs])
            S.op("act", lambda e: e.activation(out=rs[:, 0:512], in_=rs[:, 0:512], func=AF.Exp, scale=-0.5), reads=[brs], writes=[brs])
            return rs, brs

        def emit_norm(l, w, inc=False):
            sh0 = 0 if w == 0 else 24
            order = (2, 0, 1) if w == 0 else (0, 1, 2)
            rss = {tb: stats_finish(tb, three=True) for tb in order}
            for tb in order:
                emit_norm_tb(l, w, sh0, tb, rss[tb])

        def emit_norm_tb(l, w, sh0, tb, rsb):
            for k in range(8):
                emit_norm_k(l, w, sh0, tb, rsb, k)

        def emit_norm_k(l, w, sh0, tb, rsb, k):
            v = 0 if tb < 2 else 1
            rs, brs = rsb
            if True:
                tm, btm = sF[3 + k % 4]
                S.op("dve", lambda e, k=k, tm=tm: e.tensor_tensor(
                    out=tm[:, 0:512], in0=xT[:, k, tb * 512:(tb + 1) * 512], in1=rs[:, 0:512], op=ALU.mult),
                    reads=xb(k, tb) + [brs], writes=[btm])
                S.op("act", lambda e, k=k, tm=tm: e.activation(
                    out=hT[:, k, tb * 512:(tb + 1) * 512], in_=tm[:, 0:512], func=AF.Identity,
                    scale=a12[:, l, w, k, v:v + 1], bias=modT[:, l, sh0 + k, v:v + 1]),
                    reads=[btm, b_a12[l], b_mod[l]], writes=[b_hT[k][tb * 4 + i] for i in range(4)])

        def emit_params():
            pt, pbf = ring5.next()
            S.op("pe", lambda e, pt=pt: e.transpose(pt[:, 0:72], stA[0:72, :], ident[0:72, 0:72]), reads=bst + CB, writes=[pbf])
            S.op("act", lambda e, pt=pt: e.activation(out=prm[:, 0:72], in_=pt[:, 0:72], func=AF.Copy), reads=[pbf], writes=[b_prm])
            pt, pbf = ring5.next()
            S.op("pe", lambda e, pt=pt: e.transpose(pt[:, 0:96], stB[0:96, :], ident[0:96, 0:96]), reads=bst + CB, writes=[pbf])
            S.op("act", lambda e, pt=pt: e.activation(out=prm[:, 72:168], in_=pt[:, 0:96], func=AF.Copy), reads=[pbf], writes=[b_prm])
            S.op("act", lambda e: e.activation(
                out=csil[:], in_=prm[:, P_CV:P_CV + 16].rearrange("p (v k) -> p k v", v=2), func=AF.Silu),
                reads=[b_prm], writes=[b_csil])


        emit_params()

        for pos, tt in enumerate([8, 9, 10, 11, 0, 1, 2, 3, 4, 5, 6, 7]):
            xi, bxi = xin[pos % 6]
            src = xp[tt * 128:(tt + 1) * 128, :] if tt < 8 else xs[(tt - 8) * 128:(tt - 7) * 128, :]
            S.dma("sp", xi, src, writes=[bxi])
            for half in range(2):
                pt, pbf = ring5.next()

                def tr(e, half=half, pt=pt, xi=xi):
                    ins = None
                    for j in range(4):
                        k = half * 4 + j
                        ins = e.transpose(pt[:, j * 128:(j + 1) * 128], xi[:, k * 128:(k + 1) * 128], ident[:])
                    return ins
                S.op("pe", tr, reads=[bxi] + CB, writes=[pbf])
                eng = "dve"
                if eng == "act":
                    S.op("act", lambda e, half=half, pt=pt, tt=tt: e.activation(
                        out=xT[:, half * 4:(half + 1) * 4, tt * 128:(tt + 1) * 128],
                        in_=pt[:].rearrange("p (a b) -> p a b", a=4), func=AF.Copy),
                        reads=[pbf], writes=[b_xT[half * 4 + j][tt] for j in range(4)])
                else:
                    S.op("dve", lambda e, half=half, pt=pt, tt=tt: e.tensor_copy(
                        out=xT[:, half * 4:(half + 1) * 4, tt * 128:(tt + 1) * 128],
                        in_=pt[:].rearrange("p (a b) -> p a b", a=4)),
                        reads=[pbf], writes=[b_xT[half * 4 + j][tt] for j in range(4)])

            if pos % 4 == 3:
                for k in range(8):
                    stats_chunk(k, only_tb=tt // 4)
                emit_mod_slab(0, pos // 4, ps=ring5.next())
        emit_mod_slab(0, 3, ps=ring5.next())
        S.dma("sp", bm[:].rearrange("p a b c -> p (a b c)"), c_bm, writes=[bc[0]])
        S.dma("sp", smpd[:].rearrange("p a b c -> p (a b c)"), c_smpd, writes=[bc[3]])
        S.dma("sp", smph[:].rearrange("p a b c -> p (a b c)"), c_smph, writes=[bc[4]])
        S.dma("sp", ysel[:], c_ysel, writes=[bc[5]])
        S.dma("sp", ropec[:], c_ropec.rearrange("(t p) f -> p t f", p=128), writes=[bc[6]])
        S.dma("sp", ropes[:], c_ropes.rearrange("(t p) f -> p t f", p=128), writes=[bc[7]])
        for l in range(DEPTH):
            S.dma("sp", gst[:, l, 0, :], qg[l:l + 1, :].partition_broadcast(128).rearrange("p o f -> p (o f)"), writes=[bc[8 + l * 2]])
            S.dma("sp", gst[:, l, 1, :], kg[l:l + 1, :].partition_broadcast(128).rearrange("p o f -> p (o f)"), writes=[bc[9 + l * 2]])
        for h2 in range(2):
            S.dma("sp", wpst[h2 * 64:(h2 + 1) * 64, :, :, :],
                  w_pool[:, h2::2, :, :].rearrange("l g c d -> c l g d"), writes=[bwp[h2]])
        for h2 in range(2):
            S.op("dve", lambda e, h2=h2: e.tensor_copy(
                out=wpbd[h2 * 64:(h2 + 1) * 64, :, :, h2 * 64:(h2 + 1) * 64],
                in_=wpst[h2 * 64:(h2 + 1) * 64, :, :, :]), reads=bwp, writes=[b_wpbd])

        TILE_ORDER = [8, 9, 10, 11, 0, 1, 2, 3, 4, 5, 6, 7]
        ring_q = Ring(pbank[0:2])
        ring_kvu = Ring(pbank[2:4])
        ring_tq = Ring(pbank[4:6])
        ring_tk = Ring(pbank[6:8])
        ring_s = Ring(ppair[0:2])
        ring_num = Ring(pbank[4:6])
        ring_den = Ring(pbank[6:8])
        ring_pt = Ring(ptp)

        def attention_stream(groups, filler=None):
            if filler is None:
                r_s, r_num, r_den = ring_s, ring_num, ring_den
            else:
                r_s, r_num, r_den = Ring(ppair[0:1]), Ring(pbank[2:4]), Ring(pbank[4:6])
            units = []
            for g in groups:
                n = len(g["ktiles"])
                for i, kt in enumerate(g["ktiles"]):
                    units.append((g, i, n, kt))

            def emit_s(u):
                g, i, n, segs = u
                c, q0, qn = g["c"], g["q0"], g["qn"]
                spair, bsa, bsb = r_s.next()
                sa, sbk = spair[:, 0:512], spair[:, 512:1024]

                def smm(e):
                    ins = None
                    for (kap, kbuf, vap, vbuf, c0, cn) in segs:
                        e.matmul(sa[:, c0:c0 + cn], lhsT=kap[0:64, :], rhs=qT[0:64, c, q0 + c0:q0 + c0 + cn], start=True, stop=True)
                        ins = e.matmul(sbk[:, c0:c0 + cn], lhsT=kap[64:128, :], rhs=qT[64:128, c, q0 + c0:q0 + c0 + cn],
                                       start=True, stop=True)
                    return ins
                S.op("pe", smm, reads=[sg[1] for sg in segs] + g["q_bufs"], writes=[bsa, bsb])
                return spair, bsa, bsb

            def emit_rest(u, sres, mid=None):
                g, i, n, segs = u
                c, q0, qn = g["c"], g["q0"], g["qn"]
                spair, bsa, bsb = sres
                if i == 0:
                    g["num"] = r_num.next()
                    g["den"] = r_den.next()
                num, bnum = g["num"]
                den, bden = g["den"]
                assert qn == 512
                ptt, bpa = ring_pt.next()
                bpb = bpa
                pa, pb_ = ptt[:, 0:512], ptt[:, 512:1024]
                S.op("act", lambda e: e.activation(out=ptt[:, 0:1024], in_=spair[:, 0:1024], func=AF.Exp, scale=0.125),
                     reads=[bsa, bsb], writes=[bpa])
                if mid is not None:
                    mid()

                def pv(e):
                    ins = None
                    for si, (kap, kbuf, vap, vbuf, c0, cn) in enumerate(segs):
                        st_, sp_ = (i == 0 and si == 0), (i == n - 1)
                        kw = dict(skip_group_check=True) if len(segs) > 1 else {}
                        e.matmul(num[0:64, c0:c0 + cn], lhsT=vap[:, 0:64], rhs=pa[:, c0:c0 + cn], start=st_, stop=sp_, **kw)
                        e.matmul(num[64:128, c0:c0 + cn], lhsT=vap[:, 64:128], rhs=pb_[:, c0:c0 + cn], start=st_, stop=sp_, **kw)
                        e.matmul(den[0:64, c0:c0 + cn], lhsT=ones_bf[:, 0:64], rhs=pa[:, c0:c0 + cn], start=st_, stop=sp_, **kw)
                        ins = e.matmul(den[64:128, c0:c0 + cn], lhsT=ones_bf[:, 0:64], rhs=pb_[:, c0:c0 + cn], start=st_, stop=sp_, **kw)
                    return ins
                S.op("pe", pv, reads=[bpa, bpb] + [sg[3] for sg in segs] + CB, writes=[bnum, bden])
                if i == n - 1:
                    def fin():
                        rd, brd = sF[1 + c % 2]
                        S.op("act", lambda e: e.activation(out=rd[:, 0:qn], in_=den[:, 0:qn], func=AF.Ln), reads=[bden], writes=[brd])
                        S.op("act", lambda e: e.activation(out=rd[:, 0:qn], in_=rd[:, 0:qn], func=AF.Exp, scale=-1.0), reads=[brd], writes=[brd])
                        S.op("dve", lambda e: e.tensor_tensor(
                            out=aoT[:, c, q0:q0 + qn], in0=num[:, 0:qn], in1=rd[:, 0:qn], op=ALU.mult),
                            reads=[bnum, brd], writes=g["ao_buf"])
                        if g.get("after") is not None:
                            g["after"](g)
                    return fin
                return None

            if filler is not None:
                deferred = None
                for u in units:
                    sres = emit_s(u)
                    fin = emit_rest(u, sres, mid=next(filler, None))
                    if deferred is not None:
                        deferred()
                    deferred = fin
                if deferred is not None:
                    deferred()
                for f in filler:
                    f()
                return
            pending = emit_s(units[0])
            deferred = None
            for idx, u in enumerate(units):
                cur = pending
                if idx + 1 < len(units):
                    pending = emit_s(units[idx + 1])
                fin = emit_rest(u, cur)
                if deferred is not None:
                    deferred()
                deferred = fin
            if deferred is not None:
                deferred()

        def emit_pool(l, tb, ring=None, as_stages=False):
            ring = ring or ring_all
            st1, st2 = [], []
            for pair in range(2):
                a_, b_ = pool_pair(l, tb, pair, ring)
                st1.append(a_)
                st2.append(b_)
            if as_stages:
                return st1 + st2
            for f in (st1[0], st2[0], st1[1], st2[1]):
                f()

        def pool_pair(l, tb, pair, ring):
            pl, bpl = sH[pair]
            if True:

                def mm(e, pair, pt):
                    ins = None
                    for g2 in range(2):
                        g = pair * 2 + g2
                        for jt in range(4):
                            tile = tb * 4 + jt
                            contrib = []
                            if tb < 2:
                                if jt % 2 == 0:
                                    contrib.append((uB[:, tile, g * 64:(g + 1) * 64], bm[:, g, 1, :]))
                                    contrib.append((uB[:, tile + 1, g * 64:(g + 1) * 64], bm[:, g, 4, :]))
                                else:
                                    contrib.append((uB[:, tile, g * 64:(g + 1) * 64], bm[:, g, 2, :]))
                                    contrib.append((uB[:, tile - 1, g * 64:(g + 1) * 64], bm[:, g, 3, :]))
                            else:
                                if jt == 0:
                                    contrib.append((uB[:, tile, g * 64:(g + 1) * 64], smpd[:, g, 0, :]))
                                    contrib.append((uhalo[0:64, g * 64:(g + 1) * 64], smph[0:64, g, 0, :]))
                                elif jt == 3:
                                    contrib.append((uB[:, tile, g * 64:(g + 1) * 64], smpd[:, g, 1, :]))
                                    contrib.append((uhalo[0:64, g * 64:(g + 1) * 64], smph[0:64, g, 1, :]))
                                else:
                                    contrib.append((uB[:, tile, g * 64:(g + 1) * 64], bm[:, g, 0, :]))
                                if jt > 0:
                                    contrib.append((uB[:, tile - 1, g * 64:(g + 1) * 64], bm[:, g, 3, :]))
                                if jt < 3:
                                    contrib.append((uB[:, tile + 1, g * 64:(g + 1) * 64], bm[:, g, 4, :]))
                            n = len(contrib)
                            for ci, (la, ra) in enumerate(contrib):
                                ins = e.matmul(pt[g2 * 64:(g2 + 1) * 64, jt * 128:(jt + 1) * 128], lhsT=la, rhs=ra,
                                               start=(ci == 0), stop=(ci == n - 1))
                    return ins
                rds = [b_uB[tb * 4 + i] for i in range(4)] + CONSTS
                if tb == 2:
                    rds = rds + b_uhalo

                def stage1():
                    pt, pbf = ring.next()
                    S.op("pe", lambda e: mm(e, pair, pt), reads=rds, writes=[pbf])
                    S.op("dve", lambda e: e.tensor_copy(out=pl[:], in_=pt[:]), reads=[pbf], writes=[bpl])

                def stage2():
                    pt2, pbf2 = ring.next()
                    S.op("pe", lambda e: e.matmul(pt2[:], lhsT=wpbd[:, l, pair, :], rhs=pl[:], start=True, stop=True),
                         reads=[bpl, b_wpbd], writes=[pbf2])
                    S.op("act", lambda e: e.activation(
                        out=poT[:, pair, tb * 512:(tb + 1) * 512], in_=pt2[:], func=AF.Copy,
                        scale=prm[:, P_PSC + l * 2 + pair:P_PSC + l * 2 + pair + 1]),
                        reads=[pbf2, b_prm], writes=[b_po[pair][tb]])
                return stage1, stage2

        def emit_conv_core(l, i, tb, yap_l, yap_c, yap_r, ybufs, cb_ap, cb_bufs):
            acc, bacc = sF[5]
            shape3 = len(yap_c.shape) == 3
            accv = acc[:, 0:512].rearrange("p (s n) -> p s n", s=2) if shape3 else acc[:, 0:512]
            wcol = lambda j: prm[:, P_CVW + (l * 3 + j) * 2 + i:P_CVW + (l * 3 + j) * 2 + i + 1]
            S.op("dve", lambda e: e.tensor_scalar(out=accv, in0=yap_c, scalar1=wcol(1), scalar2=None, op0=ALU.mult),
                 reads=ybufs + [b_prm], writes=[bacc])
            S.op("dve", lambda e: e.scalar_tensor_tensor(out=accv, in0=yap_l, scalar=wcol(0), in1=accv, op0=ALU.mult, op1=ALU.add),
                 reads=ybufs + [b_prm, bacc], writes=[bacc])
            S.op("dve", lambda e: e.scalar_tensor_tensor(out=accv, in0=yap_r, scalar=wcol(2), in1=accv, op0=ALU.mult, op1=ALU.add),
                 reads=ybufs + [b_prm, bacc], writes=[bacc])
            S.op("dve", lambda e: e.tensor_tensor(out=coT[:, i, tb * 512:(tb + 1) * 512], in0=cb_ap, in1=acc[:, 0:512], op=ALU.mult),
                 reads=cb_bufs + [bacc], writes=[b_co[i][tb]])

        def emit_layer(l):
            transfer((F_ALL + B_R1) if l > 0 else [b for _, b in xin] + bst + bwp, A_R1 + A_R2)
            if l == 0:
                emit_norm(l, 0, inc=True)
            S.op("dve", lambda e: e.tensor_copy(
                out=g10[:, 0:512].rearrange("p (h d) -> p h d", h=8),
                in_=gst[:, l, 0, :].unsqueeze(1).broadcast_to([128, 8, 64])), reads=CONSTS, writes=[b_g10])
            S.op("dve", lambda e: e.tensor_copy(
                out=g10[:, 512:640].rearrange("p (h d) -> p h d", h=2),
                in_=gst[:, l, 1, :].unsqueeze(1).broadcast_to([128, 2, 64])), reads=CONSTS, writes=[b_g10])
            S.dma("pool", vall[:, 16:20, :], cv[l].rearrange("(t p) c -> p t c", p=128), writes=[b_vall[4]])
            ckin, b_ckin = ysm
            ckv = ckin[:, 0, 0:512].rearrange("p (t c) -> p t c", t=4)
            S.dma("sp", ckv, ck[l].rearrange("(t p) c -> p t c", p=128), writes=[b_ckin])
            wq, bwq = W.acquire(("tmq", l))
            wk, bwk = W.acquire(("tmk", l))
            pend = []
            tm_state = {}

            def tm_post(tt, src, bsrc):
                tq, btq = ring_tq.next()
                tk, btk = ring_tk.next()

                def trq(e):
                    ins = None
                    for c in range(4):
                        ins = e.transpose(tq[:, c * 128:(c + 1) * 128], src[:, c * 128:(c + 1) * 128], ident[:])
                    return ins
                S.op("pe", trq, reads=[bsrc] + CB, writes=[btq])
                S.op("pe", lambda e: e.transpose(tk[:, 0:128], src[:, 512:640], ident[:]), reads=[bsrc] + CB, writes=[btk])
                S.op("act", lambda e: e.activation(out=qT[:, :, tt * 128:(tt + 1) * 128],
                                                   in_=tq[:].rearrange("p (c n) -> p c n", c=4), func=AF.Copy),
                     reads=[btq], writes=[b_qT[tt]])
                if tt < 8:
                    S.op("dve", lambda e: e.tensor_copy(out=kT[:, tt * 128:(tt + 1) * 128], in_=tk[:, 0:128]),
                         reads=[btk], writes=[b_kT[tt]])
                else:
                    S.op("dve", lambda e: e.tensor_copy(out=pubKT[:, (tt - 8) * 128:(tt - 7) * 128], in_=tk[:, 0:128]),
                         reads=[btk], writes=[b_pubKT[tt - 8]])

            for idx, tt in enumerate(TILE_ORDER):
                qa, bqa = ring_q.next()
                kb, bkb = ring_kvu.next()

                def mmq(e, tt=tt, qa=qa):
                    ins = None
                    for k in range(8):
                        ins = e.matmul(qa[:], lhsT=hT[:, k, tt * 128:(tt + 1) * 128], rhs=wq[:, k * 512:(k + 1) * 512],
                                       start=(k == 0), stop=(k == 7))
                    return ins
                S.op("pe", mmq, reads=hb_tile(tt) + bwq, writes=[bqa])

                def mmk(e, tt=tt, kb=kb):
                    ins = None
                    for k in range(8):
                        ins = e.matmul(kb[:, 0:256], lhsT=hT[:, k, tt * 128:(tt + 1) * 128], rhs=wk[:, k * 256:(k + 1) * 256],
                                       start=(k == 0), stop=(k == 7))
                    for k in range(8):
                        ins = e.matmul(kb[:, 256:512], lhsT=hT[:, k, tt * 128:(tt + 1) * 128],
                                       rhs=wk[:, 2048 + k * 256:2048 + (k + 1) * 256], start=(k == 0), stop=(k == 7))
                    return ins
                S.op("pe", mmk, reads=hb_tile(tt) + bwk, writes=[bkb])
                if len(pend) >= 2:
                    tm_post(*pend.pop(0))
                if idx == 5:
                    bp = [Buf(f"pub{l}_{i}") for i in range(4)]
                    S.dma("sp", pub[l][0:256, :].rearrange("(p a) c -> p (a c)", a=2), pubKT, reads=b_pubKT, writes=[bp[0]])
                    S.dma("sp", pub[l][256:512, :].rearrange("(t q) (two c) -> (q two) t c", t=4, two=2), vB[:, 8:12, :],
                          reads=b_vB[8:12], writes=[bp[1]])
                    S.dma("sp", pub[l][512:520, :], uB[0:8, 8, :], reads=[b_uB[8]], writes=[bp[2]])
                    S.dma("sp", pub[l][520:528, :], uB[120:128, 11, :], reads=[b_uB[11]], writes=[bp[3]])
                    tm_state["b_g"] = Buf(f"gath{l}")
                    S.custom("pool", lambda e, l=l: e.collective_compute(
                        "AllGather", ALU.bypass, replica_groups=[[0, 1, 2, 3], [4, 5, 6, 7]], ins=[pub[l]], outs=[gath[l]]),
                        reads=bp, writes=[tm_state["b_g"]])
                sq, bsq = sF[0]
                S.op("act", lambda e, qa=qa, sq=sq: e.activation(out=sq[:, 0:512], in_=qa[:], func=AF.Square), reads=[bqa], writes=[bsq])
                S.op("act", lambda e, kb=kb, sq=sq: e.activation(out=sq[:, 512:640], in_=kb[:, 0:128], func=AF.Square), reads=[bkb], writes=[bsq])
                ss, bss = small[idx % 2]
                S.op("dve", lambda e, sq=sq, ss=ss: e.tensor_reduce(
                    out=ss[:, 0:10], in_=sq[:, 0:640].rearrange("p (h d) -> p h d", h=10), axis=AX.X, op=ALU.add),
                    reads=[bsq], writes=[bss])
                S.op("act", lambda e, ss=ss: e.activation(out=ss[:, 0:10], in_=ss[:, 0:10], func=AF.Sqrt, scale=1.0 / 64, bias=eps_t[:]),
                     reads=[bss] + CB, writes=[bss])
                S.op("dve", lambda e, ss=ss: e.reciprocal(out=ss[:, 0:10], in_=ss[:, 0:10]), reads=[bss], writes=[bss])
                qk, bqk = sF[1 + idx % 2]
                S.op("dve", lambda e, qa=qa, qk=qk, ss=ss: e.tensor_tensor(
                    out=qk[:, 0:512].rearrange("p (c two d) -> p two c d", two=2, d=64),
                    in0=qa[:].rearrange("p (two c d) -> p two c d", two=2, d=64),
                    in1=ss[:, 0:8].rearrange("p (two c) -> p two c", two=2).unsqueeze(3).broadcast_to([128, 2, 4, 64]),
                    op=ALU.mult), reads=[bqa, bss], writes=[bqk])
                S.op("dve", lambda e, kb=kb, qk=qk, ss=ss: e.tensor_tensor(
                    out=qk[:, 512:640].rearrange("p (h d) -> p h d", h=2),
                    in0=kb[:, 0:128].rearrange("p (h d) -> p h d", h=2),
                    in1=ss[:, 8:10].unsqueeze(2).broadcast_to([128, 2, 64]),
                    op=ALU.mult), reads=[bkb, bss], writes=[bqk])
                S.op("dve", lambda e, qk=qk: e.tensor_tensor(out=qk[:, 0:640], in0=qk[:, 0:640], in1=g10[:, :], op=ALU.mult),
                     reads=[bqk, b_g10], writes=[bqk])
                S.op("act", lambda e, kb=kb, tt=tt: e.activation(out=vB[:, tt, :], in_=kb[:, 128:256], func=AF.Copy),
                     reads=[bkb], writes=[b_vB[tt]])
                S.op("act", lambda e, kb=kb, tt=tt: e.activation(out=uB[:, tt, :], in_=kb[:, 256:512], func=AF.Copy),
                     reads=[bkb], writes=[b_uB[tt]])
                if tt < 8:
                    vft, bvf = vf[idx % 2]
                    S.op("act", lambda e, kb=kb, vft=vft: e.activation(out=vft[:], in_=kb[:, 128:256], func=AF.Copy),
                         reads=[bkb], writes=[bvf])
                    s_, hf = tt // 2, tt % 2
                    S.dma("sp", nv[s_, l, hf * 128:(hf + 1) * 128, :], vft[:], reads=[bvf])
                    S.dma("sp", nk[s_, l, hf * 128:(hf + 1) * 128, :], qk[:, 512:640], reads=[bqk])
                    pend.append((tt, qk, bqk))
                else:
                    ti = tt - 8
                    qr, bqr = sF[3 + idx % 2]
                    t1, bt1 = sF[5]
                    t2, bt2 = sF[6]
                    xv = qk[:, 0:640].rearrange("p (h a j f) -> p h a j f", h=10, a=2, j=2)
                    ov = qr[:, 0:640].rearrange("p (h a j f) -> p h a j f", h=10, a=2, j=2)
                    x1, x2 = xv[:, :, :, 0, :], xv[:, :, :, 1, :]
                    o1, o2 = ov[:, :, :, 0, :], ov[:, :, :, 1, :]
                    cb_ = ropec[:, ti, :].rearrange("p (a f) -> p a f", a=2).unsqueeze(1).broadcast_to([128, 10, 2, 16])
                    sb_ = ropes[:, ti, :].rearrange("p (a f) -> p a f", a=2).unsqueeze(1).broadcast_to([128, 10, 2, 16])
                    t1v = t1[:, 0:320].rearrange("p (h a f) -> p h a f", h=10, a=2)
                    t2v = t2[:, 0:320].rearrange("p (h a f) -> p h a f", h=10, a=2)
                    S.op("dve", lambda e, x1=x1, t1v=t1v, cb_=cb_: e.tensor_tensor(out=t1v, in0=x1, in1=cb_, op=ALU.mult),
                         reads=[bqk] + CONSTS, writes=[bt1])
                    S.op("dve", lambda e, x2=x2, t2v=t2v, sb_=sb_: e.tensor_tensor(out=t2v, in0=x2, in1=sb_, op=ALU.mult),
                         reads=[bqk] + CONSTS, writes=[bt2])
                    S.op("dve", lambda e, o1=o1, t1v=t1v, t2v=t2v: e.tensor_tensor(out=o1, in0=t1v, in1=t2v, op=ALU.subtract),
                         reads=[bt1, bt2], writes=[bqr])
                    S.op("dve", lambda e, x1=x1, t1v=t1v, sb_=sb_: e.tensor_tensor(out=t1v, in0=x1, in1=sb_, op=ALU.mult),
                         reads=[bqk] + CONSTS, writes=[bt1])
                    S.op("dve", lambda e, x2=x2, t2v=t2v, cb_=cb_: e.tensor_tensor(out=t2v, in0=x2, in1=cb_, op=ALU.mult),
                         reads=[bqk] + CONSTS, writes=[bt2])
                    S.op("dve", lambda e, o2=o2, t1v=t1v, t2v=t2v: e.tensor_tensor(out=o2, in0=t1v, in1=t2v, op=ALU.add),
                         reads=[bt1, bt2], writes=[bqr])
                    pend.append((tt, qr, bqr))
            while pend:
                tm_post(*pend.pop(0))
            W.release()
            W.release()

            pt, pbf = ring5.next()

            def trc(e, pt=pt):
                ins = None
                for t in range(4):
                    ins = e.transpose(pt[:, t * 128:(t + 1) * 128], ckv[:, t, :], ident[:])
                return ins
            S.op("pe", trc, reads=[b_ckin] + CB, writes=[pbf])
            S.op("act", lambda e, pt=pt: e.activation(out=kTall[:, 2048:2560], in_=pt[:], func=AF.Copy), reads=[pbf], writes=[b_kTall[4]])

            wc = [W.acquire(("conv", l, i)) for i in range(2)]
            pt, pbf = ring_all.next()

            def yh(e, pt=pt):
                ins = None
                for i in range(2):
                    wt = wc[i][0]
                    for which, off in ((0, 1024), (1, 2048)):
                        for k in range(8):
                            ins = e.matmul(pt[0:2, which * 256 + i * 128:which * 256 + (i + 1) * 128],
                                           lhsT=hT[:, k, 1024:1536:511], rhs=wt[:, off + k * 128:off + (k + 1) * 128],
                                           start=(k == 0), stop=(k == 7))
                return ins
            S.op("pe", yh, reads=hb_tb(2) + wc[0][1] + wc[1][1], writes=[pbf])
            S.op("act", lambda e, pt=pt: e.activation(out=yhtmp[0][:], in_=pt[0:2, 256:512], func=AF.Copy), reads=[pbf], writes=[yhtmp[1]])
            S.op("dve", lambda e, pt=pt: e.tensor_tensor(out=yhal[0][:], in0=pt[0:2, 0:256], in1=yhtmp[0][:], op=ALU.mult),
                 reads=[pbf, yhtmp[1]], writes=[yhal[1]])
            b_g = tm_state["b_g"]
            bpy = Buf(f"pubY{l}")
            S.dma("sp", pubY[l], yhal[0][:], reads=[yhal[1]], writes=[bpy])
            b_gY = Buf(f"gathY{l}")
            S.custom("pool", lambda e, l=l: e.collective_compute(
                "AllGather", ALU.bypass, replica_groups=[[0, 1, 2, 3], [4, 5, 6, 7]], ins=[pubY[l]], outs=[gathY[l]]),
                reads=[bpy], writes=[b_gY])
            ring_cv = Ring(pbank[6:8])
            conv_groups = []

            def conv_block(i, tb):
                wt, bwt = wc[i]
                st_ = {}

                def grp(off):
                    pt_, bpt_ = ring_cv.next()

                    def mm(e):
                        ins = None
                        for k in range(8):
                            ins = e.matmul(pt_[:], lhsT=wt[:, off + k * 128:off + (k + 1) * 128],
                                           rhs=hT[:, k, tb * 512:(tb + 1) * 512], start=(k == 0), stop=(k == 7))
                        return ins
                    S.op("pe", mm, reads=hb_tb(tb) + bwt, writes=[bpt_])
                    return pt_, bpt_
                ccs, bccs = sF[6]

                def g_cc():
                    pcc, bcc = grp(1024)
                    S.op("act", lambda e: e.activation(out=ccs[:, 0:512], in_=pcc[:], func=AF.Copy), reads=[bcc], writes=[bccs])

                def g_ch():
                    pch, bch = grp(2048)
                    if tb < 2:
                        yb, byb = ybuf[0]
                        ybv = yb[:, 0:516].rearrange("p (s n) -> p s n", s=2)
                        S.op("dve", lambda e: e.tensor_tensor(
                            out=ybv[:, :, 1:257], in0=pch[:].rearrange("p (s n) -> p s n", s=2),
                            in1=ccs[:, 0:512].rearrange("p (s n) -> p s n", s=2), op=ALU.mult),
                            reads=[bch, bccs], writes=[byb])
                    else:
                        S.op("dve", lambda e: e.tensor_tensor(
                            out=ysm[0][:, i, 1:513], in0=pch[:], in1=ccs[:, 0:512], op=ALU.mult),
                            reads=[bch, bccs], writes=[ysm[1]])

                def g_cb():
                    pcb, bcb = grp(0)
                    if tb < 2:
                        yb, byb = ybuf[0]
                        ybv = yb[:, 0:516].rearrange("p (s n) -> p s n", s=2)
                        emit_conv_core(l, i, tb, ybv[:, :, 0:256], ybv[:, :, 1:257], ybv[:, :, 2:258], [byb], pcb[:], [bcb])
                    else:
                        S.op("act", lambda e: e.activation(out=cbs[0][:, i, :], in_=pcb[:], func=AF.Copy),
                             reads=[bcb], writes=[cbs[1]])
                return [g_cc, g_ch, g_cb]

            for i in range(2):
                for tb in (2, 0, 1):
                    conv_groups.extend(conv_block(i, tb))

            groups = []
            for tb in range(2):
                for c in range(4):
                    kts = []
                    for kt in range(2):
                        segs = []
                        for sq_ in range(2):
                            tile = (tb * 2 + sq_) * 2 + kt
                            segs.append((kT[:, tile * 128:(tile + 1) * 128], b_kT[tile], vB[:, tile, :], b_vB[tile], sq_ * 256, 256))
                        kts.append(segs)
                    groups.append(dict(c=c, q0=tb * 512, qn=512, ktiles=kts, ao_buf=[b_ao[c][tb * 2], b_ao[c][tb * 2 + 1]],
                                       q_bufs=[b_qT[tb * 4 + i] for i in range(4)], after=None))
            for r in range(4):
                S.dma("sp", kTall[:, r * 512:(r + 1) * 512],
                      gath[l][r * PUBR:r * PUBR + 256, :].rearrange("(p a) c -> p (a c)", a=2), reads=[b_g], writes=[b_kTall[r]])
                S.dma("sp", vall[:, r * 4:(r + 1) * 4, :],
                      gath[l][r * PUBR + 256:r * PUBR + 512, :].rearrange("(t q) (two c) -> (q two) t c", t=4, two=2),
                      reads=[b_g], writes=[b_vall[r]])
                S.dma("sp", uhalo[r * 16:(r + 1) * 16, :], gath[l][r * PUBR + 512:r * PUBR + 528, :], reads=[b_g], writes=[b_uhalo[r]])
                S.dma("sp", yhalo[r * 2:(r + 1) * 2, :], gathY[l][r * 2:(r + 1) * 2, :], reads=[b_gY], writes=[b_yhalo[r]])

            def f_ysl():
                pt, pbf = ring_cv.next()

                def ysl(e):
                    ins = None
                    for i in range(2):
                        ins = e.matmul(pt[:, i * 2:i * 2 + 2], lhsT=yhalo[0:8, i * 128:(i + 1) * 128], rhs=ysel[0:8, 0:2], start=True, stop=True)
                    return ins
                S.op("pe", ysl, reads=b_yhalo + CONSTS, writes=[pbf])
                S.op("act", lambda e: e.activation(
                    out=ysm[0][:, :, 0:514:513], in_=pt[:, 0:4].rearrange("p (i w) -> p i w", i=2), func=AF.Copy),
                    reads=[pbf], writes=[ysm[1]])

            def f_convfin():
                for i in range(2):
                    emit_conv_core(l, i, 2, ysm[0][:, i, 0:512], ysm[0][:, i, 1:513], ysm[0][:, i, 2:514], [ysm[1]],
                                   cbs[0][:, i, :], [cbs[1]])
            fillers = list(conv_groups)
            for tb in range(2):
                fillers.extend(emit_pool(l, tb, ring=ring_cv, as_stages=True))
            fillers.extend([f_ysl, f_convfin])
            fillers.extend(emit_pool(l, 2, ring=ring_cv, as_stages=True))

            def spread(fs, n_units):
                it = iter(fs)
                n_two = max(0, len(fs) - n_units)
                for ui in range(n_units):
                    f = next(it, None)
                    g_ = next(it, None) if ui < n_two else None
                    if f is None:
                        return
                    yield (lambda f=f, g_=g_: (f(), g_() if g_ is not None else None))
                for f in it:
                    yield f
            attention_stream(groups, filler=spread(fillers, sum(len(g["ktiles"]) for g in groups)))
            W.release()
            W.release()

            groups = []
            for c in range(4):
                kts = []
                for kt in range(20):
                    kts.append([(kTall[:, kt * 128:(kt + 1) * 128], b_kTall[kt // 4], vall[:, kt, :], b_vall[kt // 4], 0, 512)])
                after = None
                if l == 0:
                    def after(g, c=c):
                        emit_mod_slab(0, 4 + 2 * c, ps=g["num"])
                        emit_mod_slab(0, 5 + 2 * c, ps=g["den"])
                groups.append(dict(c=c, q0=1024, qn=512, ktiles=kts, ao_buf=[b_ao[c][4]], q_bufs=[b_qT[8 + i] for i in range(4)],
                                   after=after))
            attention_stream(groups)

            transfer(A_R1, B_R1)
            for j in range(8):
                wt, bwt = W.acquire(("mrg", l, j))
                for tb in range(3):
                    gb = [ring_all.next() for _ in range(3)]

                    def mg(e, wt=wt, tb=tb, gb=gb):
                        ins = None
                        for br in range(3):
                            for k in range(8):
                                ins = e.matmul(gb[br][0][:], lhsT=wt[:, br * 1024 + k * 128:br * 1024 + (k + 1) * 128],
                                               rhs=hT[:, k, tb * 512:(tb + 1) * 512], start=(k == 0), stop=(k == 7))
                        return ins
                    S.op("pe", mg, reads=hb_tb(tb) + bwt, writes=[g[1] for g in gb])
                    sgs = []
                    for br in range(3):
                        sg, bsg = sH[br]
                        S.op("act", lambda e, br=br, sg=sg, gb=gb: e.activation(out=sg[:], in_=gb[br][0][:], func=AF.Sigmoid),
                             reads=[gb[br][1]], writes=[bsg])
                        sgs.append((sg, bsg))
                    bb = [ring_all.next() for _ in range(3)]

                    def mb(e, wt=wt, tb=tb, bb=bb):
                        ins = None
                        for k in range(2):
                            ins = e.matmul(bb[0][0][:], lhsT=wt[:, 3072 + k * 128:3072 + (k + 1) * 128],
                                           rhs=poT[:, k, tb * 512:(tb + 1) * 512], start=(k == 0), stop=(k == 1))
                        for k in range(2):
                            ins = e.matmul(bb[1][0][:], lhsT=wt[:, 3328 + k * 128:3328 + (k + 1) * 128],
                                           rhs=coT[:, k, tb * 512:(tb + 1) * 512], start=(k == 0), stop=(k == 1))
                        for k in range(4):
                            ins = e.matmul(bb[2][0][:], lhsT=wt[:, 3584 + k * 128:3584 + (k + 1) * 128],
                                           rhs=aoT[:, k, tb * 512:(tb + 1) * 512], start=(k == 0), stop=(k == 3))
                        return ins
                    aobufs = [b_ao[c][sg_] for c in range(4) for sg_ in ((0, 1) if tb == 0 else (2, 3) if tb == 1 else (4,))]
                    S.op("pe", mb, reads=bwt + [b_po[0][tb], b_po[1][tb], b_co[0][tb], b_co[1][tb]] + aobufs,
                         writes=[b[1] for b in bb])
                    t0, bt0 = sF[0]
                    t1, bt1 = sF[1]
                    S.op("dve", lambda e, t0=t0, bb=bb, sgs=sgs: e.tensor_tensor(out=t0[:, 0:512], in0=bb[0][0][:], in1=sgs[0][0][:], op=ALU.mult),
                         reads=[bb[0][1], sgs[0][1]], writes=[bt0])
                    S.op("dve", lambda e, t1=t1, bb=bb, sgs=sgs: e.tensor_tensor(out=t1[:, 0:512], in0=bb[1][0][:], in1=sgs[1][0][:], op=ALU.mult),
                         reads=[bb[1][1], sgs[1][1]], writes=[bt1])
                    S.op("dve", lambda e, t0=t0, t1=t1: e.tensor_tensor(out=t0[:, 0:512], in0=t0[:, 0:512], in1=t1[:, 0:512], op=ALU.add),
                         reads=[bt0, bt1], writes=[bt0])
                    S.op("dve", lambda e, t1=t1, bb=bb, sgs=sgs: e.tensor_tensor(out=t1[:, 0:512], in0=bb[2][0][:], in1=sgs[2][0][:], op=ALU.mult),
                         reads=[bb[2][1], sgs[2][1]], writes=[bt1])
                    S.op("dve", lambda e, t0=t0, t1=t1, j=j, tb=tb: e.tensor_tensor(
                        out=mgT[:, j, tb * 512:(tb + 1) * 512], in0=t0[:, 0:512], in1=t1[:, 0:512], op=ALU.add),
                        reads=[bt0, bt1], writes=[b_mg[j][tb]])
                W.release()

            for h in range(2):
                wt, bwt = W.acquire(("wo", l, h))
                for jj in range(4):
                    j = h * 4 + jj
                    for tb in range(3):
                        v = 0 if tb < 2 else 1
                        pt, pbf = ring5.next()

                        def mo(e, wt=wt, jj=jj, tb=tb, pt=pt):
                            ins = None
                            for k in range(8):
                                ins = e.matmul(pt[:], lhsT=wt[:, k * 512 + jj * 128:k * 512 + (jj + 1) * 128],
                                               rhs=mgT[:, k, tb * 512:(tb + 1) * 512], start=(k == 0), stop=(k == 7))
                            return ins
                        S.op("pe", mo, reads=bwt + [b_mg[k][tb] for k in range(8)], writes=[pbf])
                        S.op("dve", lambda e, pt=pt, j=j, tb=tb, v=v: e.scalar_tensor_tensor(
                            out=xT[:, j, tb * 512:(tb + 1) * 512], in0=pt[:], scalar=modT[:, l, 16 + j, v:v + 1],
                            in1=xT[:, j, tb * 512:(tb + 1) * 512], op0=ALU.mult, op1=ALU.add),
                            reads=[pbf, b_mod[l]] + xb(j, tb), writes=xb(j, tb))
                    if j > 0:
                        stats_chunk(j - 1)
                W.release()
            stats_chunk(7)

            emit_norm(l, 1, inc=True)
            transfer(B_R1 + A_R2, F_ALL)
            nxt = iter(range(12))
            for half in range(2):
                for fi in range(11):
                    wt, bwt = W.acquire(("gu", l, half * 11 + fi))
                    for tb in range(3):
                        pg, bpg = ring_all.next()
                        pu, bpu = ring_all.next()

                        def mgu(e, wt=wt, tb=tb, pg=pg, pu=pu):
                            ins = None
                            for pt_, off in ((pg, 0), (pu, 1024)):
                                for k in range(8):
                                    ins = e.matmul(pt_[:], lhsT=wt[:, off + k * 128:off + (k + 1) * 128],
                                                   rhs=hT[:, k, tb * 512:(tb + 1) * 512], start=(k == 0), stop=(k == 7))
                            return ins
                        S.op("pe", mgu, reads=hb_tb(tb) + bwt, writes=[bpg, bpu])
                        sl, bsl = sF[tb % 2]
                        S.op("act", lambda e, pg=pg, sl=sl: e.activation(out=sl[:, 0:512], in_=pg[:], func=AF.Silu), reads=[bpg], writes=[bsl])
                        S.op("dve", lambda e, pu=pu, sl=sl, fi=fi, tb=tb: e.tensor_tensor(
                            out=actT(fi)[:, tb * 512:(tb + 1) * 512], in0=pu[:], in1=sl[:, 0:512], op=ALU.mult),
                            reads=[bpu, bsl], writes=[b_act[fi][tb]])
                    W.release()
                    if l + 1 < DEPTH and fi % 2 == 1:
                        emit_mod_slab(l + 1, next(nxt))
                pend_d = []
                cnt_d = {0: 0, 1: 0, 2: 0}
                modq_d = []

                def flush_d():
                    k_, tb_ = pend_d.pop(0)
                    stats_chunk(k_, only_tb=tb_)
                    cnt_d[tb_] += 1
                    if l + 1 == DEPTH:
                        S.op("dve", lambda e, k_=k_, tb_=tb_: e.tensor_scalar(
                            out=xT[:, k_, tb_ * 512:(tb_ + 1) * 512], in0=xT[:, k_, tb_ * 512:(tb_ + 1) * 512],
                            scalar1=prm[:, P_FG + k_:P_FG + k_ + 1], scalar2=None, op0=ALU.mult),
                            reads=xb(k_, tb_) + [b_prm], writes=xb(k_, tb_))
                    if cnt_d[tb_] == 8 and l + 1 < DEPTH:
                        rsb_ = stats_finish(tb_, three=True)
                        for k2 in range(8):
                            modq_d.append(lambda tb_=tb_, rsb_=rsb_, k2=k2: emit_norm_k(l + 1, 0, 0, tb_, rsb_, k2))
                for c in range(4):
                    wt, bwt = W.acquire(("down", l, half, c))
                    if half == 1 and c == 3 and l + 1 < DEPTH:
                        jt_order = [(jj, tb) for tb in (2, 0, 1) for jj in range(2)]
                    else:
                        jt_order = [(jj, tb) for jj in range(2) for tb in range(3)]
                    for (jj, tb) in jt_order:
                        j = c * 2 + jj
                        if True:
                            v = 0 if tb < 2 else 1
                            pt, pbf = ring5.next() if half == 1 else ring_all.next()

                            def md(e, wt=wt, jj=jj, tb=tb, pt=pt):
                                ins = None
                                for k in range(11):
                                    ins = e.matmul(pt[:], lhsT=wt[:, k * 256 + jj * 128:k * 256 + (jj + 1) * 128],
                                                   rhs=actT(k)[:, tb * 512:(tb + 1) * 512], start=(k == 0), stop=(k == 10))
                                return ins
                            S.op("pe", md, reads=bwt + [b_act[k][tb] for k in range(11)], writes=[pbf])
                            S.op("dve", lambda e, pt=pt, j=j, tb=tb, v=v: e.scalar_tensor_tensor(
                                out=xT[:, j, tb * 512:(tb + 1) * 512], in0=pt[:], scalar=modT[:, l, 40 + j, v:v + 1],
                                in1=xT[:, j, tb * 512:(tb + 1) * 512], op0=ALU.mult, op1=ALU.add),
                                reads=[pbf, b_mod[l]] + xb(j, tb), writes=xb(j, tb))
                            if half == 1:
                                pend_d.append((j, tb))
                                if len(pend_d) > 3:
                                    flush_d()
                                for _ in range(2):
                                    if modq_d:
                                        modq_d.pop(0)()
                    W.release()
                    if l + 1 < DEPTH and c == 1:
                        emit_mod_slab(l + 1, next(nxt), ps=(ring5.next() if half == 1 else None))
                while pend_d:
                    flush_d()
                while modq_d:
                    modq_d.pop(0)()

        for l in range(DEPTH):
            emit_layer(l)

        b_fin = [Buf(f"fin{i}") for i in range(4)]
        transfer(F_ALL, b_fin)
        yo = [(R1f[:, i * 1024:(i + 1) * 1024], [b_fin[2 * i], b_fin[2 * i + 1]]) for i in range(2)]
        fin_rs = [stats_finish(tb, three=True) for tb in range(3)]
        ptk, bptk = ring_all.next()

        def trr(e):
            ins = None
            for tb in range(3):
                for jt in range(4):
                    i = tb * 4 + jt
                    ins = e.transpose(ptk[:, i:i + 1], fin_rs[tb][0][0:1, jt * 128:(jt + 1) * 128], ident[0:1, 0:1])
            return ins
        S.op("pe", trr, reads=[fin_rs[tb][1] for tb in range(3)] + CB, writes=[bptk])
        rtok, brtok = small[0]
        S.op("act", lambda e: e.activation(out=rtok[:, 0:12], in_=ptk[:, 0:12], func=AF.Copy), reads=[bptk], writes=[brtok])
        for tt in range(12):
            yt, byt = yo[tt % 2]
            for half in range(2):
                pt, pbf = ring_all.next()

                def trf(e, half=half, pt=pt, tt=tt):
                    ins = None
                    for j in range(4):
                        k = half * 4 + j
                        ins = e.transpose(pt[:, j * 128:(j + 1) * 128], xT[:, k, tt * 128:(tt + 1) * 128], ident[:])
                    return ins
                S.op("pe", trf, reads=[b_xT[half * 4 + j][tt] for j in range(4)] + CB, writes=[pbf])
                if half == 0:
                    S.op("act", lambda e, pt=pt, yt=yt, tt=tt: e.activation(
                        out=yt[:, 0:512], in_=pt[:], func=AF.Copy, scale=rtok[:, tt:tt + 1]),
                        reads=[pbf, brtok], writes=[byt[0]])
                else:
                    S.op("dve", lambda e, pt=pt, yt=yt, tt=tt: e.tensor_scalar(
                        out=yt[:, 512:1024], in0=pt[:], scalar1=rtok[:, tt:tt + 1], scalar2=None, op0=ALU.mult),
                        reads=[pbf, brtok], writes=[byt[1]])
            dst = yp[tt * 128:(tt + 1) * 128, :] if tt < 8 else ys[(tt - 8) * 128:(tt - 7) * 128, :]
            S.dma("sp", dst, yt, reads=byt)

        S.finish()
        S.replay()
        build_program.stats = dict(sbuf=sb_bytes[0], n_ins=dict(S.n_ins), n_wait=S.n_wait, n_sems=len(S.sems), slabs=len(W.plan))
    return nc


def _band(Lseq, w):
    M = np.zeros((Lseq, Lseq), np.float32)
    cnt = np.zeros(Lseq, np.float32)
    for t in range(Lseq):
        lo = min(max(t - w // 2, 0), Lseq)
        hi = min(max(t + w // 2, 0), Lseq)
        M[lo:hi, t] = 1.0
        cnt[t] = hi - lo
        M[t, t] -= cnt[t]
    return M, cnt


_CONST_CACHE = {}


def _host_consts():
    if _CONST_CACHE:
        return _CONST_CACHE
    wins = (2, 4, 8, 16)
    Ms, cnts, c256 = [], [], []
    for w in wins:
        M, c = _band(2048, w)
        M = M / c[None, :]
        Ms.append(M)
        cnts.append(c)
        M2, c2 = _band(256, w)
        M2 = M2 / c2[None, :]
        c256.append(c2)
        assert np.array_equal(M2[0:128, 0:128], M[0:128, 0:128])
        assert np.array_equal(M2[128:256, 128:256], M[1920:2048, 1920:2048])
        assert np.array_equal(M2[0:128, 128:256], M[0:128, 128:256])
        assert np.array_equal(M2[128:256, 0:128], M[128:256, 0:128])
    bm = np.zeros((128, 4, 5, 128), np.float32)
    for g in range(4):
        M = Ms[g]
        bm[:, g, 0] = M[128:256, 128:256]
        bm[:, g, 1] = M[0:128, 0:128]
        bm[:, g, 2] = M[1920:2048, 1920:2048]
        bm[:, g, 3] = M[0:128, 128:256]
        bm[:, g, 4] = M[128:256, 0:128]
    per_rank = []
    inv = (10000.0 ** (-np.arange(16, dtype=np.float32) / np.float32(16))).astype(np.float32)
    for r in range(4):
        s0 = r * 512
        smpd = np.zeros((128, 4, 2, 128), np.float32)
        smph = np.zeros((64, 4, 2, 128), np.float32)
        for g in range(4):
            M = Ms[g]
            smpd[:, g, 0] = M[s0:s0 + 128, s0:s0 + 128]
            smpd[:, g, 1] = M[s0 + 384:s0 + 512, s0 + 384:s0 + 512]
            for rr in range(4):
                for i in range(16):
                    tok = rr * 512 + i if i < 8 else rr * 512 + 504 + (i - 8)
                    row = rr * 16 + i
                    if rr == r - 1 and i >= 8:
                        smph[row, g, 0] = M[tok, s0:s0 + 128]
                    if rr == r + 1 and i < 8:
                        smph[row, g, 1] = M[tok, s0 + 384:s0 + 512]
        ysel = np.zeros((8, 2), np.float32)
        if r > 0:
            ysel[(r - 1) * 2 + 1, 0] = 1.0
        if r < 3:
            ysel[(r + 1) * 2 + 0, 1] = 1.0
        t = np.arange(s0, s0 + 512)
        pr = (t // 64).astype(np.float32)
        pc = (t % 64).astype(np.float32)
        ang = np.stack([pr[:, None] * inv[None, :], pc[:, None] * inv[None, :]], axis=1).astype(np.float32)
        per_rank.append(dict(
            c_smpd=smpd.reshape(128, 1024).astype(ml_dtypes.bfloat16),
            c_smph=smph.reshape(64, 1024).astype(ml_dtypes.bfloat16),
            c_ysel=ysel.astype(ml_dtypes.bfloat16),
            c_ropec=np.cos(ang).astype(np.float32).reshape(512, 32),
            c_ropes=np.sin(ang).astype(np.float32).reshape(512, 32),
        ))
    _CONST_CACHE.update(dict(
        c_ident=np.eye(128, dtype=np.float32),
        c_bm=bm.reshape(128, 2560).astype(ml_dtypes.bfloat16),
        per_rank=per_rank,
    ))
    return _CONST_CACHE


_NC = None


def kernel(x_prompt, x_sample, cache_k, cache_v, c, c_ctx, norm1_g, norm2_g, w_ada, b_ada, w_in,
           w_pool, pool_scale, conv_w, q_norm_g, k_norm_g, w_br_pool, w_br_conv, w_br_attn, w_o,
           w_gate, w_up, w_down, final_g):
    global _NC
    f = lambda a: np.ascontiguousarray(np.asarray(a), dtype=np.float32)
    x_prompt, x_sample, cache_k, cache_v, c, c_ctx = map(f, (x_prompt, x_sample, cache_k, cache_v, c, c_ctx))
    shared = dict(n1g=f(norm1_g), n2g=f(norm2_g), w_ada=f(w_ada), b_ada=f(b_ada), w_in=f(w_in), w_pool=f(w_pool),
                  pool_scale=f(pool_scale), conv_w=f(conv_w), qg=f(q_norm_g), kg=f(k_norm_g), w_brp=f(w_br_pool),
                  w_brc=f(w_br_conv), w_bra=f(w_br_attn), w_o=f(w_o), w_gate=f(w_gate), w_up=f(w_up),
                  w_down=f(w_down), final_g=f(final_g))
    hc = _host_consts()
    if _NC is None:
        _NC = build_program()
    nc = _NC
    in_maps = []
    for core in range(8):
        b, r = core // 4, core % 4
        m = dict(shared)
        m["xp"] = x_prompt[core * 4:(core + 1) * 4].reshape(1024, D)
        m["xs"] = x_sample[b, r * 512:(r + 1) * 512, :]
        m["ck"] = cache_k[b].reshape(DEPTH, 512, 128)
        m["cv"] = cache_v[b].reshape(DEPTH, 512, 128)
        m["cvec"] = np.stack([c_ctx, c[b]], axis=0)
        m["c_ident"] = hc["c_ident"]
        m["c_bm"] = hc["c_bm"]
        m.update(hc["per_rank"][r])
        in_maps.append({k: np.ascontiguousarray(v) for k, v in m.items()})
    res = run_bass_kernel_spmd(nc, in_maps, core_ids=list(range(8)))
    R = res.results
    y_prompt = np.concatenate([np.asarray(R[i]["yp"], dtype=np.float32).reshape(4, 256, D) for i in range(8)], axis=0)
    y_sample = np.stack([np.concatenate([np.asarray(R[b * 4 + r]["ys"], dtype=np.float32) for r in range(4)], axis=0)
                         for b in range(2)], axis=0)
    nk_ = np.concatenate([np.asarray(R[i]["nk"], dtype=np.float32).reshape(4, DEPTH, 256, 2, 64) for i in range(8)], axis=0)
    nv_ = np.concatenate([np.asarray(R[i]["nv"], dtype=np.float32).reshape(4, DEPTH, 256, 2, 64) for i in range(8)], axis=0)
    return (y_prompt, y_sample, nk_, nv_)
```

```python
import contextlib
import numpy as np
import ml_dtypes
import concourse.bass as bass
import concourse.mybir as mybir
from concourse.bass_utils import run_bass_kernel_spmd

F32 = mybir.dt.float32
BF16 = mybir.dt.bfloat16
AF = mybir.ActivationFunctionType
ALU = mybir.AluOpType
AX = mybir.AxisListType

D = 1024
DEPTH = 2
NT = 1536
IN_W = 4864
OFF_Q = 1024
OFF_G = 1792
FFN = 2816
EPS = 1e-6
NSLOT = 3
PUBR = 528
P_N1G, P_N2G, P_FG, P_PSC, P_CVW, P_CV, P_BADA = 0, 16, 32, 40, 44, 56, 72


class Buf:
    __slots__ = ("name", "w", "r")

    def __init__(self, name=""):
        self.name = name
        self.w = None
        self.r = []


def transfer(old, new):
    ts = []
    for b in old:
        if b.w is not None:
            ts.append(b.w)
        ts.extend(b.r)
    red = {}
    for s, v in ts:
        if red.get(s, 0) < v:
            red[s] = v
    ts = list(red.items())
    for b in new:
        b.w = None
        b.r = list(ts)


class Sched:
    ENGS = ("pe", "act", "dve", "pool", "sp")

    def __init__(self, nc, stack, n_lanes=None):
        self.nc = nc
        self.stack = stack
        self.prog = {e: [] for e in self.ENGS}
        self.sems = []
        self.esem = {}
        self.cnt = {}
        for e in ("pe", "act", "dve", "pool"):
            self.esem[e] = self._newsem("s_" + e)
            self.cnt[e] = 0
        self.seen = {e: {} for e in self.ENGS}
        n_lanes = n_lanes or {"sp": 12, "pool": 8, "act": 6}
        self.lanes = {q: [[self._newsem(f"l_{q}{i}"), 0] for i in range(n)] for q, n in n_lanes.items()}
        self.lane_rr = {q: 0 for q in n_lanes}
        self.customs = []
        self.n_wait = 0
        self.n_ins = {e: 0 for e in self.ENGS}

    def _newsem(self, name):
        s = self.stack.enter_context(self.nc.semaphore(name))
        self.sems.append(s)
        return len(self.sems) - 1

    def _deps(self, engine, reads, writes):
        need = {}

        def add(t):
            s, v = t
            if engine == "pe" and s == self.esem["pe"]:
                return
            if need.get(s, 0) < v:
                need[s] = v
        for b in reads:
            if b.w is not None:
                add(b.w)
        for b in writes:
            if b.w is not None:
                add(b.w)
            for t in b.r:
                add(t)
        out = []
        seen = self.seen[engine]
        for s, v in need.items():
            if seen.get(s, 0) >= v:
                continue
            seen[s] = v
            out.append((s, v))
        return out

    def _commit(self, ticket, reads, writes):
        for b in writes:
            b.w = ticket
            b.r = []
        for b in reads:
            b.r.append(ticket)

    def op(self, engine, fn, reads=(), writes=()):
        waits = self._deps(engine, reads, writes)
        sem = self.esem[engine]
        self.cnt[engine] += 1
        ticket = (sem, self.cnt[engine])
        sems = self.sems

        def run(e, waits=waits, fn=fn, sem=sem):
            for s, v in waits:
                e.wait_ge(sems[s], v)
            ins = fn(e)
            ins.then_inc(sems[sem], 1)
        self.prog[engine].append(run)
        self.n_wait += len(waits)
        self.n_ins[engine] += 1
        self._commit(ticket, reads, writes)
        return ticket

    def dma(self, queue, out, in_, reads=(), writes=(), **kw):
        lanes = self.lanes[queue]
        i = self.lane_rr[queue]
        self.lane_rr[queue] = (i + 1) % len(lanes)
        lane = lanes[i]
        s = lane[0]
        waits = self._deps(queue, reads, writes)
        seen = self.seen[queue]
        if seen.get(s, 0) < lane[1]:
            seen[s] = lane[1]
            waits.append((s, lane[1]))
        lane[1] += 16
        ticket = (s, lane[1])
        sems = self.sems

        def run(e, waits=waits, s=s, out=out, in_=in_, kw=kw):
            for ws, v in waits:
                e.wait_ge(sems[ws], v)
            e.dma_start(out=out, in_=in_, **kw).then_inc(sems[s], 16)
        self.prog[queue].append(run)
        self.n_wait += len(waits)
        self.n_ins[queue] += 1
        self._commit(ticket, reads, writes)
        return ticket

    def custom(self, engine, fn, reads=(), writes=(), sem_inc=1):
        s = self._newsem(f"c_{len(self.sems)}")
        waits = self._deps(engine, reads, writes)
        ticket = (s, sem_inc)
        self.customs.append(ticket)
        sems = self.sems

        def run(e, waits=waits, s=s):
            for ws, v in waits:
                e.wait_ge(sems[ws], v)
            fn(e).then_inc(sems[s], sem_inc)
        self.prog[engine].append(run)
        self._commit(ticket, reads, writes)
        return ticket

    def finish(self):
        waits = []
        for q, lanes in self.lanes.items():
            for s, c in lanes:
                if c > 0:
                    waits.append((s, c))
        for e in ("pe", "act", "dve", "pool"):
            if self.cnt[e] > 0:
                waits.append((self.esem[e], self.cnt[e]))
        waits.extend(self.customs)
        sems = self.sems

        def run(e, waits=waits):
            for s, v in waits:
                e.wait_ge(sems[s], v)
        self.prog["sp"].append(run)

    def replay(self):
        nc = self.nc
        with nc.Block() as block:
            @block.sync
            def _(e):
                for f in self.prog["sp"]:
                    f(e)

            @block.scalar
            def _(e):
                for f in self.prog["act"]:
                    f(e)

            @block.vector
            def _(e):
                for f in self.prog["dve"]:
                    f(e)

            @block.gpsimd
            def _(e):
                for f in self.prog["pool"]:
                    f(e)

            @block.tensor
            def _(e):
                for f in self.prog["pe"]:
                    f(e)


class Ring:
    def __init__(self, items):
        self.items = items
        self.i = 0

    def next(self):
        it = self.items[self.i]
        self.i = (self.i + 1) % len(self.items)
        return it


class WStream:
    def __init__(self, S, slots):
        self.S = S
        self.slots = slots
        self.keys = []
        self.plan = []
        self.loaded = 0
        self.gate = []
        self.acquired = 0
        self.released = 0

    def add(self, pieces, key=None):
        self.plan.append(pieces)
        self.keys.append(key)

    def _pump(self):
        ns = len(self.slots)
        while self.loaded < len(self.plan) and self.loaded - ns < self.released:
            n = self.loaded
            t, b = self.slots[n % ns]
            for pi, (off, kc, ncols, src, plo, phi) in enumerate(self.plan[n]):
                dst = t[plo:phi, off:off + kc * ncols].rearrange("p (k n) -> p k n", k=kc)
                self.S.dma("pool", dst, src, reads=(self.gate if n == 0 else ()), writes=[b[pi]])
            self.loaded += 1

    def acquire(self, expect=None):
        self._pump()
        n = self.acquired
        assert n < self.loaded, "weight ring deadlock: too many slabs held"
        assert expect is None or self.keys[n] == expect, (n, self.keys[n], expect)
        self.acquired += 1
        t, b = self.slots[n % len(self.slots)]
        return t, b

    def release(self):
        self.released += 1
        self._pump()


def build_program():
    nc = bass.Bass("TRN2", target_bir_lowering=False)

    def din(name, shape, dt=F32):
        return nc.dram_tensor(name, list(shape), dt, kind="ExternalInput").ap()

    def dout(name, shape, dt=F32):
        return nc.dram_tensor(name, list(shape), dt, kind="ExternalOutput").ap()

    xp = din("xp", [1024, D])
    xs = din("xs", [512, D])
    ck = din("ck", [DEPTH, 512, 128])
    cv = din("cv", [DEPTH, 512, 128])
    cvec = din("cvec", [2, D])
    n1g = din("n1g", [DEPTH, D])
    n2g = din("n2g", [DEPTH, D])
    w_ada = din("w_ada", [DEPTH, D, 6 * D])
    b_ada = din("b_ada", [DEPTH, 6 * D])
    w_in = din("w_in", [DEPTH, D, IN_W])
    w_pool = din("w_pool", [DEPTH, 4, 64, 64])
    pool_scale = din("pool_scale", [DEPTH, 256])
    conv_w = din("conv_w", [DEPTH, 3, 256])
    qg = din("qg", [DEPTH, 64])
    kg = din("kg", [DEPTH, 64])
    w_brp = din("w_brp", [DEPTH, 256, D])
    w_brc = din("w_brc", [DEPTH, 256, D])
    w_bra = din("w_bra", [DEPTH, 512, D])
    w_o = din("w_o", [DEPTH, D, D])
    w_gate = din("w_gate", [DEPTH, D, FFN])
    w_up = din("w_up", [DEPTH, D, FFN])
    w_down = din("w_down", [DEPTH, FFN, D])
    final_g = din("final_g", [D])
    c_ident = din("c_ident", [128, 128])
    c_bm = din("c_bm", [128, 4 * 5 * 128], BF16)
    c_smpd = din("c_smpd", [128, 4 * 2 * 128], BF16)
    c_smph = din("c_smph", [64, 4 * 2 * 128], BF16)
    c_ysel = din("c_ysel", [8, 2], BF16)
    c_ropec = din("c_ropec", [512, 32])
    c_ropes = din("c_ropes", [512, 32])

    yp = dout("yp", [1024, D])
    ys = dout("ys", [512, D])
    nk = dout("nk", [4, DEPTH, 256, 128])
    nv = dout("nv", [4, DEPTH, 256, 128])

    pub = [nc.dram_tensor(f"pub{l}", [PUBR, 256], BF16, kind="Internal").ap() for l in range(DEPTH)]
    gath = [nc.dram_tensor(f"gath{l}", [4 * PUBR, 256], BF16, kind="Internal").ap() for l in range(DEPTH)]
    pubY = [nc.dram_tensor(f"pubY{l}", [2, 256], BF16, kind="Internal").ap() for l in range(DEPTH)]
    gathY = [nc.dram_tensor(f"gathY{l}", [8, 256], BF16, kind="Internal").ap() for l in range(DEPTH)]

    with contextlib.ExitStack() as st:
        S = Sched(nc, st)

        sb_bytes = [0]

        def sb(name, shape, dt):
            n = 1
            for d_ in shape[1:]:
                n *= d_
            sb_bytes[0] += n * (4 if dt == F32 else 2)
            return st.enter_context(nc.sbuf_tensor(name, list(shape), dt))

        pbank = []
        ppair = []
        for i in range(4):
            t = st.enter_context(nc.psum_tensor(f"pp{i}", [128, 1024], F32))
            b0_, b1_ = Buf(f"pb{2 * i}"), Buf(f"pb{2 * i + 1}")
            pbank.append((t[:, 0:512], b0_))
            pbank.append((t[:, 512:1024], b1_))
            ppair.append((t, b0_, b1_))
        ring_all = Ring(pbank)

        ident = sb("ident", [128, 128], F32)
        ones_bf = sb("ones_bf", [128, 128], BF16)
        eps_t = sb("eps_t", [128, 1], F32)
        bm = sb("bm", [128, 4, 5, 128], BF16)
        smpd = sb("smpd", [128, 4, 2, 128], BF16)
        smph = sb("smph", [64, 4, 2, 128], BF16)
        ysel = sb("ysel", [8, 2], BF16)
        ropec = sb("ropec", [128, 4, 32], F32)
        ropes = sb("ropes", [128, 4, 32], F32)
        gst = sb("gst", [128, DEPTH, 2, 64], F32)
        g10 = sb("g10", [128, 640], F32)
        prm = sb("prm", [128, 168], F32)
        csil = sb("csil", [128, 8, 2], BF16)
        modT = sb("modT", [128, DEPTH, 48, 2], F32)
        a12 = sb("a12", [128, DEPTH, 2, 8, 2], F32)
        wpbd = sb("wpbd", [128, DEPTH, 2, 128], BF16)
        b_ident = Buf("ident")
        b_ones = Buf("ones")
        b_epsb = Buf("eps")
        CB = [b_ident, b_ones, b_epsb]
        b_prm = Buf("prm")
        b_csil = Buf("csil")
        b_mod = [Buf(f"mod{l}") for l in range(DEPTH)]
        b_a12 = [Buf(f"a12{l}") for l in range(DEPTH)]
        b_g10 = Buf("g10")
        b_wpbd = Buf("wpbd")

        xT = sb("xT", [128, 8, NT], F32)
        hT = sb("hT", [128, 8, NT], BF16)
        b_xT = [[Buf(f"xT{k}_{t}") for t in range(12)] for k in range(8)]
        b_hT = [[Buf(f"hT{k}_{t}") for t in range(12)] for k in range(8)]

        def xb(k, tb):
            return [b_xT[k][tb * 4 + i] for i in range(4)]

        def hb_tb(tb):
            return [b_hT[k][tb * 4 + i] for k in range(8) for i in range(4)]

        def hb_tile(tt):
            return [b_hT[k][tt] for k in range(8)]

        slots = [(sb(f"wslot{i}", [128, 4096], BF16), [Buf(f"wslot{i}_{j}") for j in range(8)]) for i in range(NSLOT)]
        W = WStream(S, slots)

        R1 = sb("R1", [128, 12288], BF16)
        R2 = sb("R2", [128, 12288], BF16)
        qT = R1[:, 0:6144].rearrange("p (c n) -> p c n", c=4)
        kT = R1[:, 6144:7168]
        pubKT = R1[:, 7168:7680]
        vB = R1[:, 7680:9216].rearrange("p (t c) -> p t c", t=12)
        uB = R1[:, 9216:12288].rearrange("p (t c) -> p t c", t=12)
        b_qT = [Buf(f"qT{t}") for t in range(12)]
        b_kT = [Buf(f"kT{t}") for t in range(8)]
        b_pubKT = [Buf(f"pubKT{t}") for t in range(4)]
        b_vB = [Buf(f"vB{t}") for t in range(12)]
        b_uB = [Buf(f"uB{t}") for t in range(12)]
        A_R1 = b_qT + b_kT + b_pubKT + b_vB + b_uB
        mgT = R1[:, :].rearrange("p (c n) -> p c n", c=8)
        b_mg = [[Buf(f"mg{j}_{tb}") for tb in range(3)] for j in range(8)]
        B_R1 = [b for row in b_mg for b in row]
        aoT = R2[:, 0:6144].rearrange("p (c n) -> p c n", c=4)
        poT = R2[:, 6144:9216].rearrange("p (c n) -> p c n", c=2)
        coT = R2[:, 9216:12288].rearrange("p (c n) -> p c n", c=2)
        b_ao = [[Buf(f"ao{c}_{sg}") for sg in range(5)] for c in range(4)]
        b_po = [[Buf(f"po{c}_{tb}") for tb in range(3)] for c in range(2)]
        b_co = [[Buf(f"co{c}_{tb}") for tb in range(3)] for c in range(2)]
        A_R2 = [b for row in b_ao for b in row] + [b for row in b_po for b in row] + [b for row in b_co for b in row]
        def actT(fi):
            if fi < 8:
                return R1[:, fi * NT:(fi + 1) * NT]
            return R2[:, (fi - 8) * NT:(fi - 7) * NT]
        b_act = [[Buf(f"act{f}_{tb}") for tb in range(3)] for f in range(11)]
        F_ALL = [b for row in b_act for b in row]
        R1f = R1[:, :].bitcast(F32)
        R2f = R2[:, :].bitcast(F32)
        stA = R1f[0:72, 0:128]
        stB = R1f[0:96, 128:256]
        wpst = R1f[:, 256:512].rearrange("p (l g d) -> p l g d", l=DEPTH, g=2)

        kTall = sb("kTall", [128, 2560], BF16)
        vall = sb("vall", [128, 20, 128], BF16)
        uhalo = sb("uhalo", [64, 256], BF16)
        yhalo = sb("yhalo", [8, 256], BF16)
        b_kTall = [Buf(f"kTall{r}") for r in range(5)]
        b_vall = [Buf(f"vall{r}") for r in range(5)]
        b_uhalo = [Buf(f"uh{r}") for r in range(4)]
        b_yhalo = [Buf(f"yh{r}") for r in range(4)]

        def scr(name, shape, dt):
            return (sb(name, shape, dt), Buf(name))
        sF = [scr(f"sF{i}", [128, 640], F32) for i in range(7)]
        sH = [scr(f"sH{i}", [128, 512], BF16) for i in range(4)]
        ptp = [scr(f"ptp{i}", [128, 1024], BF16) for i in range(2)]
        small = [scr(f"sm{i}", [128, 16], F32) for i in range(4)]
        ybuf = [scr(f"ybuf{i}", [128, 516], F32) for i in range(1)]
        ysm = scr("ysm", [128, 2, 514], F32)
        cbs = scr("cbs", [128, 2, 512], BF16)
        vf = [scr(f"vf{i}", [128, 128], F32) for i in range(2)]
        yhal = scr("yhal", [2, 256], BF16)
        yhtmp = (sF[0][0][0:2, 0:256], sF[0][1])
        xin = [(R2f[:, i * 1024:(i + 1) * 1024], Buf(f"xin{i}")) for i in range(6)]

        S.dma("sp", ident[:], c_ident, writes=[b_ident])
        bst = [Buf(f"st{i}") for i in range(7)]
        S.dma("act", stA[0:16, :], n1g.rearrange("l (k c) -> (l k) c", c=128), writes=[bst[0]])
        S.dma("act", stA[16:32, :], n2g.rearrange("l (k c) -> (l k) c", c=128), writes=[bst[1]])
        S.dma("act", stA[32:40, :], final_g.rearrange("(k c) -> k c", c=128), writes=[bst[2]])
        S.dma("act", stA[40:44, :], pool_scale.rearrange("l (k c) -> (l k) c", c=128), writes=[bst[3]])
        S.dma("act", stA[44:56, :], conv_w.rearrange("l j (k c) -> (l j k) c", c=128), writes=[bst[4]])
        S.dma("act", stA[56:72, :], cvec.rearrange("v (k c) -> (v k) c", c=128), writes=[bst[5]])
        S.dma("act", stB[0:96, :], b_ada.rearrange("l (k c) -> (l k) c", c=128), writes=[bst[6]])
        bc = [Buf(f"c{i}") for i in range(16)]
        CONSTS = CB + bc
        bwp = [Buf(f"wp{i}") for i in range(2)]
        S.op("pool", lambda e: e.memset(ones_bf[:], 1.0), writes=[b_ones])
        S.op("pool", lambda e: e.memset(eps_t[:], EPS), writes=[b_epsb])
        for i in range(1):
            S.op("pool", lambda e, i=i: e.memset(ybuf[i][0][:], 0.0), writes=[ybuf[i][1]])
        S.op("pool", lambda e: e.memset(wpbd[:].rearrange("p a b c -> p (a b c)"), 0.0), writes=[b_wpbd])

        def wsl(ap2d):
            return ap2d.rearrange("(k p) n -> p k n", p=128)

        def plan_ada_slab(l, jb):
            W.add([(0, 8, 512, wsl(w_ada[l, :, jb * 512:(jb + 1) * 512]), 0, 128)], key=("ada", l, jb))

        def plan_layer(l):
            W.add([(0, 8, 512, wsl(w_in[l, :, OFF_Q:OFF_Q + 512]), 0, 128)], key=("tmq", l))
            W.add([(0, 8, 256, wsl(w_in[l, :, 1536:1792]), 0, 128),
                   (2048, 8, 256, wsl(w_in[l, :, 0:256]), 0, 128)], key=("tmk", l))
            for i in range(2):
                W.add([(0, 8, 128, wsl(w_in[l, :, 256 + i * 128:256 + (i + 1) * 128]), 0, 128),
                       (1024, 8, 128, wsl(w_in[l, :, 512 + i * 128:512 + (i + 1) * 128]), 0, 128),
                       (2048, 8, 128, wsl(w_in[l, :, 768 + i * 128:768 + (i + 1) * 128]), 0, 128)], key=("conv", l, i))
            if l == 0:
                for jb in range(4, 12):
                    plan_ada_slab(0, jb)
            for j in range(8):
                pcs = []
                for br in range(3):
                    c0 = OFF_G + br * 1024 + j * 128
                    pcs.append((br * 1024, 8, 128, wsl(w_in[l, :, c0:c0 + 128]), 0, 128))
                pcs.append((3072, 2, 128, wsl(w_brp[l, :, j * 128:(j + 1) * 128]), 0, 128))
                pcs.append((3328, 2, 128, wsl(w_brc[l, :, j * 128:(j + 1) * 128]), 0, 128))
                for h2 in range(2):
                    src = w_bra[l, h2 * 256:(h2 + 1) * 256, j * 128:(j + 1) * 128].rearrange("(c p) n -> p c n", p=64)
                    pcs.append((3584, 4, 128, src, h2 * 64, (h2 + 1) * 64))
                W.add(pcs, key=("mrg", l, j))
            for h in range(2):
                W.add([(0, 8, 512, wsl(w_o[l, :, h * 512:(h + 1) * 512]), 0, 128)], key=("wo", l, h))
            nxt = iter(range(12))
            for half in range(2):
                for fi in range(11):
                    f = half * 11 + fi
                    W.add([(0, 8, 128, wsl(w_gate[l, :, f * 128:(f + 1) * 128]), 0, 128),
                           (1024, 8, 128, wsl(w_up[l, :, f * 128:(f + 1) * 128]), 0, 128)], key=("gu", l, f))
                    if l + 1 < DEPTH and fi % 2 == 1:
                        plan_ada_slab(l + 1, next(nxt))
                for c in range(4):
                    W.add([(0, 11, 256, wsl(w_down[l, half * 1408:(half + 1) * 1408, c * 256:(c + 1) * 256]), 0, 128)],
                          key=("down", l, half, c))
                    if l + 1 < DEPTH and c == 1:
                        plan_ada_slab(l + 1, next(nxt))

        for jb in range(4):
            plan_ada_slab(0, jb)
        for l in range(DEPTH):
            plan_layer(l)

        def emit_mod_slab(l, jb, ps=None):
            wt, wb = W.acquire(("ada", l, jb))
            pt, pbf = ps if ps is not None else ring_all.next()

            def mm(e, wt=wt, pt=pt):
                ins = None
                for jj in range(4):
                    for k in range(8):
                        ins = e.matmul(pt[:, jj * 2:jj * 2 + 2], lhsT=wt[:, k * 512 + jj * 128:k * 512 + (jj + 1) * 128],
                                       rhs=csil[:, k, :], start=(k == 0), stop=(k == 7))
                return ins
            S.op("pe", mm, reads=wb + [b_csil], writes=[pbf])
            W.release()
            S.op("dve", lambda e, pt=pt, jb=jb: e.tensor_tensor(
                out=modT[:, l, jb * 4:(jb + 1) * 4, :],
                in0=pt[:, 0:8].rearrange("p (a b) -> p a b", b=2),
                in1=prm[:, P_BADA + l * 48 + jb * 4:P_BADA + l * 48 + jb * 4 + 4].unsqueeze(2).broadcast_to([128, 4, 2]),
                op=ALU.add), reads=[pbf, b_prm], writes=[b_mod[l]])
            if jb == 3 or jb == 9:
                w, sc0, pg = (0, 8, P_N1G) if jb == 3 else (1, 32, P_N2G)
                S.op("dve", lambda e: e.scalar_tensor_tensor(
                    out=a12[:, l, w, :, :], in0=modT[:, l, sc0:sc0 + 8, :], scalar=1.0,
                    in1=prm[:, pg + l * 8:pg + l * 8 + 8].unsqueeze(2).broadcast_to([128, 8, 2]),
                    op0=ALU.add, op1=ALU.mult), reads=[b_mod[l], b_prm], writes=[b_a12[l]])

        def emit_stats(tb, src_bufs_fn):
            pt, pbf = ring_all.next()
            for k in range(8):
                x2, bx2 = sH[k % 2]
                S.op("act", lambda e, k=k, x2=x2: e.activation(out=x2[:], in_=xT[:, k, tb * 512:(tb + 1) * 512], func=AF.Square),
                     reads=xb(k, tb), writes=[bx2])
                S.op("pe", lambda e, k=k, x2=x2, pt=pt: e.matmul(pt[:], lhsT=ones_bf[:], rhs=x2[:], start=(k == 0), stop=(k == 7)),
                     reads=[bx2] + CB, writes=[pbf])
            rs, brs = sF[tb % 2]
            S.op("act", lambda e, pt=pt, rs=rs: e.activation(out=rs[:, 0:512], in_=pt[:], func=AF.Sqrt, scale=1.0 / D, bias=eps_t[:]),
                 reads=[pbf] + CB, writes=[brs])
            S.op("dve", lambda e, rs=rs: e.reciprocal(out=rs[:, 0:512], in_=rs[:, 0:512]), reads=[brs], writes=[brs])
            return rs, brs

        ring5 = Ring(pbank[0:5])
        stat_banks = pbank[5:8]
        ring_x2 = Ring(sH[0:4])

        def stats_chunk(k, only_tb=None):
            for tb in (range(3) if only_tb is None else [only_tb]):
                x2, bx2 = ring_x2.next()
                pt, pbf = stat_banks[tb]
                S.op("act", lambda e, x2=x2, tb=tb: e.activation(out=x2[:], in_=xT[:, k, tb * 512:(tb + 1) * 512], func=AF.Square),
                     reads=xb(k, tb), writes=[bx2])
                S.op("pe", lambda e, x2=x2, pt=pt: e.matmul(pt[:], lhsT=ones_bf[:], rhs=x2[:], start=(k == 0), stop=(k == 7)),
                     reads=[bx2] + CB, writes=[pbf])

        def stats_finish(tb, three=False):
            pt, pbf = stat_banks[tb]
            rs, brs = sF[tb] if three else sF[tb % 2]
            S.op("act", lambda e: e.activation(out=rs[:, 0:512], in_=pt[:], func=AF.Ln, scale=1.0 / D, bias=eps_t[:]),
                 reads=[pbf] + CB, writes=[brs])
            S.op("act", lambda e: e.activation(out=rs[:, 0:512], in_=rs[:, 0:512], func=AF.Exp, scale=-0.5), reads=[brs], writes=[brs])
            return rs, brs

        def emit_norm(l, w, inc=False):
            sh0 = 0 if w == 0 else 24
            order = (2, 0, 1) if w == 0 else (0, 1, 2)
            rss = {tb: stats_finish(tb, three=True) for tb in order}
            for tb in order:
                emit_norm_tb(l, w, sh0, tb, rss[tb])

        def emit_norm_tb(l, w, sh0, tb, rsb):
            for k in range(8):
                emit_norm_k(l, w, sh0, tb, rsb, k)

        def emit_norm_k(l, w, sh0, tb, rsb, k):
            v = 0 if tb < 2 else 1
            rs, brs = rsb
            if True:
                tm, btm = sF[3 + k % 4]
                S.op("dve", lambda e, k=k, tm=tm: e.tensor_tensor(
                    out=tm[:, 0:512], in0=xT[:, k, tb * 512:(tb + 1) * 512], in1=rs[:, 0:512], op=ALU.mult),
                    reads=xb(k, tb) + [brs], writes=[btm])
                S.op("act", lambda e, k=k, tm=tm: e.activation(
                    out=hT[:, k, tb * 512:(tb + 1) * 512], in_=tm[:, 0:512], func=AF.Identity,
                    scale=a12[:, l, w, k, v:v + 1], bias=modT[:, l, sh0 + k, v:v + 1]),
                    reads=[btm, b_a12[l], b_mod[l]], writes=[b_hT[k][tb * 4 + i] for i in range(4)])

        def emit_params():
            pt, pbf = ring5.next()
            S.op("pe", lambda e, pt=pt: e.transpose(pt[:, 0:72], stA[0:72, :], ident[0:72, 0:72]), reads=bst + CB, writes=[pbf])
            S.op("act", lambda e, pt=pt: e.activation(out=prm[:, 0:72], in_=pt[:, 0:72], func=AF.Copy), reads=[pbf], writes=[b_prm])
            pt, pbf = ring5.next()
            S.op("pe", lambda e, pt=pt: e.transpose(pt[:, 0:96], stB[0:96, :], ident[0:96, 0:96]), reads=bst + CB, writes=[pbf])
            S.op("act", lambda e, pt=pt: e.activation(out=prm[:, 72:168], in_=pt[:, 0:96], func=AF.Copy), reads=[pbf], writes=[b_prm])
            S.op("act", lambda e: e.activation(
                out=csil[:], in_=prm[:, P_CV:P_CV + 16].rearrange("p (v k) -> p k v", v=2), func=AF.Silu),
                reads=[b_prm], writes=[b_csil])


        emit_params()

        for pos, tt in enumerate([8, 9, 10, 11, 0, 1, 2, 3, 4, 5, 6, 7]):
            xi, bxi = xin[pos % 6]
            src = xp[tt * 128:(tt + 1) * 128, :] if tt < 8 else xs[(tt - 8) * 128:(tt - 7) * 128, :]
            S.dma("sp", xi, src, writes=[bxi])
            for half in range(2):
                pt, pbf = ring5.next()

                def tr(e, half=half, pt=pt, xi=xi):
                    ins = None
                    for j in range(4):
                        k = half * 4 + j
                        ins = e.transpose(pt[:, j * 128:(j + 1) * 128], xi[:, k * 128:(k + 1) * 128], ident[:])
                    return ins
                S.op("pe", tr, reads=[bxi] + CB, writes=[pbf])
                eng = "dve"
                if eng == "act":
                    S.op("act", lambda e, half=half, pt=pt, tt=tt: e.activation(
                        out=xT[:, half * 4:(half + 1) * 4, tt * 128:(tt + 1) * 128],
                        in_=pt[:].rearrange("p (a b) -> p a b", a=4), func=AF.Copy),
                        reads=[pbf], writes=[b_xT[half * 4 + j][tt] for j in range(4)])
                else:
                    S.op("dve", lambda e, half=half, pt=pt, tt=tt: e.tensor_copy(
                        out=xT[:, half * 4:(half + 1) * 4, tt * 128:(tt + 1) * 128],
                        in_=pt[:].rearrange("p (a b) -> p a b", a=4)),
                        reads=[pbf], writes=[b_xT[half * 4 + j][tt] for j in range(4)])

            if pos % 4 == 3:
                for k in range(8):
                    stats_chunk(k, only_tb=tt // 4)
                emit_mod_slab(0, pos // 4, ps=ring5.next())
        emit_mod_slab(0, 3, ps=ring5.next())
        S.dma("sp", bm[:].rearrange("p a b c -> p (a b c)"), c_bm, writes=[bc[0]])
        S.dma("sp", smpd[:].rearrange("p a b c -> p (a b c)"), c_smpd, writes=[bc[3]])
        S.dma("sp", smph[:].rearrange("p a b c -> p (a b c)"), c_smph, writes=[bc[4]])
        S.dma("sp", ysel[:], c_ysel, writes=[bc[5]])
        S.dma("sp", ropec[:], c_ropec.rearrange("(t p) f -> p t f", p=128), writes=[bc[6]])
        S.dma("sp", ropes[:], c_ropes.rearrange("(t p) f -> p t f", p=128), writes=[bc[7]])
        for l in range(DEPTH):
            S.dma("sp", gst[:, l, 0, :], qg[l:l + 1, :].partition_broadcast(128).rearrange("p o f -> p (o f)"), writes=[bc[8 + l * 2]])
            S.dma("sp", gst[:, l, 1, :], kg[l:l + 1, :].partition_broadcast(128).rearrange("p o f -> p (o f)"), writes=[bc[9 + l * 2]])
        for h2 in range(2):
            S.dma("sp", wpst[h2 * 64:(h2 + 1) * 64, :, :, :],
                  w_pool[:, h2::2, :, :].rearrange("l g c d -> c l g d"), writes=[bwp[h2]])
        for h2 in range(2):
            S.op("dve", lambda e, h2=h2: e.tensor_copy(
                out=wpbd[h2 * 64:(h2 + 1) * 64, :, :, h2 * 64:(h2 + 1) * 64],
                in_=wpst[h2 * 64:(h2 + 1) * 64, :, :, :]), reads=bwp, writes=[b_wpbd])

        TILE_ORDER = [8, 9, 10, 11, 0, 1, 2, 3, 4, 5, 6, 7]
        ring_q = Ring(pbank[0:2])
        ring_kvu = Ring(pbank[2:4])
        ring_tq = Ring(pbank[4:6])
        ring_tk = Ring(pbank[6:8])
        ring_s = Ring(ppair[0:2])
        ring_num = Ring(pbank[4:6])
        ring_den = Ring(pbank[6:8])
        ring_pt = Ring(ptp)

        def attention_stream(groups, filler=None):
            if filler is None:
                r_s, r_num, r_den = ring_s, ring_num, ring_den
            else:
                r_s, r_num, r_den = Ring(ppair[0:1]), Ring(pbank[2:4]), Ring(pbank[4:6])
            units = []
            for g in groups:
                n = len(g["ktiles"])
                for i, kt in enumerate(g["ktiles"]):
                    units.append((g, i, n, kt))

            def emit_s(u):
                g, i, n, segs = u
                c, q0, qn = g["c"], g["q0"], g["qn"]
                spair, bsa, bsb = r_s.next()
                sa, sbk = spair[:, 0:512], spair[:, 512:1024]

                def smm(e):
                    ins = None
                    for (kap, kbuf, vap, vbuf, c0, cn) in segs:
                        e.matmul(sa[:, c0:c0 + cn], lhsT=kap[0:64, :], rhs=qT[0:64, c, q0 + c0:q0 + c0 + cn], start=True, stop=True)
                        ins = e.matmul(sbk[:, c0:c0 + cn], lhsT=kap[64:128, :], rhs=qT[64:128, c, q0 + c0:q0 + c0 + cn],
                                       start=True, stop=True)
                    return ins
                S.op("pe", smm, reads=[sg[1] for sg in segs] + g["q_bufs"], writes=[bsa, bsb])
                return spair, bsa, bsb

            def emit_rest(u, sres, mid=None):
                g, i, n, segs = u
                c, q0, qn = g["c"], g["q0"], g["qn"]
                spair, bsa, bsb = sres
                if i == 0:
                    g["num"] = r_num.next()
                    g["den"] = r_den.next()
                num, bnum = g["num"]
                den, bden = g["den"]
                assert qn == 512
                ptt, bpa = ring_pt.next()
                bpb = bpa
                pa, pb_ = ptt[:, 0:512], ptt[:, 512:1024]
                S.op("act", lambda e: e.activation(out=ptt[:, 0:1024], in_=spair[:, 0:1024], func=AF.Exp, scale=0.125),
                     reads=[bsa, bsb], writes=[bpa])
                if mid is not None:
                    mid()

                def pv(e):
                    ins = None
                    for si, (kap, kbuf, vap, vbuf, c0, cn) in enumerate(segs):
                        st_, sp_ = (i == 0 and si == 0), (i == n - 1)
                        kw = dict(skip_group_check=True) if len(segs) > 1 else {}
                        e.matmul(num[0:64, c0:c0 + cn], lhsT=vap[:, 0:64], rhs=pa[:, c0:c0 + cn], start=st_, stop=sp_, **kw)
                        e.matmul(num[64:128, c0:c0 + cn], lhsT=vap[:, 64:128], rhs=pb_[:, c0:c0 + cn], start=st_, stop=sp_, **kw)
                        e.matmul(den[0:64, c0:c0 + cn], lhsT=ones_bf[:, 0:64], rhs=pa[:, c0:c0 + cn], start=st_, stop=sp_, **kw)
                        ins = e.matmul(den[64:128, c0:c0 + cn], lhsT=ones_bf[:, 0:64], rhs=pb_[:, c0:c0 + cn], start=st_, stop=sp_, **kw)
                    return ins
                S.op("pe", pv, reads=[bpa, bpb] + [sg[3] for sg in segs] + CB, writes=[bnum, bden])
                if i == n - 1:
                    def fin():
                        rd, brd = sF[1 + c % 2]
                        S.op("act", lambda e: e.activation(out=rd[:, 0:qn], in_=den[:, 0:qn], func=AF.Ln), reads=[bden], writes=[brd])
                        S.op("act", lambda e: e.activation(out=rd[:, 0:qn], in_=rd[:, 0:qn], func=AF.Exp, scale=-1.0), reads=[brd], writes=[brd])
                        S.op("dve", lambda e: e.tensor_tensor(
                            out=aoT[:, c, q0:q0 + qn], in0=num[:, 0:qn], in1=rd[:, 0:qn], op=ALU.mult),
                            reads=[bnum, brd], writes=g["ao_buf"])
                        if g.get("after") is not None:
                            g["after"](g)
                    return fin
                return None

            if filler is not None:
                deferred = None
                for u in units:
                    sres = emit_s(u)
                    fin = emit_rest(u, sres, mid=next(filler, None))
                    if deferred is not None:
                        deferred()
                    deferred = fin
                if deferred is not None:
                    deferred()
                for f in filler:
                    f()
                return
            pending = emit_s(units[0])
            deferred = None
            for idx, u in enumerate(units):
                cur = pending
                if idx + 1 < len(units):
                    pending = emit_s(units[idx + 1])
                fin = emit_rest(u, cur)
                if deferred is not None:
                    deferred()
                deferred = fin
            if deferred is not None:
                deferred()

        def emit_pool(l, tb, ring=None, as_stages=False):
            ring = ring or ring_all
            st1, st2 = [], []
            for pair in range(2):
                a_, b_ = pool_pair(l, tb, pair, ring)
                st1.append(a_)
                st2.append(b_)
            if as_stages:
                return st1 + st2
            for f in (st1[0], st2[0], st1[1], st2[1]):
                f()

        def pool_pair(l, tb, pair, ring):
            pl, bpl = sH[pair]
            if True:

                def mm(e, pair, pt):
                    ins = None
                    for g2 in range(2):
                        g = pair * 2 + g2
                        for jt in range(4):
                            tile = tb * 4 + jt
                            contrib = []
                            if tb < 2:
                                if jt % 2 == 0:
                                    contrib.append((uB[:, tile, g * 64:(g + 1) * 64], bm[:, g, 1, :]))
                                    contrib.append((uB[:, tile + 1, g * 64:(g + 1) * 64], bm[:, g, 4, :]))
                                else:
                                    contrib.append((uB[:, tile, g * 64:(g + 1) * 64], bm[:, g, 2, :]))
                                    contrib.append((uB[:, tile - 1, g * 64:(g + 1) * 64], bm[:, g, 3, :]))
                            else:
                                if jt == 0:
                                    contrib.append((uB[:, tile, g * 64:(g + 1) * 64], smpd[:, g, 0, :]))
                                    contrib.append((uhalo[0:64, g * 64:(g + 1) * 64], smph[0:64, g, 0, :]))
                                elif jt == 3:
                                    contrib.append((uB[:, tile, g * 64:(g + 1) * 64], smpd[:, g, 1, :]))
                                    contrib.append((uhalo[0:64, g * 64:(g + 1) * 64], smph[0:64, g, 1, :]))
                                else:
                                    contrib.append((uB[:, tile, g * 64:(g + 1) * 64], bm[:, g, 0, :]))
                                if jt > 0:
                                    contrib.append((uB[:, tile - 1, g * 64:(g + 1) * 64], bm[:, g, 3, :]))
                                if jt < 3:
                                    contrib.append((uB[:, tile + 1, g * 64:(g + 1) * 64], bm[:, g, 4, :]))
                            n = len(contrib)
                            for ci, (la, ra) in enumerate(contrib):
                                ins = e.matmul(pt[g2 * 64:(g2 + 1) * 64, jt * 128:(jt + 1) * 128], lhsT=la, rhs=ra,
                                               start=(ci == 0), stop=(ci == n - 1))
                    return ins
                rds = [b_uB[tb * 4 + i] for i in range(4)] + CONSTS
                if tb == 2:
                    rds = rds + b_uhalo

                def stage1():
                    pt, pbf = ring.next()
                    S.op("pe", lambda e: mm(e, pair, pt), reads=rds, writes=[pbf])
                    S.op("dve", lambda e: e.tensor_copy(out=pl[:], in_=pt[:]), reads=[pbf], writes=[bpl])

                def stage2():
                    pt2, pbf2 = ring.next()
                    S.op("pe", lambda e: e.matmul(pt2[:], lhsT=wpbd[:, l, pair, :], rhs=pl[:], start=True, stop=True),
                         reads=[bpl, b_wpbd], writes=[pbf2])
                    S.op("act", lambda e: e.activation(
                        out=poT[:, pair, tb * 512:(tb + 1) * 512], in_=pt2[:], func=AF.Copy,
                        scale=prm[:, P_PSC + l * 2 + pair:P_PSC + l * 2 + pair + 1]),
                        reads=[pbf2, b_prm], writes=[b_po[pair][tb]])
                return stage1, stage2

        def emit_conv_core(l, i, tb, yap_l, yap_c, yap_r, ybufs, cb_ap, cb_bufs):
            acc, bacc = sF[5]
            shape3 = len(yap_c.shape) == 3
            accv = acc[:, 0:512].rearrange("p (s n) -> p s n", s=2) if shape3 else acc[:, 0:512]
            wcol = lambda j: prm[:, P_CVW + (l * 3 + j) * 2 + i:P_CVW + (l * 3 + j) * 2 + i + 1]
            S.op("dve", lambda e: e.tensor_scalar(out=accv, in0=yap_c, scalar1=wcol(1), scalar2=None, op0=ALU.mult),
                 reads=ybufs + [b_prm], writes=[bacc])
            S.op("dve", lambda e: e.scalar_tensor_tensor(out=accv, in0=yap_l, scalar=wcol(0), in1=accv, op0=ALU.mult, op1=ALU.add),
                 reads=ybufs + [b_prm, bacc], writes=[bacc])
            S.op("dve", lambda e: e.scalar_tensor_tensor(out=accv, in0=yap_r, scalar=wcol(2), in1=accv, op0=ALU.mult, op1=ALU.add),
                 reads=ybufs + [b_prm, bacc], writes=[bacc])
            S.op("dve", lambda e: e.tensor_tensor(out=coT[:, i, tb * 512:(tb + 1) * 512], in0=cb_ap, in1=acc[:, 0:512], op=ALU.mult),
                 reads=cb_bufs + [bacc], writes=[b_co[i][tb]])

        def emit_layer(l):
            transfer((F_ALL + B_R1) if l > 0 else [b for _, b in xin] + bst + bwp, A_R1 + A_R2)
            if l == 0:
                emit_norm(l, 0, inc=True)
            S.op("dve", lambda e: e.tensor_copy(
                out=g10[:, 0:512].rearrange("p (h d) -> p h d", h=8),
                in_=gst[:, l, 0, :].unsqueeze(1).broadcast_to([128, 8, 64])), reads=CONSTS, writes=[b_g10])
            S.op("dve", lambda e: e.tensor_copy(
                out=g10[:, 512:640].rearrange("p (h d) -> p h d", h=2),
                in_=gst[:, l, 1, :].unsqueeze(1).broadcast_to([128, 2, 64])), reads=CONSTS, writes=[b_g10])
            S.dma("pool", vall[:, 16:20, :], cv[l].rearrange("(t p) c -> p t c", p=128), writes=[b_vall[4]])
            ckin, b_ckin = ysm
            ckv = ckin[:, 0, 0:512].rearrange("p (t c) -> p t c", t=4)
            S.dma("sp", ckv, ck[l].rearrange("(t p) c -> p t c", p=128), writes=[b_ckin])
            wq, bwq = W.acquire(("tmq", l))
            wk, bwk = W.acquire(("tmk", l))
            pend = []
            tm_state = {}

            def tm_post(tt, src, bsrc):
                tq, btq = ring_tq.next()
                tk, btk = ring_tk.next()

                def trq(e):
                    ins = None
                    for c in range(4):
                        ins = e.transpose(tq[:, c * 128:(c + 1) * 128], src[:, c * 128:(c + 1) * 128], ident[:])
                    return ins
                S.op("pe", trq, reads=[bsrc] + CB, writes=[btq])
                S.op("pe", lambda e: e.transpose(tk[:, 0:128], src[:, 512:640], ident[:]), reads=[bsrc] + CB, writes=[btk])
                S.op("act", lambda e: e.activation(out=qT[:, :, tt * 128:(tt + 1) * 128],
                                                   in_=tq[:].rearrange("p (c n) -> p c n", c=4), func=AF.Copy),
                     reads=[btq], writes=[b_qT[tt]])
                if tt < 8:
                    S.op("dve", lambda e: e.tensor_copy(out=kT[:, tt * 128:(tt + 1) * 128], in_=tk[:, 0:128]),
                         reads=[btk], writes=[b_kT[tt]])
                else:
                    S.op("dve", lambda e: e.tensor_copy(out=pubKT[:, (tt - 8) * 128:(tt - 7) * 128], in_=tk[:, 0:128]),
                         reads=[btk], writes=[b_pubKT[tt - 8]])

            for idx, tt in enumerate(TILE_ORDER):
                qa, bqa = ring_q.next()
                kb, bkb = ring_kvu.next()

                def mmq(e, tt=tt, qa=qa):
                    ins = None
                    for k in range(8):
                        ins = e.matmul(qa[:], lhsT=hT[:, k, tt * 128:(tt + 1) * 128], rhs=wq[:, k * 512:(k + 1) * 512],
                                       start=(k == 0), stop=(k == 7))
                    return ins
                S.op("pe", mmq, reads=hb_tile(tt) + bwq, writes=[bqa])

                def mmk(e, tt=tt, kb=kb):
                    ins = None
                    for k in range(8):
                        ins = e.matmul(kb[:, 0:256], lhsT=hT[:, k, tt * 128:(tt + 1) * 128], rhs=wk[:, k * 256:(k + 1) * 256],
                                       start=(k == 0), stop=(k == 7))
                    for k in range(8):
                        ins = e.matmul(kb[:, 256:512], lhsT=hT[:, k, tt * 128:(tt + 1) * 128],
                                       rhs=wk[:, 2048 + k * 256:2048 + (k + 1) * 256], start=(k == 0), stop=(k == 7))
                    return ins
                S.op("pe", mmk, reads=hb_tile(tt) + bwk, writes=[bkb])
                if len(pend) >= 2:
                    tm_post(*pend.pop(0))
                if idx == 5:
                    bp = [Buf(f"pub{l}_{i}") for i in range(4)]
                    S.dma("sp", pub[l][0:256, :].rearrange("(p a) c -> p (a c)", a=2), pubKT, reads=b_pubKT, writes=[bp[0]])
                    S.dma("sp", pub[l][256:512, :].rearrange("(t q) (two c) -> (q two) t c", t=4, two=2), vB[:, 8:12, :],
                          reads=b_vB[8:12], writes=[bp[1]])
                    S.dma("sp", pub[l][512:520, :], uB[0:8, 8, :], reads=[b_uB[8]], writes=[bp[2]])
                    S.dma("sp", pub[l][520:528, :], uB[120:128, 11, :], reads=[b_uB[11]], writes=[bp[3]])
                    tm_state["b_g"] = Buf(f"gath{l}")
                    S.custom("pool", lambda e, l=l: e.collective_compute(
                        "AllGather", ALU.bypass, replica_groups=[[0, 1, 2, 3], [4, 5, 6, 7]], ins=[pub[l]], outs=[gath[l]]),
                        reads=bp, writes=[tm_state["b_g"]])
                sq, bsq = sF[0]
                S.op("act", lambda e, qa=qa, sq=sq: e.activation(out=sq[:, 0:512], in_=qa[:], func=AF.Square), reads=[bqa], writes=[bsq])
                S.op("act", lambda e, kb=kb, sq=sq: e.activation(out=sq[:, 512:640], in_=kb[:, 0:128], func=AF.Square), reads=[bkb], writes=[bsq])
                ss, bss = small[idx % 2]
                S.op("dve", lambda e, sq=sq, ss=ss: e.tensor_reduce(
                    out=ss[:, 0:10], in_=sq[:, 0:640].rearrange("p (h d) -> p h d", h=10), axis=AX.X, op=ALU.add),
                    reads=[bsq], writes=[bss])
                S.op("act", lambda e, ss=ss: e.activation(out=ss[:, 0:10], in_=ss[:, 0:10], func=AF.Sqrt, scale=1.0 / 64, bias=eps_t[:]),
                     reads=[bss] + CB, writes=[bss])
                S.op("dve", lambda e, ss=ss: e.reciprocal(out=ss[:, 0:10], in_=ss[:, 0:10]), reads=[bss], writes=[bss])
                qk, bqk = sF[1 + idx % 2]
                S.op("dve", lambda e, qa=qa, qk=qk, ss=ss: e.tensor_tensor(
                    out=qk[:, 0:512].rearrange("p (c two d) -> p two c d", two=2, d=64),
                    in0=qa[:].rearrange("p (two c d) -> p two c d", two=2, d=64),
                    in1=ss[:, 0:8].rearrange("p (two c) -> p two c", two=2).unsqueeze(3).broadcast_to([128, 2, 4, 64]),
                    op=ALU.mult), reads=[bqa, bss], writes=[bqk])
                S.op("dve", lambda e, kb=kb, qk=qk, ss=ss: e.tensor_tensor(
                    out=qk[:, 512:640].rearrange("p (h d) -> p h d", h=2),
                    in0=kb[:, 0:128].rearrange("p (h d) -> p h d", h=2),
                    in1=ss[:, 8:10].unsqueeze(2).broadcast_to([128, 2, 64]),
                    op=ALU.mult), reads=[bkb, bss], writes=[bqk])
                S.op("dve", lambda e, qk=qk: e.tensor_tensor(out=qk[:, 0:640], in0=qk[:, 0:640], in1=g10[:, :], op=ALU.mult),
                     reads=[bqk, b_g10], writes=[bqk])
                S.op("act", lambda e, kb=kb, tt=tt: e.activation(out=vB[:, tt, :], in_=kb[:, 128:256], func=AF.Copy),
                     reads=[bkb], writes=[b_vB[tt]])
                S.op("act", lambda e, kb=kb, tt=tt: e.activation(out=uB[:, tt, :], in_=kb[:, 256:512], func=AF.Copy),
                     reads=[bkb], writes=[b_uB[tt]])
                if tt < 8:
                    vft, bvf = vf[idx % 2]
                    S.op("act", lambda e, kb=kb, vft=vft: e.activation(out=vft[:], in_=kb[:, 128:256], func=AF.Copy),
                         reads=[bkb], writes=[bvf])
                    s_, hf = tt // 2, tt % 2
                    S.dma("sp", nv[s_, l, hf * 128:(hf + 1) * 128, :], vft[:], reads=[bvf])
                    S.dma("sp", nk[s_, l, hf * 128:(hf + 1) * 128, :], qk[:, 512:640], reads=[bqk])
                    pend.append((tt, qk, bqk))
                else:
                    ti = tt - 8
                    qr, bqr = sF[3 + idx % 2]
                    t1, bt1 = sF[5]
                    t2, bt2 = sF[6]
                    xv = qk[:, 0:640].rearrange("p (h a j f) -> p h a j f", h=10, a=2, j=2)
                    ov = qr[:, 0:640].rearrange("p (h a j f) -> p h a j f", h=10, a=2, j=2)
                    x1, x2 = xv[:, :, :, 0, :], xv[:, :, :, 1, :]
                    o1, o2 = ov[:, :, :, 0, :], ov[:, :, :, 1, :]
                    cb_ = ropec[:, ti, :].rearrange("p (a f) -> p a f", a=2).unsqueeze(1).broadcast_to([128, 10, 2, 16])
                    sb_ = ropes[:, ti, :].rearrange("p (a f) -> p a f", a=2).unsqueeze(1).broadcast_to([128, 10, 2, 16])
                    t1v = t1[:, 0:320].rearrange("p (h a f) -> p h a f", h=10, a=2)
                    t2v = t2[:, 0:320].rearrange("p (h a f) -> p h a f", h=10, a=2)
                    S.op("dve", lambda e, x1=x1, t1v=t1v, cb_=cb_: e.tensor_tensor(out=t1v, in0=x1, in1=cb_, op=ALU.mult),
                         reads=[bqk] + CONSTS, writes=[bt1])
                    S.op("dve", lambda e, x2=x2, t2v=t2v, sb_=sb_: e.tensor_tensor(out=t2v, in0=x2, in1=sb_, op=ALU.mult),
                         reads=[bqk] + CONSTS, writes=[bt2])
                    S.op("dve", lambda e, o1=o1, t1v=t1v, t2v=t2v: e.tensor_tensor(out=o1, in0=t1v, in1=t2v, op=ALU.subtract),
                         reads=[bt1, bt2], writes=[bqr])
                    S.op("dve", lambda e, x1=x1, t1v=t1v, sb_=sb_: e.tensor_tensor(out=t1v, in0=x1, in1=sb_, op=ALU.mult),
                         reads=[bqk] + CONSTS, writes=[bt1])
                    S.op("dve", lambda e, x2=x2, t2v=t2v, cb_=cb_: e.tensor_tensor(out=t2v, in0=x2, in1=cb_, op=ALU.mult),
                         reads=[bqk] + CONSTS, writes=[bt2])
                    S.op("dve", lambda e, o2=o2, t1v=t1v, t2v=t2v: e.tensor_tensor(out=o2, in0=t1v, in1=t2v, op=ALU.add),
                         reads=[bt1, bt2], writes=[bqr])
                    pend.append((tt, qr, bqr))
            while pend:
                tm_post(*pend.pop(0))
            W.release()
            W.release()

            pt, pbf = ring5.next()

            def trc(e, pt=pt):
                ins = None
                for t in range(4):
                    ins = e.transpose(pt[:, t * 128:(t + 1) * 128], ckv[:, t, :], ident[:])
                return ins
            S.op("pe", trc, reads=[b_ckin] + CB, writes=[pbf])
            S.op("act", lambda e, pt=pt: e.activation(out=kTall[:, 2048:2560], in_=pt[:], func=AF.Copy), reads=[pbf], writes=[b_kTall[4]])

            wc = [W.acquire(("conv", l, i)) for i in range(2)]
            b_g = tm_state["b_g"]
            ring_cv = Ring(pbank[6:8])
            yst = {}

            def f_yh():
                pt, pbf = ring_cv.next()

                def yh(e):
                    ins = None
                    for i in range(2):
                        wt = wc[i][0]
                        for which, off in ((0, 1024), (1, 2048)):
                            for k in range(8):
                                ins = e.matmul(pt[0:2, which * 256 + i * 128:which * 256 + (i + 1) * 128],
                                               lhsT=hT[:, k, 1024:1536:511], rhs=wt[:, off + k * 128:off + (k + 1) * 128],
                                               start=(k == 0), stop=(k == 7))
                    return ins
                S.op("pe", yh, reads=hb_tb(2) + wc[0][1] + wc[1][1], writes=[pbf])
                S.op("act", lambda e: e.activation(out=yhtmp[0][:], in_=pt[0:2, 256:512], func=AF.Copy), reads=[pbf], writes=[yhtmp[1]])
                S.op("dve", lambda e: e.tensor_tensor(out=yhal[0][:], in0=pt[0:2, 0:256], in1=yhtmp[0][:], op=ALU.mult),
                     reads=[pbf, yhtmp[1]], writes=[yhal[1]])
                bpy = Buf(f"pubY{l}")
                S.dma("sp", pubY[l], yhal[0][:], reads=[yhal[1]], writes=[bpy])
                yst["b_gY"] = Buf(f"gathY{l}")
                S.custom("pool", lambda e: e.collective_compute(
                    "AllGather", ALU.bypass, replica_groups=[[0, 1, 2, 3], [4, 5, 6, 7]], ins=[pubY[l]], outs=[gathY[l]]),
                    reads=[bpy], writes=[yst["b_gY"]])
            conv_groups = []

            def conv_block(i, tb):
                wt, bwt = wc[i]
                st_ = {}

                def grp(off):
                    pt_, bpt_ = ring_cv.next()

                    def mm(e):
                        ins = None
                        for k in range(8):
                            ins = e.matmul(pt_[:], lhsT=wt[:, off + k * 128:off + (k + 1) * 128],
                                           rhs=hT[:, k, tb * 512:(tb + 1) * 512], start=(k == 0), stop=(k == 7))
                        return ins
                    S.op("pe", mm, reads=hb_tb(tb) + bwt, writes=[bpt_])
                    return pt_, bpt_
                ccs, bccs = sF[6]

                def g_cc():
                    pcc, bcc = grp(1024)
                    S.op("act", lambda e: e.activation(out=ccs[:, 0:512], in_=pcc[:], func=AF.Copy), reads=[bcc], writes=[bccs])

                def g_ch():
                    pch, bch = grp(2048)
                    if tb < 2:
                        yb, byb = ybuf[0]
                        ybv = yb[:, 0:516].rearrange("p (s n) -> p s n", s=2)
                        S.op("dve", lambda e: e.tensor_tensor(
                            out=ybv[:, :, 1:257], in0=pch[:].rearrange("p (s n) -> p s n", s=2),
                            in1=ccs[:, 0:512].rearrange("p (s n) -> p s n", s=2), op=ALU.mult),
                            reads=[bch, bccs], writes=[byb])
                    else:
                        S.op("dve", lambda e: e.tensor_tensor(
                            out=ysm[0][:, i, 1:513], in0=pch[:], in1=ccs[:, 0:512], op=ALU.mult),
                            reads=[bch, bccs], writes=[ysm[1]])

                def g_cb():
                    pcb, bcb = grp(0)
                    if tb < 2:
                        yb, byb = ybuf[0]
                        ybv = yb[:, 0:516].rearrange("p (s n) -> p s n", s=2)
                        emit_conv_core(l, i, tb, ybv[:, :, 0:256], ybv[:, :, 1:257], ybv[:, :, 2:258], [byb], pcb[:], [bcb])
                    else:
                        S.op("act", lambda e: e.activation(out=cbs[0][:, i, :], in_=pcb[:], func=AF.Copy),
                             reads=[bcb], writes=[cbs[1]])
                return [g_cc, g_ch, g_cb]

            for i in range(2):
                for tb in (2, 0, 1):
                    conv_groups.extend(conv_block(i, tb))

            groups = []
            for tb in range(2):
                for c in range(4):
                    kts = []
                    for kt in range(2):
                        segs = []
                        for sq_ in range(2):
                            tile = (tb * 2 + sq_) * 2 + kt
                            segs.append((kT[:, tile * 128:(tile + 1) * 128], b_kT[tile], vB[:, tile, :], b_vB[tile], sq_ * 256, 256))
                        kts.append(segs)
                    groups.append(dict(c=c, q0=tb * 512, qn=512, ktiles=kts, ao_buf=[b_ao[c][tb * 2], b_ao[c][tb * 2 + 1]],
                                       q_bufs=[b_qT[tb * 4 + i] for i in range(4)], after=None))
            for r in range(4):
                S.dma("sp", kTall[:, r * 512:(r + 1) * 512],
                      gath[l][r * PUBR:r * PUBR + 256, :].rearrange("(p a) c -> p (a c)", a=2), reads=[b_g], writes=[b_kTall[r]])
                S.dma("sp", vall[:, r * 4:(r + 1) * 4, :],
                      gath[l][r * PUBR + 256:r * PUBR + 512, :].rearrange("(t q) (two c) -> (q two) t c", t=4, two=2),
                      reads=[b_g], writes=[b_vall[r]])
                S.dma("sp", uhalo[r * 16:(r + 1) * 16, :], gath[l][r * PUBR + 512:r * PUBR + 528, :], reads=[b_g], writes=[b_uhalo[r]])

            def f_ysl():
                for r in range(4):
                    S.dma("sp", yhalo[r * 2:(r + 1) * 2, :], gathY[l][r * 2:(r + 1) * 2, :], reads=[yst["b_gY"]], writes=[b_yhalo[r]])
                pt, pbf = ring_cv.next()

                def ysl(e):
                    ins = None
                    for i in range(2):
                        ins = e.matmul(pt[:, i * 2:i * 2 + 2], lhsT=yhalo[0:8, i * 128:(i + 1) * 128], rhs=ysel[0:8, 0:2], start=True, stop=True)
                    return ins
                S.op("pe", ysl, reads=b_yhalo + CONSTS, writes=[pbf])
                S.op("act", lambda e: e.activation(
                    out=ysm[0][:, :, 0:514:513], in_=pt[:, 0:4].rearrange("p (i w) -> p i w", i=2), func=AF.Copy),
                    reads=[pbf], writes=[ysm[1]])

            def f_convfin():
                for i in range(2):
                    emit_conv_core(l, i, 2, ysm[0][:, i, 0:512], ysm[0][:, i, 1:513], ysm[0][:, i, 2:514], [ysm[1]],
                                   cbs[0][:, i, :], [cbs[1]])
            fillers = conv_groups[:9] + [f_yh] + conv_groups[9:]
            for tb in range(2):
                fillers.extend(emit_pool(l, tb, ring=ring_cv, as_stages=True))
            fillers.extend([f_ysl, f_convfin])
            fillers.extend(emit_pool(l, 2, ring=ring_cv, as_stages=True))

            def spread(fs, n_units):
                it = iter(fs)
                n_two = max(0, len(fs) - n_units)
                for ui in range(n_units):
                    f = next(it, None)
                    g_ = next(it, None) if ui < n_two else None
                    if f is None:
                        return
                    yield (lambda f=f, g_=g_: (f(), g_() if g_ is not None else None))
                for f in it:
                    yield f
            attention_stream(groups, filler=spread(fillers, sum(len(g["ktiles"]) for g in groups)))
            W.release()
            W.release()

            groups = []
            for c in range(4):
                kts = []
                for kt in range(20):
                    kts.append([(kTall[:, kt * 128:(kt + 1) * 128], b_kTall[kt // 4], vall[:, kt, :], b_vall[kt // 4], 0, 512)])
                after = None
                if l == 0:
                    def after(g, c=c):
                        emit_mod_slab(0, 4 + 2 * c, ps=g["num"])
                        emit_mod_slab(0, 5 + 2 * c, ps=g["den"])
                groups.append(dict(c=c, q0=1024, qn=512, ktiles=kts, ao_buf=[b_ao[c][4]], q_bufs=[b_qT[8 + i] for i in range(4)],
                                   after=after))
            attention_stream(groups)

            transfer(A_R1, B_R1)
            for j in range(8):
                wt, bwt = W.acquire(("mrg", l, j))
                for tb in range(3):
                    gb = [ring_all.next() for _ in range(3)]

                    def mg(e, wt=wt, tb=tb, gb=gb):
                        ins = None
                        for br in range(3):
                            for k in range(8):
                                ins = e.matmul(gb[br][0][:], lhsT=wt[:, br * 1024 + k * 128:br * 1024 + (k + 1) * 128],
                                               rhs=hT[:, k, tb * 512:(tb + 1) * 512], start=(k == 0), stop=(k == 7))
                        return ins
                    S.op("pe", mg, reads=hb_tb(tb) + bwt, writes=[g[1] for g in gb])
                    sgs = []
                    for br in range(3):
                        sg, bsg = sH[br]
                        S.op("act", lambda e, br=br, sg=sg, gb=gb: e.activation(out=sg[:], in_=gb[br][0][:], func=AF.Sigmoid),
                             reads=[gb[br][1]], writes=[bsg])
                        sgs.append((sg, bsg))
                    bb = [ring_all.next() for _ in range(3)]

                    def mb(e, wt=wt, tb=tb, bb=bb):
                        ins = None
                        for k in range(2):
                            ins = e.matmul(bb[0][0][:], lhsT=wt[:, 3072 + k * 128:3072 + (k + 1) * 128],
                                           rhs=poT[:, k, tb * 512:(tb + 1) * 512], start=(k == 0), stop=(k == 1))
                        for k in range(2):
                            ins = e.matmul(bb[1][0][:], lhsT=wt[:, 3328 + k * 128:3328 + (k + 1) * 128],
                                           rhs=coT[:, k, tb * 512:(tb + 1) * 512], start=(k == 0), stop=(k == 1))
                        for k in range(4):
                            ins = e.matmul(bb[2][0][:], lhsT=wt[:, 3584 + k * 128:3584 + (k + 1) * 128],
                                           rhs=aoT[:, k, tb * 512:(tb + 1) * 512], start=(k == 0), stop=(k == 3))
                        return ins
                    aobufs = [b_ao[c][sg_] for c in range(4) for sg_ in ((0, 1) if tb == 0 else (2, 3) if tb == 1 else (4,))]
                    S.op("pe", mb, reads=bwt + [b_po[0][tb], b_po[1][tb], b_co[0][tb], b_co[1][tb]] + aobufs,
                         writes=[b[1] for b in bb])
                    t0, bt0 = sF[0]
                    t1, bt1 = sF[1]
                    S.op("dve", lambda e, t0=t0, bb=bb, sgs=sgs: e.tensor_tensor(out=t0[:, 0:512], in0=bb[0][0][:], in1=sgs[0][0][:], op=ALU.mult),
                         reads=[bb[0][1], sgs[0][1]], writes=[bt0])
                    S.op("dve", lambda e, t1=t1, bb=bb, sgs=sgs: e.tensor_tensor(out=t1[:, 0:512], in0=bb[1][0][:], in1=sgs[1][0][:], op=ALU.mult),
                         reads=[bb[1][1], sgs[1][1]], writes=[bt1])
                    S.op("dve", lambda e, t0=t0, t1=t1: e.tensor_tensor(out=t0[:, 0:512], in0=t0[:, 0:512], in1=t1[:, 0:512], op=ALU.add),
                         reads=[bt0, bt1], writes=[bt0])
                    S.op("dve", lambda e, t1=t1, bb=bb, sgs=sgs: e.tensor_tensor(out=t1[:, 0:512], in0=bb[2][0][:], in1=sgs[2][0][:], op=ALU.mult),
                         reads=[bb[2][1], sgs[2][1]], writes=[bt1])
                    S.op("dve", lambda e, t0=t0, t1=t1, j=j, tb=tb: e.tensor_tensor(
                        out=mgT[:, j, tb * 512:(tb + 1) * 512], in0=t0[:, 0:512], in1=t1[:, 0:512], op=ALU.add),
                        reads=[bt0, bt1], writes=[b_mg[j][tb]])
                W.release()

            for h in range(2):
                wt, bwt = W.acquire(("wo", l, h))
                for jj in range(4):
                    j = h * 4 + jj
                    for tb in range(3):
                        v = 0 if tb < 2 else 1
                        pt, pbf = ring5.next()

                        def mo(e, wt=wt, jj=jj, tb=tb, pt=pt):
                            ins = None
                            for k in range(8):
                                ins = e.matmul(pt[:], lhsT=wt[:, k * 512 + jj * 128:k * 512 + (jj + 1) * 128],
                                               rhs=mgT[:, k, tb * 512:(tb + 1) * 512], start=(k == 0), stop=(k == 7))
                            return ins
                        S.op("pe", mo, reads=bwt + [b_mg[k][tb] for k in range(8)], writes=[pbf])
                        S.op("dve", lambda e, pt=pt, j=j, tb=tb, v=v: e.scalar_tensor_tensor(
                            out=xT[:, j, tb * 512:(tb + 1) * 512], in0=pt[:], scalar=modT[:, l, 16 + j, v:v + 1],
                            in1=xT[:, j, tb * 512:(tb + 1) * 512], op0=ALU.mult, op1=ALU.add),
                            reads=[pbf, b_mod[l]] + xb(j, tb), writes=xb(j, tb))
                    if j > 0:
                        stats_chunk(j - 1)
                W.release()
            stats_chunk(7)

            emit_norm(l, 1, inc=True)
            transfer(B_R1 + A_R2, F_ALL)
            nxt = iter(range(12))
            for half in range(2):
                for fi in range(11):
                    wt, bwt = W.acquire(("gu", l, half * 11 + fi))
                    for tb in range(3):
                        pg, bpg = ring_all.next()
                        pu, bpu = ring_all.next()

                        def mgu(e, wt=wt, tb=tb, pg=pg, pu=pu):
                            ins = None
                            for pt_, off in ((pg, 0), (pu, 1024)):
                                for k in range(8):
                                    ins = e.matmul(pt_[:], lhsT=wt[:, off + k * 128:off + (k + 1) * 128],
                                                   rhs=hT[:, k, tb * 512:(tb + 1) * 512], start=(k == 0), stop=(k == 7))
                            return ins
                        S.op("pe", mgu, reads=hb_tb(tb) + bwt, writes=[bpg, bpu])
                        sl, bsl = sF[tb % 2]
                        S.op("act", lambda e, pg=pg, sl=sl: e.activation(out=sl[:, 0:512], in_=pg[:], func=AF.Silu), reads=[bpg], writes=[bsl])
                        S.op("dve", lambda e, pu=pu, sl=sl, fi=fi, tb=tb: e.tensor_tensor(
                            out=actT(fi)[:, tb * 512:(tb + 1) * 512], in0=pu[:], in1=sl[:, 0:512], op=ALU.mult),
                            reads=[bpu, bsl], writes=[b_act[fi][tb]])
                    W.release()
                    if l + 1 < DEPTH and fi % 2 == 1:
                        emit_mod_slab(l + 1, next(nxt))
                pend_d = []
                cnt_d = {0: 0, 1: 0, 2: 0}
                modq_d = []

                def flush_d():
                    k_, tb_ = pend_d.pop(0)
                    stats_chunk(k_, only_tb=tb_)
                    cnt_d[tb_] += 1
                    if l + 1 == DEPTH:
                        S.op("dve", lambda e, k_=k_, tb_=tb_: e.tensor_scalar(
                            out=xT[:, k_, tb_ * 512:(tb_ + 1) * 512], in0=xT[:, k_, tb_ * 512:(tb_ + 1) * 512],
                            scalar1=prm[:, P_FG + k_:P_FG + k_ + 1], scalar2=None, op0=ALU.mult),
                            reads=xb(k_, tb_) + [b_prm], writes=xb(k_, tb_))
                    if cnt_d[tb_] == 8 and l + 1 < DEPTH:
                        rsb_ = stats_finish(tb_, three=True)
                        for k2 in range(8):
                            modq_d.append(lambda tb_=tb_, rsb_=rsb_, k2=k2: emit_norm_k(l + 1, 0, 0, tb_, rsb_, k2))
                for c in range(4):
                    wt, bwt = W.acquire(("down", l, half, c))
                    if half == 1 and c == 3 and l + 1 < DEPTH:
                        jt_order = [(jj, tb) for tb in (2, 0, 1) for jj in range(2)]
                    else:
                        jt_order = [(jj, tb) for jj in range(2) for tb in range(3)]
                    for (jj, tb) in jt_order:
                        j = c * 2 + jj
                        if True:
                            v = 0 if tb < 2 else 1
                            pt, pbf = ring5.next() if half == 1 else ring_all.next()

                            def md(e, wt=wt, jj=jj, tb=tb, pt=pt):
                                ins = None
                                for k in range(11):
                                    ins = e.matmul(pt[:], lhsT=wt[:, k * 256 + jj * 128:k * 256 + (jj + 1) * 128],
                                                   rhs=actT(k)[:, tb * 512:(tb + 1) * 512], start=(k == 0), stop=(k == 10))
                                return ins
                            S.op("pe", md, reads=bwt + [b_act[k][tb] for k in range(11)], writes=[pbf])
                            S.op("dve", lambda e, pt=pt, j=j, tb=tb, v=v: e.scalar_tensor_tensor(
                                out=xT[:, j, tb * 512:(tb + 1) * 512], in0=pt[:], scalar=modT[:, l, 40 + j, v:v + 1],
                                in1=xT[:, j, tb * 512:(tb + 1) * 512], op0=ALU.mult, op1=ALU.add),
                                reads=[pbf, b_mod[l]] + xb(j, tb), writes=xb(j, tb))
                            if half == 1:
                                pend_d.append((j, tb))
                                if len(pend_d) > 3:
                                    flush_d()
                                for _ in range(2):
                                    if modq_d:
                                        modq_d.pop(0)()
                    W.release()
                    if l + 1 < DEPTH and c == 1:
                        emit_mod_slab(l + 1, next(nxt), ps=(ring5.next() if half == 1 else None))
                while pend_d:
                    flush_d()
                while modq_d:
                    modq_d.pop(0)()

        for l in range(DEPTH):
            emit_layer(l)

        b_fin = [Buf(f"fin{i}") for i in range(4)]
        transfer(F_ALL, b_fin)
        yo = [(R1f[:, i * 1024:(i + 1) * 1024], [b_fin[2 * i], b_fin[2 * i + 1]]) for i in range(2)]
        fin_rs = [stats_finish(tb, three=True) for tb in range(3)]
        ptk, bptk = ring_all.next()

        def trr(e):
            ins = None
            for tb in range(3):
                for jt in range(4):
                    i = tb * 4 + jt
                    ins = e.transpose(ptk[:, i:i + 1], fin_rs[tb][0][0:1, jt * 128:(jt + 1) * 128], ident[0:1, 0:1])
            return ins
        S.op("pe", trr, reads=[fin_rs[tb][1] for tb in range(3)] + CB, writes=[bptk])
        rtok, brtok = small[0]
        S.op("act", lambda e: e.activation(out=rtok[:, 0:12], in_=ptk[:, 0:12], func=AF.Copy), reads=[bptk], writes=[brtok])
        for tt in range(12):
            yt, byt = yo[tt % 2]
            for half in range(2):
                pt, pbf = ring_all.next()

                def trf(e, half=half, pt=pt, tt=tt):
                    ins = None
                    for j in range(4):
                        k = half * 4 + j
                        ins = e.transpose(pt[:, j * 128:(j + 1) * 128], xT[:, k, tt * 128:(tt + 1) * 128], ident[:])
                    return ins
                S.op("pe", trf, reads=[b_xT[half * 4 + j][tt] for j in range(4)] + CB, writes=[pbf])
                if half == 0:
                    S.op("act", lambda e, pt=pt, yt=yt, tt=tt: e.activation(
                        out=yt[:, 0:512], in_=pt[:], func=AF.Copy, scale=rtok[:, tt:tt + 1]),
                        reads=[pbf, brtok], writes=[byt[0]])
                else:
                    S.op("dve", lambda e, pt=pt, yt=yt, tt=tt: e.tensor_scalar(
                        out=yt[:, 512:1024], in0=pt[:], scalar1=rtok[:, tt:tt + 1], scalar2=None, op0=ALU.mult),
                        reads=[pbf, brtok], writes=[byt[1]])
            dst = yp[tt * 128:(tt + 1) * 128, :] if tt < 8 else ys[(tt - 8) * 128:(tt - 7) * 128, :]
            S.dma("sp", dst, yt, reads=byt)

        S.finish()
        S.replay()
        build_program.stats = dict(sbuf=sb_bytes[0], n_ins=dict(S.n_ins), n_wait=S.n_wait, n_sems=len(S.sems), slabs=len(W.plan))
    return nc


def _band(Lseq, w):
    M = np.zeros((Lseq, Lseq), np.float32)
    cnt = np.zeros(Lseq, np.float32)
    for t in range(Lseq):
        lo = min(max(t - w // 2, 0), Lseq)
        hi = min(max(t + w // 2, 0), Lseq)
        M[lo:hi, t] = 1.0
        cnt[t] = hi - lo
        M[t, t] -= cnt[t]
    return M, cnt


_CONST_CACHE = {}


def _host_consts():
    if _CONST_CACHE:
        return _CONST_CACHE
    wins = (2, 4, 8, 16)
    Ms, cnts, c256 = [], [], []
    for w in wins:
        M, c = _band(2048, w)
        M = M / c[None, :]
        Ms.append(M)
        cnts.append(c)
        M2, c2 = _band(256, w)
        M2 = M2 / c2[None, :]
        c256.append(c2)
        assert np.array_equal(M2[0:128, 0:128], M[0:128, 0:128])
        assert np.array_equal(M2[128:256, 128:256], M[1920:2048, 1920:2048])
        assert np.array_equal(M2[0:128, 128:256], M[0:128, 128:256])
        assert np.array_equal(M2[128:256, 0:128], M[128:256, 0:128])
    bm = np.zeros((128, 4, 5, 128), np.float32)
    for g in range(4):
        M = Ms[g]
        bm[:, g, 0] = M[128:256, 128:256]
        bm[:, g, 1] = M[0:128, 0:128]
        bm[:, g, 2] = M[1920:2048, 1920:2048]
        bm[:, g, 3] = M[0:128, 128:256]
        bm[:, g, 4] = M[128:256, 0:128]
    per_rank = []
    inv = (10000.0 ** (-np.arange(16, dtype=np.float32) / np.float32(16))).astype(np.float32)
    for r in range(4):
        s0 = r * 512
        smpd = np.zeros((128, 4, 2, 128), np.float32)
        smph = np.zeros((64, 4, 2, 128), np.float32)
        for g in range(4):
            M = Ms[g]
            smpd[:, g, 0] = M[s0:s0 + 128, s0:s0 + 128]
            smpd[:, g, 1] = M[s0 + 384:s0 + 512, s0 + 384:s0 + 512]
            for rr in range(4):
                for i in range(16):
                    tok = rr * 512 + i if i < 8 else rr * 512 + 504 + (i - 8)
                    row = rr * 16 + i
                    if rr == r - 1 and i >= 8:
                        smph[row, g, 0] = M[tok, s0:s0 + 128]
                    if rr == r + 1 and i < 8:
                        smph[row, g, 1] = M[tok, s0 + 384:s0 + 512]
        ysel = np.zeros((8, 2), np.float32)
        if r > 0:
            ysel[(r - 1) * 2 + 1, 0] = 1.0
        if r < 3:
            ysel[(r + 1) * 2 + 0, 1] = 1.0
        t = np.arange(s0, s0 + 512)
        pr = (t // 64).astype(np.float32)
        pc = (t % 64).astype(np.float32)
        ang = np.stack([pr[:, None] * inv[None, :], pc[:, None] * inv[None, :]], axis=1).astype(np.float32)
        per_rank.append(dict(
            c_smpd=smpd.reshape(128, 1024).astype(ml_dtypes.bfloat16),
            c_smph=smph.reshape(64, 1024).astype(ml_dtypes.bfloat16),
            c_ysel=ysel.astype(ml_dtypes.bfloat16),
            c_ropec=np.cos(ang).astype(np.float32).reshape(512, 32),
            c_ropes=np.sin(ang).astype(np.float32).reshape(512, 32),
        ))
    _CONST_CACHE.update(dict(
        c_ident=np.eye(128, dtype=np.float32),
        c_bm=bm.reshape(128, 2560).astype(ml_dtypes.bfloat16),
        per_rank=per_rank,
    ))
    return _CONST_CACHE


_NC = None


def kernel(x_prompt, x_sample, cache_k, cache_v, c, c_ctx, norm1_g, norm2_g, w_ada, b_ada, w_in,
           w_pool, pool_scale, conv_w, q_norm_g, k_norm_g, w_br_pool, w_br_conv, w_br_attn, w_o,
           w_gate, w_up, w_down, final_g):
    global _NC
    f = lambda a: np.ascontiguousarray(np.asarray(a), dtype=np.float32)
    x_prompt, x_sample, cache_k, cache_v, c, c_ctx = map(f, (x_prompt, x_sample, cache_k, cache_v, c, c_ctx))
    shared = dict(n1g=f(norm1_g), n2g=f(norm2_g), w_ada=f(w_ada), b_ada=f(b_ada), w_in=f(w_in), w_pool=f(w_pool),
                  pool_scale=f(pool_scale), conv_w=f(conv_w), qg=f(q_norm_g), kg=f(k_norm_g), w_brp=f(w_br_pool),
                  w_brc=f(w_br_conv), w_bra=f(w_br_attn), w_o=f(w_o), w_gate=f(w_gate), w_up=f(w_up),
                  w_down=f(w_down), final_g=f(final_g))
    hc = _host_consts()
    if _NC is None:
        _NC = build_program()
    nc = _NC
    in_maps = []
    for core in range(8):
        b, r = core // 4, core % 4
        m = dict(shared)
        m["xp"] = x_prompt[core * 4:(core + 1) * 4].reshape(1024, D)
        m["xs"] = x_sample[b, r * 512:(r + 1) * 512, :]
        m["ck"] = cache_k[b].reshape(DEPTH, 512, 128)
        m["cv"] = cache_v[b].reshape(DEPTH, 512, 128)
        m["cvec"] = np.stack([c_ctx, c[b]], axis=0)
        m["c_ident"] = hc["c_ident"]
        m["c_bm"] = hc["c_bm"]
        m.update(hc["per_rank"][r])
        in_maps.append({k: np.ascontiguousarray(v) for k, v in m.items()})
    res = run_bass_kernel_spmd(nc, in_maps, core_ids=list(range(8)))
    R = res.results
    y_prompt = np.concatenate([np.asarray(R[i]["yp"], dtype=np.float32).reshape(4, 256, D) for i in range(8)], axis=0)
    y_sample = np.stack([np.concatenate([np.asarray(R[b * 4 + r]["ys"], dtype=np.float32) for r in range(4)], axis=0)
                         for b in range(2)], axis=0)
    nk_ = np.concatenate([np.asarray(R[i]["nk"], dtype=np.float32).reshape(4, DEPTH, 256, 2, 64) for i in range(8)], axis=0)
    nv_ = np.concatenate([np.asarray(R[i]["nv"], dtype=np.float32).reshape(4, DEPTH, 256, 2, 64) for i in range(8)], axis=0)
    return (y_prompt, y_sample, nk_, nv_)
```
